# Optimizing a Trainium2 kernel written in Bass

```python
import math
import jax, jax.numpy as jnp
from jax import lax
import numpy as np

D_MODEL = 1024
BATCH = 16
SEQ = 2048
DEPTH = 2

CHUNK = 64
Q_BLOCK = 128
PLE_DIM = 256
D_MIX = D_MODEL
D_ATTN = D_MIX // 2
D_CONV = D_MIX - D_ATTN
N_HEADS_A = 4
HEAD_DIM_V = D_ATTN // N_HEADS_A
HEAD_DIM_QK = HEAD_DIM_V // 2
CONV_WIDTH = 3
N_IN_PARTS = 8
D_IN = 4 * D_ATTN + 4 * D_CONV
EPS = 1e-6
SUBLN_EPS = 1e-5

kernel_name = "hybrid_diffattn_shortconv_chunk_causal"


def rms_norm(x, gain, eps=EPS):
    xf = x.astype(jnp.float32)
    y = xf * lax.rsqrt(jnp.mean(xf * xf, axis=-1, keepdims=True) + eps)
    return (y * gain.astype(jnp.float32)).astype(x.dtype)


def alibi_slopes(n_heads):
    return jnp.exp2(-8.0 * jnp.arange(1, n_heads + 1, dtype=jnp.float32) / n_heads)


def diff_attention(q, k, v, lam, slopes):
    seq = q.shape[1]
    scale = HEAD_DIM_QK ** -0.5
    outs = []
    for q0 in range(0, seq, Q_BLOCK):
        k_end = q0 + Q_BLOCK
        t = jnp.arange(q0, k_end)
        s = jnp.arange(k_end)
        scores = jnp.einsum('bqhmd,bkhmd->bhmqk', q[:, q0:k_end], k[:, :k_end]).astype(jnp.float32) * scale
        dist = jnp.abs(t[:, None] - s[None, :]).astype(jnp.float32)
        allowed = (s[None, :] // CHUNK) <= (t[:, None] // CHUNK)
        bias = jnp.where(allowed[None], -slopes[:, None, None] * dist[None], -jnp.inf)
        probs = jax.nn.softmax(scores + bias[None, :, None], axis=-1)
        weights = probs[:, :, 0] - lam * probs[:, :, 1]
        outs.append(jnp.einsum('bhqk,bkhd->bqhd', weights.astype(v.dtype), v[:, :k_end]))
    return jnp.concatenate(outs, axis=1)


def setup_inputs(seed: int = 0) -> dict:
    key = jax.random.key(seed)
    ks = jax.random.split(key, 16)
    f32 = jnp.float32
    x = jax.random.normal(ks[0], (BATCH, SEQ, D_MODEL), f32)
    p = jax.random.normal(ks[1], (DEPTH, BATCH, SEQ, PLE_DIM), f32)
    pre_norm_gain = 1.0 + 0.02 * jax.random.normal(ks[2], (DEPTH, D_MODEL), f32)
    w_in = jax.random.normal(ks[3], (DEPTH, D_MODEL, D_IN), f32) * D_MODEL ** -0.5
    lambda_q1 = 0.1 * jax.random.normal(ks[4], (DEPTH, HEAD_DIM_QK), f32)
    lambda_k1 = 0.1 * jax.random.normal(ks[5], (DEPTH, HEAD_DIM_QK), f32)
    lambda_q2 = 0.1 * jax.random.normal(ks[6], (DEPTH, HEAD_DIM_QK), f32)
    lambda_k2 = 0.1 * jax.random.normal(ks[7], (DEPTH, HEAD_DIM_QK), f32)
    subln_gain = 1.0 + 0.02 * jax.random.normal(ks[8], (DEPTH, HEAD_DIM_V), f32)
    conv_w = jax.random.normal(ks[9], (DEPTH, CONV_WIDTH, D_CONV), f32) * CONV_WIDTH ** -0.5
    w_out = jax.random.normal(ks[10], (DEPTH, D_MIX, D_MODEL), f32) * D_MIX ** -0.5
    post_norm_gain = 1.0 + 0.02 * jax.random.normal(ks[11], (DEPTH, D_MODEL), f32)
    w_ple_proj = jax.random.normal(ks[12], (DEPTH, PLE_DIM, D_MODEL), f32) * PLE_DIM ** -0.5
    ple_norm_gain = 1.0 + 0.02 * jax.random.normal(ks[13], (DEPTH, D_MODEL), f32)
    w_ple_gate = jax.random.normal(ks[14], (DEPTH, D_MODEL, D_MODEL), f32) * D_MODEL ** -0.5
    return {"x": x, "p": p, "pre_norm_gain": pre_norm_gain, "w_in": w_in,
            "lambda_q1": lambda_q1, "lambda_k1": lambda_k1, "lambda_q2": lambda_q2,
            "lambda_k2": lambda_k2, "subln_gain": subln_gain, "conv_w": conv_w,
            "w_out": w_out, "post_norm_gain": post_norm_gain, "w_ple_proj": w_ple_proj,
            "ple_norm_gain": ple_norm_gain, "w_ple_gate": w_ple_gate}


def reference(x, p, pre_norm_gain, w_in, lambda_q1, lambda_k1, lambda_q2, lambda_k2,
              subln_gain, conv_w, w_out, post_norm_gain, w_ple_proj, ple_norm_gain, w_ple_gate):
    bsz, seq, _ = x.shape
    slopes = alibi_slopes(N_HEADS_A)
    for i in range(DEPTH):
        lambda_init = 0.8 - 0.6 * math.exp(-0.3 * i)
        h = rms_norm(x, pre_norm_gain[i])
        proj = h @ w_in[i]
        q, k, v, z_a, b_gate, c_gate, u, z_c = jnp.split(proj, N_IN_PARTS, axis=-1)

        q = q.reshape(bsz, seq, N_HEADS_A, 2, HEAD_DIM_QK)
        k = k.reshape(bsz, seq, N_HEADS_A, 2, HEAD_DIM_QK)
        v = v.reshape(bsz, seq, N_HEADS_A, HEAD_DIM_V)
        lam = (jnp.exp(jnp.sum((lambda_q1[i] * lambda_k1[i]).astype(jnp.float32)))
               - jnp.exp(jnp.sum((lambda_q2[i] * lambda_k2[i]).astype(jnp.float32)))
               + lambda_init)
        attn = diff_attention(q, k, v, lam, slopes)
        attn = rms_norm(attn, subln_gain[i], SUBLN_EPS) * (1.0 - lambda_init)
        attn = attn.reshape(bsz, seq, D_ATTN) * jax.nn.silu(z_a)

        g = c_gate * u
        g_pad = jnp.pad(g, ((0, 0), (CONV_WIDTH - 1, 0), (0, 0)))
        conv = conv_w[i, 0] * g_pad[:, 0:seq]
        for j in range(1, CONV_WIDTH):
            conv = conv + conv_w[i, j] * g_pad[:, j:j + seq]
        conv_out = b_gate * conv * jax.nn.silu(z_c)

        mixed = jnp.concatenate([attn, conv_out], axis=-1) @ w_out[i]
        x = x + rms_norm(mixed, post_norm_gain[i])

        e = rms_norm(p[i] @ w_ple_proj[i], ple_norm_gain[i])
        x = x + jax.nn.sigmoid(x @ w_ple_gate[i]) * e
    return x
```

```python
import math
from contextlib import ExitStack

import numpy as np
import ml_dtypes

import concourse.bass as bass
import concourse.mybir as mybir
from concourse.bass_utils import run_bass_kernel_spmd

F32 = mybir.dt.float32
BF16 = mybir.dt.bfloat16
AF = mybir.ActivationFunctionType
ALU = mybir.AluOpType
AX = mybir.AxisListType

S = 2048
D = 1024
NT = S // 128
NL = 2
NCORES = 8
EPS = 1e-6
SUBLN_EPS = 1e-5
SLOPES = [2.0 ** (-8.0 * (h + 1) / 4) for h in range(4)]
MASKV = -30000.0


class Sched:
    ENG = ["pe", "act", "dve", "pool", "sp"]

    def __init__(self, nc):
        self.nc = nc
        self.q = {e: [] for e in self.ENG}
        self.cnt = {e: 0 for e in self.ENG}
        self.waited = {e: {} for e in self.ENG}
        self.lastw = {}
        self.readers = {}
        self.dmacnt = {}
        self.pending = {e: [] for e in self.ENG}

    def _deps(self, eng, reads, writes):
        toks = list(self.pending[eng])
        self.pending[eng] = []
        for k in reads:
            if k in self.lastw:
                toks.append(self.lastw[k])
        for k in writes:
            if k in self.lastw:
                toks.append(self.lastw[k])
            toks.extend(self.readers.get(k, ()))
        need = {}
        for (s, v) in toks:
            if eng == "pe" and s == ("e", "pe"):
                continue
            if v > need.get(s, 0):
                need[s] = v
        out = []
        for s, v in need.items():
            if self.waited[eng].get(s, 0) < v:
                self.waited[eng][s] = v
                out.append((s, v))
        return out

    def _commit(self, tok, reads, writes):
        for k in writes:
            self.lastw[k] = tok
            self.readers[k] = []
        for k in reads:
            if k in writes:
                continue
            self.readers.setdefault(k, []).append(tok)

    @staticmethod
    def _excl(reads, writes):
        r = [k for k in reads if k[0] != "ps"]
        w = list(writes) + [k for k in reads if k[0] == "ps"]
        return r, w

    def op(self, eng, fn, reads=(), writes=()):
        reads, writes = self._excl(reads, writes)
        waits = self._deps(eng, reads, writes)
        self.cnt[eng] += 1
        tok = (("e", eng), self.cnt[eng])
        self.q[eng].append((waits, fn, tok))
        self._commit(tok, reads, writes)

    def dma(self, eng, fn, sem, reads=(), writes=()):
        waits = self._deps(eng, reads, writes)
        self.dmacnt[sem] = self.dmacnt.get(sem, 0) + 16
        tok = (("d", sem), self.dmacnt[sem])
        self.q[eng].append((waits, fn, tok))
        self._commit(tok, reads, writes)

    def barrier(self):
        toks = [(("e", e), self.cnt[e]) for e in self.ENG if self.cnt[e] > 0]
        toks += [(("d", s), v) for s, v in self.dmacnt.items()]
        for e in self.ENG:
            self.pending[e].extend(toks)

    def emit(self):
        nc = self.nc
        sems = {}
        with ExitStack() as st:
            for e in self.ENG:
                sems[("e", e)] = st.enter_context(nc.semaphore("s_" + e))
            for i, s in enumerate(self.dmacnt):
                sems[("d", s)] = st.enter_context(nc.semaphore("d_%d" % i))
            block = st.enter_context(nc.Block())
            final = [(("e", e), self.cnt[e]) for e in self.ENG if self.cnt[e] > 0]
            final += [(("d", s), v) for s, v in self.dmacnt.items()]

            def mk(e):
                def body(engobj):
                    for waits, fn, tok in self.q[e]:
                        for s, v in waits:
                            engobj.wait_ge(sems[s], v)
                        inst = fn(engobj)
                        inst.then_inc(sems[tok[0]], 1 if tok[0][0] == "e" else 16)
                    if e == "sp":
                        for s, v in final:
                            engobj.wait_ge(sems[s], v)
                return body

            block.tensor(mk("pe"))
            block.scalar(mk("act"))
            block.vector(mk("dve"))
            block.gpsimd(mk("pool"))
            block.sync(mk("sp"))


class _Stop(Exception):
    pass


def build(nseq=2, layers=(0, 1), debug=False, upto=None, dbg_li=0):
    nc = bass.Bass("TRN2", target_bir_lowering=False)
    NLK = len(layers)

    def din(name, shape, dt=F32):
        return nc.dram_tensor(name, list(shape), dt, kind="ExternalInput").ap()

    x_d = din("x", [nseq, S, D])
    p_d = din("p", [NL, nseq, S, 256])
    win_d = din("w_in", [NL, 8, 128, 8, 512])
    wout_d = din("w_out", [NL, 128, 8, 1024])
    wgate_d = din("w_gate", [NL, 128, 8, 1024])
    wple_d = din("w_ple", [NL, 128, 2, 1024])
    preg_d = din("pre_g", [128, NL * 8])
    postg_d = din("post_g", [NL, 1024])
    pleg_d = din("ple_g", [NL, 1024])
    subln_d = din("subln_g", [NL, 512])
    convw_d = din("conv_w", [128, NL * 12])
    lam_d = din("lam", [NL, 256])
    ident_d = din("ident", [128, 128], BF16)
    dbase_d = din("dbase", [128, 640], BF16)
    bcol_d = din("bcol", [128, 68])
    out_d = nc.dram_tensor("out", [nseq, S, D], F32, kind="ExternalOutput").ap()
    dbg_d = {}
    if debug:
        for nm, shp, dt in [("dQ", [128, 4 * S], BF16), ("dK", [128, 4 * S], BF16),
                            ("dV", [128, 16 * 4 * 130], BF16), ("dza", [128, 16 * 512], BF16),
                            ("dcv", [128, 4 * S], BF16), ("dmx", [128, 4 * S], BF16),
                            ("dhT", [128, 8 * 1024], BF16)]:
            dbg_d[nm] = nc.dram_tensor(nm, shp, dt, kind="ExternalOutput").ap()

    sc = Sched(nc)
    st = ExitStack()

    def sb(name, shape, dt):
        return st.enter_context(nc.sbuf_tensor("sb_" + name, list(shape), dt))

    with st:
        x_sb = sb("x_sb", [128, NT, D], F32)
        hT_r = sb("hT_r", [128, 8 * 1024], BF16)
        wbuf = sb("wbuf", [128, 2, 8, 512], BF16)
        Q_r = sb("Q_r", [128, 4 * S], BF16)
        K_r = sb("K_r", [128, 4 * S], BF16)
        V_r = sb("V_r", [128, 16 * 4 * 130], BF16)
        za_r = sb("za_r", [128, 16 * 512], BF16)
        cv_r = sb("cv_r", [128, 4 * S], BF16)
        hn = sb("hn", [128, 2, D], BF16)
        tmp_r = sb("tmp_r", [128, 4480], F32)
        pb = sb("pb", [128, 2, 256], BF16)
        preg = sb("preg", [128, NL * 8], F32)
        convw = sb("convw", [128, NL * 12], F32)
        sg4 = sb("sg4", [128, 512], F32)
        lamv = sb("lamv", [128, 256], F32)
        ident = sb("ident", [128, 128], BF16)
        dbase = sb("dbase", [128, 640], BF16)
        bcol = sb("bcol", [128, 68], F32)
        small = sb("small", [128, 64], F32)
        lamtmp = sb("lamtmp", [128, 128], F32)
        agb_t = sb("agb_t", [128, 2, 512], BF16)
        ps = st.enter_context(nc.psum_tensor("ps", [128, 8, 512], F32))
        psbf = ps.bitcast(BF16)

        hT = hT_r[:, :].rearrange("p (c t) -> p c t", c=8)
        mixT = hT_r[:, :].rearrange("p (c t) -> p c t", c=4)
        Qv = Q_r[:, :].rearrange("p (h t) -> p h t", h=4)
        Kv = K_r[:, :].rearrange("p (h t) -> p h t", h=4)
        woutv = wbuf[:, :, :, :].rearrange("p s c n -> p (s c n)").rearrange("p (c n) -> p c n", c=8)
        wgatev = K_r[:, :].rearrange("p (c n) -> p c n", c=8)
        Vv = V_r[:, :].rearrange("p (t h e) -> p t h e", t=16, h=4)
        wplev = V_r[:, 0:2048].rearrange("p (c n) -> p c n", c=2)
        postg = V_r[:, 2048:4096].bitcast(F32)
        pleg = V_r[:, 4096:6144].bitcast(F32)
        zav = za_r[:, :].rearrange("p (t n) -> p t n", t=16)
        za_f = za_r[:, :].bitcast(F32)
        bufA = za_f[:, 0:2048].rearrange("p (a b) -> p a b", a=2)
        bufB = za_f[:, 2048:4096].rearrange("p (a b) -> p a b", a=2)
        xb2 = tmp_r[:, 0:1024].bitcast(BF16).rearrange("p (a b) -> p a b", a=2)
        pT2 = tmp_r[:, 1024:1280].bitcast(BF16).rearrange("p (a c t) -> p a c t", a=2, c=2)
        cvv = cv_r[:, :].rearrange("p (c t) -> p c t", c=4)
        cs = tmp_r[:, 0:512]
        gb = tmp_r[:, 512:1540].rearrange("p (a b) -> p a b", a=2)
        cva = tmp_r[:, 1540:2052]
        szb = tmp_r[:, 2052:2564]
        bzb = tmp_r[:, 2564:3076]
        junk = tmp_r[:, 3456:3968].bitcast(BF16)
        PTr = tmp_r[:, 0:768].bitcast(BF16).rearrange("p (s q) -> p s q", s=3)
        a1 = tmp_r[:, 768:896]
        araw = tmp_r[:, 896:1920].rearrange("p (a b) -> p a b", a=2)
        sqb = tmp_r[:, 1920:2432]
        agb2 = agb_t[:, :, :]
        zf2 = tmp_r[:, 2432:3456].rearrange("p (a b) -> p a b", a=2)
        Qz = tmp_r[:, 3456:4480].bitcast(BF16).rearrange("p (z m q) -> p z m q", z=4, m=2)
        ss = small[:, 0:4]
        rs = small[:, 4:8]
        rstd = small[:, 8:12]
        neghalf = small[:, 12:16]
        lam_s = small[:, 16:20]
        rl = small[:, 24:26]
        rl2 = small[:, 26:27]
        ssh = small[:, 28:32]
        rsh = small[:, 32:36]
        rstdh = small[:, 36:40]
        crr = small[:, 40:48].rearrange("p (c k) -> p c k", c=4)

        def ld(eng, out_ap, in_ap, sem, writes):
            sc.dma(eng, lambda e: e.dma_start(out=out_ap, in_=in_ap), sem, writes=writes)

        ld("sp", ident[:, :], ident_d, "c0", [("ident",)])
        ld("sp", dbase[:, :], dbase_d, "c0", [("dbase",)])
        ld("sp", bcol[:, :], bcol_d, "c0", [("bcol",)])
        ld("sp", preg[:, :], preg_d, "c0", [("preg",)])
        ld("sp", convw[:, :], convw_d, "c0", [("convw",)])
        sc.barrier()
        sc.op("dve", lambda e: e.memset(neghalf, -0.5), writes=[("neghalf",)])
        sc.op("pool", lambda e: e.memset(V_r[:, :], 1.0), writes=[("V", t) for t in range(16)])

        bank_ctr = [0]

        def nextbank():
            b = bank_ctr[0] % 8
            bank_ctr[0] += 1
            return b

        alt = [0]

        def alt_eng():
            alt[0] += 1
            return "act" if alt[0] % 2 else "dve"

        wq = []
        for si in range(nseq):
            for l in layers:
                for hf in range(2):
                    for j in range(8):
                        wq.append((l, j))
        wissued = [0]

        def issue_w():
            n = wissued[0]
            if n >= len(wq):
                return
            l, j = wq[n]
            slot = n % 2
            sc.dma("pool", lambda e: e.dma_start(out=wbuf[:, slot, :, :], in_=win_d[l, j],
                                                  max_dma_last_dim=4096),
                   "w%d" % slot, writes=[("w", slot)])
            wissued[0] += 1

        issue_w()
        wused = [0]

        for li0, l0 in enumerate(layers):
            linit = 0.8 - 0.6 * math.exp(-0.3 * l0)
            sc.dma("sp", lambda e, l0=l0: e.dma_start(out=lamv[:, :].unsqueeze(1),
                                                  in_=lam_d[l0:l0 + 1, :].partition_broadcast(128)),
                   "lamv", writes=[("lamv",)])
            lv = lamv[:, :].rearrange("p (a b d) -> p a b d", a=2, b=2)
            prv = lamtmp[:, :].rearrange("p (a d) -> p a d", a=2)
            sc.op("dve", lambda e, lv=lv, prv=prv: e.tensor_tensor(out=prv, in0=lv[:, :, 0, :], in1=lv[:, :, 1, :],
                                                               op=ALU.mult),
                  reads=[("lamv",)], writes=[("lamtmp",)])
            sc.op("dve", lambda e, prv=prv: e.tensor_reduce(out=lam_s[:, 0:2], in_=prv, axis=AX.X, op=ALU.add),
                  reads=[("lamtmp",)], writes=[("lam_s",)])
            sc.op("act", lambda e: e.activation(out=lam_s[:, 2:4], in_=lam_s[:, 0:2], func=AF.Exp),
                  reads=[("lam_s",)], writes=[("lam_e",)])
            sc.op("dve", lambda e, linit=linit, li0=li0: e.tensor_scalar(
                out=small[:, 20 + li0:21 + li0], in0=lam_s[:, 3:4], scalar1=lam_s[:, 2:3],
                scalar2=-linit, op0=ALU.subtract, op1=ALU.add),
                reads=[("lam_e",)], writes=[("neglam", li0)])
        try:
          if upto == "consts":
              raise _Stop()
          for si in range(nseq):
            for li, l in enumerate(layers):
                lambda_init = 0.8 - 0.6 * math.exp(-0.3 * l)
                first = (li == 0)
                last = (li == NLK - 1)
                neglam = small[:, 20 + li:21 + li]
                sc.dma("sp", lambda e, l=l: e.dma_start(out=sg4[:, :].unsqueeze(1),
                                                   in_=subln_d[l:l + 1, :].partition_broadcast(128)),
                       "sg4", writes=[("sg4",)])
                sc.op("dve", lambda e, lambda_init=lambda_init: e.tensor_scalar(out=sg4[:, :], in0=sg4[:, :],
                                                       scalar1=(1.0 - lambda_init), scalar2=None, op0=ALU.mult),
                      reads=[("sg4",)], writes=[("sg4",)])

                if upto == "lam":
                    raise _Stop()
                for hf in range(2):
                    def n_stat(t):
                        sl = t % 4
                        if first:
                            sc.dma("sp", lambda e, t=t, si=si: e.dma_start(out=x_sb[:, t, :],
                                                                    in_=x_d[si, 128 * t:128 * t + 128, :]),
                                   "x%d" % t, writes=[("x", t)])
                        sc.op("act", lambda e, t=t, sl=sl: e.activation(out=junk[:, :], in_=x_sb[:, t, :],
                                                                      func=AF.Square,
                                                                      accum_out=ss[:, sl:sl + 1]),
                              reads=[("x", t)], writes=[("ss", sl), ("junk",)])
                        sc.op("dve", lambda e, sl=sl: e.tensor_scalar(out=rs[:, sl:sl + 1], in0=ss[:, sl:sl + 1],
                                                                    scalar1=1.0 / D, scalar2=EPS,
                                                                    op0=ALU.mult, op1=ALU.add),
                              reads=[("ss", sl)], writes=[("rs", sl)])
                        sc.op("pool", lambda e, sl=sl: e.tensor_tensor(out=rstd[:, sl:sl + 1], in0=rs[:, sl:sl + 1],
                                                                     in1=neghalf[:, 0:1], op=ALU.pow),
                              reads=[("rs", sl), ("neghalf",)], writes=[("rstd", sl)])

                    def n_norm(t):
                        sl = t % 4
                        hb = t % 2
                        tt = t % 4
                        par = (t // 4) % 2
                        sc.op("dve", lambda e, t=t, sl=sl, hb=hb: e.tensor_scalar(
                            out=hn[:, hb, :], in0=x_sb[:, t, :], scalar1=rstd[:, sl:sl + 1], scalar2=None,
                            op0=ALU.mult),
                            reads=[("x", t), ("rstd", sl)], writes=[("hn", hb)])

                        def tr(e, tt=tt, hb=hb, par=par):
                            ins = None
                            for c in range(8):
                                bk = 4 * par + c // 2
                                off = (c % 2) * 512 + 128 * tt
                                ins = e.transpose(out=psbf[:, bk, off:off + 128],
                                                  in_=hn[:, hb, 128 * c:128 * c + 128], identity=ident[:, :])
                            return ins
                        sc.op("pe", tr, reads=[("hn", hb), ("ident",)],
                              writes=[("ps", 4 * par + b) for b in range(4)])

                    def n_evac(G):
                        g = G % 2
                        par = G % 2
                        for c in range(8):
                            bk = 4 * par + c // 2
                            off = (c % 2) * 512
                            eng = "act" if (c // 2) % 2 == 0 else "dve"
                            gcol = preg[:, l * 8 + c:l * 8 + c + 1]
                            dst = hT[:, c, 512 * g:512 * g + 512]
                            src = psbf[:, bk, off:off + 512]
                            if eng == "act":
                                sc.op("act", lambda e, dst=dst, src=src, gcol=gcol: e.mul(out=dst, in_=src, mul=gcol),
                                      reads=[("ps", bk), ("preg",)], writes=[("hT", g, c)])
                            else:
                                sc.op("dve", lambda e, dst=dst, src=src, gcol=gcol: e.tensor_scalar(
                                    out=dst, in0=src, scalar1=gcol, scalar2=None, op0=ALU.mult),
                                    reads=[("ps", bk), ("preg",)], writes=[("hT", g, c)])

                    tbase = 8 * hf
                    n_stat(tbase)
                    n_stat(tbase + 1)
                    for k in range(8):
                        if k + 2 < 8:
                            n_stat(tbase + k + 2)
                        n_norm(tbase + k)
                        if k % 4 == 3:
                            n_evac(2 * hf + k // 4)
                    if upto == "N":
                        raise _Stop()

                    if debug and si == 0 and li == dbg_li and hf == 0:
                        sc.dma("sp", lambda e: e.dma_start(out=dbg_d["dhT"], in_=hT_r[:, :]), "dbg",
                               reads=[("hT", g, c) for g in range(2) for c in range(8)])

                    for j in range(8):
                        slot = wused[0] % 2
                        if not (hf == 1 and j == 7) and wissued[0] <= wused[0] + 1:
                            issue_w()
                        wused[0] += 1
                        wk = ("w", slot)
                        if j < 2:
                            dstv = Qv if j == 0 else Kv
                            dk = "Q" if j == 0 else "K"
                            for m in range(4):
                                for g in range(2):
                                    G = 2 * hf + g
                                    bk = nextbank()

                                    def mm(e, m=m, g=g, bk=bk, slot=slot):
                                        ins = None
                                        for kc in range(8):
                                            ins = e.matmul(ps[:, bk, :], wbuf[:, slot, kc, 128 * m:128 * m + 128],
                                                           hT[:, kc, 512 * g:512 * g + 512],
                                                           start=(kc == 0), stop=(kc == 7))
                                        return ins
                                    sc.op("pe", mm, reads=[wk] + [("hT", g, c) for c in range(8)],
                                          writes=[("ps", bk)])
                                    dst = dstv[:, m, 512 * G:512 * G + 512]
                                    eng = alt_eng()
                                    if eng == "act":
                                        sc.op("act", lambda e, dst=dst, bk=bk: e.copy(out=dst, in_=ps[:, bk, :]),
                                              reads=[("ps", bk)], writes=[(dk, m, G)])
                                    else:
                                        sc.op("dve", lambda e, dst=dst, bk=bk: e.tensor_copy(out=dst, in_=ps[:, bk, :]),
                                              reads=[("ps", bk)], writes=[(dk, m, G)])
                        elif j < 4:
                            for tt in range(8):
                                t = 8 * hf + tt
                                g = tt // 4
                                bk = nextbank()

                                def mm(e, tt=tt, bk=bk, slot=slot):
                                    ins = None
                                    for kc in range(8):
                                        ins = e.matmul(ps[:, bk, :], hT[:, kc, 128 * tt:128 * tt + 128],
                                                       wbuf[:, slot, kc, :], start=(kc == 0), stop=(kc == 7))
                                    return ins
                                sc.op("pe", mm, reads=[wk] + [("hT", g, c) for c in range(8)], writes=[("ps", bk)])
                                if j == 2:
                                    eng = alt_eng()
                                    dst = Vv[:, t, :, 0:128]
                                    src = ps[:, bk, :].rearrange("p (h e) -> p h e", h=4)
                                    if eng == "act":
                                        sc.op("act", lambda e, dst=dst, src=src: e.copy(out=dst, in_=src),
                                              reads=[("ps", bk)], writes=[("V", t)])
                                    else:
                                        sc.op("dve", lambda e, dst=dst, src=src: e.tensor_copy(out=dst, in_=src),
                                              reads=[("ps", bk)], writes=[("V", t)])
                                else:
                                    sc.op("act", lambda e, bk=bk: e.activation(out=szb, in_=ps[:, bk, :], func=AF.Silu),
                                          reads=[("ps", bk)], writes=[("szb",)])
                                    sc.op("dve", lambda e, t=t: e.tensor_tensor(out=zav[:, t, :], in0=szb, in1=sg4[:, :],
                                                                              op=ALU.mult),
                                          reads=[("szb",), ("sg4",)], writes=[("za", t)])
                        else:
                            c = j - 4
                            for g in range(2):
                                G = 2 * hf + g
                                gp = G % 2
                                bks = [nextbank() for _ in range(4)]
                                for part in range(4):
                                    bk = bks[part]

                                    def mm(e, part=part, g=g, bk=bk, slot=slot):
                                        ins = None
                                        for kc in range(8):
                                            ins = e.matmul(ps[:, bk, :],
                                                           wbuf[:, slot, kc, 128 * part:128 * part + 128],
                                                           hT[:, kc, 512 * g:512 * g + 512],
                                                           start=(kc == 0), stop=(kc == 7))
                                        return ins
                                    sc.op("pe", mm, reads=[wk] + [("hT", g, cc) for cc in range(8)],
                                          writes=[("ps", bk)])
                                bb, bc, bu, bz = bks
                                if G == 0:
                                    sc.op("pool", lambda e, gp=gp: e.memset(gb[:, gp, 0:2], 0.0), writes=[("gbc", gp)])
                                else:
                                    sc.op("pool", lambda e, gp=gp, c=c: e.tensor_copy(out=gb[:, gp, 0:2], in_=crr[:, c, :]),
                                          reads=[("crr", c)], writes=[("gbc", gp)])
                                sc.op("act", lambda e, bc=bc: e.copy(out=cs, in_=ps[:, bc, :]),
                                      reads=[("ps", bc)], writes=[("cs",)])
                                sc.op("dve", lambda e, bu=bu, gp=gp: e.tensor_tensor(out=gb[:, gp, 2:514], in0=ps[:, bu, :],
                                                                                   in1=cs, op=ALU.mult),
                                      reads=[("ps", bu), ("cs",)], writes=[("gb", gp)])
                                w0 = convw[:, l * 12 + c * 3 + 0:l * 12 + c * 3 + 1]
                                w1 = convw[:, l * 12 + c * 3 + 1:l * 12 + c * 3 + 2]
                                w2 = convw[:, l * 12 + c * 3 + 2:l * 12 + c * 3 + 3]
                                sc.op("dve", lambda e, gp=gp, w0=w0: e.tensor_scalar(out=cva, in0=gb[:, gp, 0:512], scalar1=w0,
                                                                                   scalar2=None, op0=ALU.mult),
                                      reads=[("gb", gp), ("gbc", gp), ("convw",)], writes=[("cva",)])
                                sc.op("dve", lambda e, gp=gp, w1=w1: e.scalar_tensor_tensor(
                                    out=cva, in0=gb[:, gp, 1:513], scalar=w1, in1=cva, op0=ALU.mult, op1=ALU.add),
                                    reads=[("gb", gp), ("gbc", gp), ("cva",)], writes=[("cva",)])
                                sc.op("dve", lambda e, gp=gp, w2=w2: e.scalar_tensor_tensor(
                                    out=cva, in0=gb[:, gp, 2:514], scalar=w2, in1=cva, op0=ALU.mult, op1=ALU.add),
                                    reads=[("gb", gp), ("cva",)], writes=[("cva",)])
                                sc.op("pool", lambda e, gp=gp, c=c: e.tensor_copy(out=crr[:, c, :], in_=gb[:, gp, 512:514]),
                                      reads=[("gb", gp)], writes=[("crr", c)])
                                sc.op("act", lambda e, bz=bz: e.activation(out=szb, in_=ps[:, bz, :], func=AF.Silu),
                                      reads=[("ps", bz)], writes=[("szb",)])
                                sc.op("dve", lambda e, bb=bb: e.tensor_tensor(out=bzb, in0=ps[:, bb, :], in1=szb, op=ALU.mult),
                                      reads=[("ps", bb), ("szb",)], writes=[("bzb",)])
                                sc.op("dve", lambda e, c=c, G=G: e.tensor_tensor(out=cvv[:, c, 512 * G:512 * G + 512],
                                                                                 in0=cva, in1=bzb, op=ALU.mult),
                                      reads=[("cva",), ("bzb",)], writes=[("cv", c, 4 * G + k) for k in range(4)])

                sc.barrier()
                if upto == "A":
                    raise _Stop()
                sc.dma("pool", lambda e, l=l: e.dma_start(out=woutv, in_=wout_d[l], max_dma_last_dim=4096), "wout",
                       writes=[("wout",), ("w", 0), ("w", 1)])
                steps = [(I, h, j) for I in range(8) for h in range(4) for j in range(2 * I + 2)]
                LOOK = 2
                nsteps = len(steps)
                sc.op("pool", lambda e: e.memset(Qz[:, :, :, :], 0.0), writes=[("Qz", z) for z in range(4)])

                def emit_qz(I, h):
                    zb = (4 * I + h) % 4
                    for m in range(2):
                        if I < 4:
                            sc.op("act", lambda e, zb=zb, m=m, I=I, h=h: e.copy(
                                out=Qz[64 * m:64 * m + 64, zb, m, :], in_=Qv[64 * m:64 * m + 64, h, 256 * I:256 * I + 256]),
                                reads=[("Q", h, I // 2)], writes=[("Qz", zb)])
                        else:
                            sc.op("pool", lambda e, zb=zb, m=m, I=I, h=h: e.tensor_copy(
                                out=Qz[64 * m:64 * m + 64, zb, m, :], in_=Qv[64 * m:64 * m + 64, h, 256 * I:256 * I + 256]),
                                reads=[("Q", h, I // 2)], writes=[("Qz", zb)])

                def emit_qk(n):
                    I, h, j = steps[n]
                    i0 = 2 * I
                    sbi = n % 3
                    zb = (4 * I + h) % 4
                    if j == 0:
                        nh = 4 * I + h + 1
                        if nh < 32:
                            emit_qz(nh // 4, nh % 4)

                    def qk(e, I=I, h=h, j=j, sbi=sbi, zb=zb, i0=i0):
                        ins = None
                        for m in range(2):
                            ins = e.matmul(ps[:, sbi, 256 * m:256 * m + 256],
                                           Kv[:, h, 128 * j:128 * j + 128], Qz[:, zb, m, :],
                                           start=(m == 0), stop=True, skip_group_check=True)
                        dc = dbase[:, 128 * h:128 * h + 128]
                        if j == i0:
                            for m in range(2):
                                ins = e.matmul(ps[:, sbi, 256 * m:256 * m + 128], ident[:, :], dc,
                                               start=False, stop=True, skip_group_check=True)
                        if j == i0 + 1:
                            for m in range(2):
                                ins = e.matmul(ps[:, sbi, 256 * m + 128:256 * m + 256], ident[:, :], dc,
                                               start=False, stop=True, skip_group_check=True)
                                ins = e.matmul(ps[:, sbi, 256 * m:256 * m + 128], ident[:, :], dbase[:, 512:640],
                                               start=False, stop=True, skip_group_check=True)
                        return ins
                    sc.op("pe", qk, reads=[("K", h, j // 4), ("Qz", zb), ("ident",), ("dbase",)],
                          writes=[("ps", sbi)])

                gcount = 0
                hits = {}

                def chk(nm):
                    hits[nm] = hits.get(nm, 0) + 1
                    if upto == nm or upto == "%s#%d" % (nm, hits[nm]):
                        raise _Stop()
                emit_qz(0, 0)
                for n in range(min(LOOK, nsteps)):
                    emit_qk(n)
                deferred = []
                for n in range(nsteps):
                    I, h, j = steps[n]
                    i0 = 2 * I
                    sbi = n % 3
                    obase = 3 + 2 * (gcount % 2)
                    while deferred and deferred[0][0] <= n:
                        deferred.pop(0)[1]()
                    bidx = h * 17 + 16 + (j - i0 - 1)
                    bias = bcol[:, bidx:bidx + 1]
                    sc.op("act", lambda e, sbi=sbi, bias=bias: e.activation(
                        out=PTr[:, sbi, :], in_=ps[:, sbi, :], func=AF.Exp, bias=bias, scale=0.125),
                        reads=[("bcol",)], writes=[("ps", sbi), ("PT", sbi)])
                    if n + LOOK < nsteps:
                        emit_qk(n + LOOK)

                    def pv(e, sbi=sbi, j=j, h=h, i0=i0, obase=obase):
                        ins = None
                        for a in range(2):
                            if a == 0 and j == i0 + 1:
                                continue
                            for m in range(2):
                                ins = e.matmul(ps[:, obase + a, 130 * m:130 * m + 130],
                                               PTr[:, sbi, 256 * m + 128 * a:256 * m + 128 * a + 128], Vv[:, j, h, :],
                                               start=(j == 0 and m == 0), stop=True, skip_group_check=True)
                        return ins
                    sc.op("pe", pv, reads=[("PT", sbi), ("V", j)], writes=[("ps", obase), ("ps", obase + 1)])
                    chk("B3")
                    if j == i0 + 1:
                        gcount += 1
                        for a in range(2):
                            ob = obase + a
                            o3 = ps[:, ob, 0:260].rearrange("p (a b) -> p a b", a=2)
                            sc.op("dve", lambda e, o3=o3: e.reciprocal(out=rl, in_=o3[:, :, 128]),
                                  writes=[("ps", ob), ("rl",)])
                            sc.op("dve", lambda e, neglam=neglam: e.tensor_scalar(out=rl2, in0=rl[:, 1:2], scalar1=neglam,
                                                                                  scalar2=None, op0=ALU.mult),
                                  reads=[("rl",), ("neglam", li)], writes=[("rl2",)])
                            sc.op("dve", lambda e, ob=ob: e.tensor_scalar(out=a1, in0=ps[:, ob, 0:128], scalar1=rl[:, 0:1],
                                                                         scalar2=None, op0=ALU.mult),
                                  reads=[("rl",)], writes=[("ps", ob), ("a1",)])
                            sc.op("dve", lambda e, ob=ob, a=a, h=h: e.scalar_tensor_tensor(
                                out=araw[:, a, 128 * h:128 * h + 128], in0=ps[:, ob, 130:258], scalar=rl2, in1=a1,
                                op0=ALU.mult, op1=ALU.add),
                                reads=[("rl2",), ("a1",)], writes=[("ps", ob), ("araw", a, h)])
                        chk("B4d")
                        if h == 3:
                            for a in range(2):
                                i = i0 + a
                                ar = araw[:, a, :]
                                ar3 = ar.rearrange("p (h d) -> p h d", h=4)
                                sq3 = sqb.rearrange("p (h d) -> p h d", h=4)
                                sc.op("dve", lambda e, ar=ar: e.tensor_tensor(out=sqb, in0=ar, in1=ar, op=ALU.mult),
                                      reads=[("araw", a, hh) for hh in range(4)], writes=[("sqb",)])
                                sc.op("dve", lambda e, sq3=sq3: e.tensor_reduce(out=ssh, in_=sq3, axis=AX.X, op=ALU.add),
                                      reads=[("sqb",)], writes=[("ssh",)])
                                sc.op("dve", lambda e: e.tensor_scalar(out=rsh, in0=ssh, scalar1=1.0 / 128, scalar2=SUBLN_EPS,
                                                                       op0=ALU.mult, op1=ALU.add),
                                      reads=[("ssh",)], writes=[("rsh",)])
                                sc.op("pool", lambda e: e.tensor_tensor(out=rstdh, in0=rsh, in1=neghalf, op=ALU.pow),
                                      reads=[("rsh",), ("neghalf",)], writes=[("rstdh",)])
                                for hh in range(4):
                                    sc.op("dve", lambda e, ar3=ar3, hh=hh, a=a, i=i: e.scalar_tensor_tensor(
                                        out=agb2[:, a, 128 * hh:128 * hh + 128], in0=ar3[:, hh, :], scalar=rstdh[:, hh:hh + 1],
                                        in1=zav[:, i, 128 * hh:128 * hh + 128], op0=ALU.mult, op1=ALU.mult),
                                        reads=[("araw", a, hh), ("rstdh",), ("za", i)], writes=[("agb", a)])

                                def tr(e, a=a):
                                    ins = None
                                    for c in range(4):
                                        ins = e.transpose(out=psbf[:, 7, 128 * c:128 * c + 128],
                                                          in_=agb2[:, a, 128 * c:128 * c + 128], identity=ident[:, :])
                                    return ins

                                def fin(tr=tr, i=i, a=a):
                                    sc.op("pe", tr, reads=[("agb", a), ("ident",)], writes=[("ps", 7)])
                                    sc.op("act", lambda e, i=i: e.copy(
                                        out=mixT[:, :, 128 * i:128 * i + 128],
                                        in_=psbf[:, 7, 0:512].rearrange("p (c t) -> p c t", c=4)),
                                        writes=[("ps", 7), ("mx", i)])
                                deferred.append((n + min(24, 8 * I + 10) + a, fin))
                            chk("B6")

                while deferred:
                    deferred.pop(0)[1]()
                if debug and si == 0 and li == dbg_li:
                    sc.barrier()
                    for nm, r in [("dQ", Q_r), ("dK", K_r), ("dV", V_r), ("dza", za_r), ("dcv", cv_r), ("dmx", hT_r)]:
                        sc.dma("sp", lambda e, nm=nm, r=r: e.dma_start(out=dbg_d[nm], in_=r[:, :]), "dbg")

                if upto == "B":
                    raise _Stop()
                sc.barrier()
                sc.dma("sp", lambda e, l=l: e.dma_start(out=postg.unsqueeze(1),
                                                   in_=postg_d[l:l + 1, :].partition_broadcast(128)),
                       "postg", writes=[("postg",)])
                sc.dma("sp", lambda e, l=l: e.dma_start(out=pleg.unsqueeze(1),
                                                   in_=pleg_d[l:l + 1, :].partition_broadcast(128)),
                       "pleg", writes=[("pleg",)])

                def issue_p(t):
                    sc.dma("pool", lambda e, t=t, l=l, si=si: e.dma_start(out=pb[:, t % 2, :], in_=p_d[l, si, 128 * t:128 * t + 128, :]),
                           "p%d" % (t % 2), writes=[("pb", t % 2)])
                issue_p(0)
                issue_p(1)
                sc.dma("pool", lambda e, l=l: e.dma_start(out=wgatev, in_=wgate_d[l], max_dma_last_dim=4096), "wgate",
                       writes=[("wgate",)])
                sc.dma("pool", lambda e, l=l: e.dma_start(out=wplev, in_=wple_d[l], max_dma_last_dim=4096), "wple",
                       writes=[("wple",)])
                j2 = junk[:, :].rearrange("p (a b) -> p a b", a=2)
                pgv = postg.rearrange("p (a b) -> p a b", a=2)
                plv = pleg.rearrange("p (a b) -> p a b", a=2)
                def c1_A(t):
                    b0 = 2 * (t % 3)

                    def mmix(e, t=t, b0=b0):
                        ins = None
                        for n in range(2):
                            for kc in range(8):
                                lhs = mixT[:, kc, 128 * t:128 * t + 128] if kc < 4 else cvv[:, kc - 4, 128 * t:128 * t + 128]
                                ins = e.matmul(ps[:, b0 + n, :], lhs, woutv[:, kc, 512 * n:512 * n + 512],
                                               start=(kc == 0), stop=(kc == 7))
                        return ins
                    sc.op("pe", mmix, reads=[("wout",), ("w", 0), ("w", 1), ("mx", t)] + [("cv", c, t) for c in range(4)],
                          writes=[("ps", b0), ("ps", b0 + 1)])

                def c1_Bsq(t):
                    par = t % 2
                    b0 = 2 * (t % 3)
                    sc.op("act", lambda e, b0=b0, par=par: e.activation(out=j2, in_=ps[:, b0:b0 + 2, :], func=AF.Square,
                                                                      accum_out=ss[:, par:par + 1]),
                          writes=[("ps", b0), ("ps", b0 + 1), ("ss", par), ("junk",)])

                def c1_Btmp(t):
                    par = t % 2
                    b0 = 2 * (t % 3)
                    bA = bufA[:, par, :]
                    bAv = bA.rearrange("p (a b) -> p a b", a=2)
                    sc.op("dve", lambda e, bAv=bAv, b0=b0: e.tensor_tensor(out=bAv, in0=ps[:, b0:b0 + 2, :], in1=pgv, op=ALU.mult),
                          reads=[("postg",)], writes=[("ps", b0), ("ps", b0 + 1), ("bufA", par)])
                    sc.op("dve", lambda e, par=par: e.tensor_scalar(out=rs[:, par:par + 1], in0=ss[:, par:par + 1],
                                                                  scalar1=1.0 / D, scalar2=EPS, op0=ALU.mult, op1=ALU.add),
                          reads=[("ss", par)], writes=[("rs", par)])
                    sc.op("pool", lambda e, par=par: e.tensor_tensor(out=rstd[:, par:par + 1], in0=rs[:, par:par + 1],
                                                                   in1=neghalf[:, 0:1], op=ALU.pow),
                          reads=[("rs", par), ("neghalf",)], writes=[("rstd", par)])

                def c1_Bstt(t):
                    par = t % 2
                    bA = bufA[:, par, :]
                    sc.op("dve", lambda e, t=t, bA=bA, par=par: e.scalar_tensor_tensor(
                        out=x_sb[:, t, :], in0=bA, scalar=rstd[:, par:par + 1], in1=x_sb[:, t, :], op0=ALU.mult, op1=ALU.add),
                        reads=[("bufA", par), ("rstd", par)], writes=[("x", t)])

                def c1_C(t):
                    par = t % 2
                    sc.op("act", lambda e, t=t, par=par: e.copy(out=xb2[:, par, :], in_=x_sb[:, t, :]),
                          reads=[("x", t)], writes=[("xb", par)])

                def c1_Dtrx(t):
                    par = t % 2

                    def trx(e, par=par):
                        ins = None
                        for c in range(8):
                            ins = e.transpose(out=psbf[:, 6, 128 * c:128 * c + 128], in_=xb2[:, par, 128 * c:128 * c + 128],
                                              identity=ident[:, :])
                        return ins
                    sc.op("pe", trx, reads=[("xb", par), ("ident",)], writes=[("ps", 6)])

                def c1_Devac(t):
                    sc.op("dve", lambda e, t=t: e.tensor_copy(
                        out=mixT[:, :, 128 * t:128 * t + 128],
                        in_=psbf[:, 6, 0:512].rearrange("p (c t) -> p c t", c=4)),
                        writes=[("ps", 6), ("mx", t)])
                    sc.op("dve", lambda e, t=t: e.tensor_copy(
                        out=cvv[:, :, 128 * t:128 * t + 128],
                        in_=psbf[:, 6, 512:1024].rearrange("p (c t) -> p c t", c=4)),
                        writes=[("ps", 6)] + [("cv", c, t) for c in range(4)])

                c1_A(0)
                c1_A(1)
                c1_A(2)
                c1_Bsq(0)
                c1_Btmp(0)
                c1_Bstt(0)
                c1_Bsq(1)
                c1_Btmp(1)
                for k in range(16):
                    if k + 3 < 16:
                        c1_A(k + 3)
                    c1_C(k)
                    c1_Dtrx(k)
                    if k + 2 < 16:
                        c1_Bsq(k + 2)
                    c1_Devac(k)
                    if k + 1 < 16:
                        c1_Bstt(k + 1)
                    if k + 2 < 16:
                        c1_Btmp(k + 2)
                issue_w()
                issue_w()
                def c2_p(t):
                    par = t % 2

                    def trp(e, par=par):
                        ins = None
                        for kc in range(2):
                            ins = e.transpose(out=psbf[:, 6, 128 * kc:128 * kc + 128],
                                              in_=pb[:, par, 128 * kc:128 * kc + 128], identity=ident[:, :])
                        return ins
                    sc.op("pe", trp, reads=[("pb", par), ("ident",)], writes=[("ps", 6)])
                    sc.op("act", lambda e, par=par: e.copy(
                        out=pT2[:, par, :, :], in_=psbf[:, 6, 0:256].rearrange("p (c t) -> p c t", c=2)),
                        writes=[("ps", 6), ("pT", par)])
                    if t + 2 < 16:
                        issue_p(t + 2)

                c2_p(0)
                for t in range(16):
                    par = t % 2
                    g0 = 2 * par
                    e0 = 4
                    if t + 1 < 16:
                        c2_p(t + 1)

                    def mgate(e, t=t, g0=g0):
                        ins = None
                        for n in range(2):
                            for kc in range(8):
                                lhs = mixT[:, kc, 128 * t:128 * t + 128] if kc < 4 else cvv[:, kc - 4, 128 * t:128 * t + 128]
                                ins = e.matmul(ps[:, g0 + n, :], lhs, wgatev[:, kc, 512 * n:512 * n + 512],
                                               start=(kc == 0), stop=(kc == 7))
                        return ins
                    sc.op("pe", mgate, reads=[("wgate",), ("mx", t)] + [("cv", c, t) for c in range(4)],
                          writes=[("ps", g0), ("ps", g0 + 1)])

                    def mple(e, t=t, e0=e0):
                        ins = None
                        for n in range(2):
                            for kc in range(2):
                                ins = e.matmul(ps[:, e0 + n, :], pT2[:, t % 2, kc, :],
                                               wplev[:, kc, 512 * n:512 * n + 512], start=(kc == 0), stop=(kc == 1))
                        return ins
                    sc.op("pe", mple, reads=[("wple",), ("pT", par)], writes=[("ps", e0), ("ps", e0 + 1)])
                    bA = bufA[:, par, :]
                    bB = bufB[:, par, :]
                    bAv = bA.rearrange("p (a b) -> p a b", a=2)
                    bBv = bB.rearrange("p (a b) -> p a b", a=2)
                    sc.op("act", lambda e, e0=e0, par=par: e.activation(out=j2, in_=ps[:, e0:e0 + 2, :], func=AF.Square,
                                                                      accum_out=ss[:, 2 + par:3 + par]),
                          writes=[("ps", e0), ("ps", e0 + 1), ("ss", 2 + par), ("junk",)])
                    sc.op("act", lambda e, bAv=bAv, g0=g0: e.activation(out=bAv, in_=ps[:, g0:g0 + 2, :], func=AF.Sigmoid),
                          writes=[("ps", g0), ("ps", g0 + 1), ("bufA", par)])
                    sc.op("dve", lambda e, bBv=bBv, e0=e0: e.tensor_tensor(out=bBv, in0=ps[:, e0:e0 + 2, :], in1=plv, op=ALU.mult),
                          reads=[("pleg",)], writes=[("ps", e0), ("ps", e0 + 1), ("bufB", par)])
                    sc.op("dve", lambda e, par=par: e.tensor_scalar(out=rs[:, 2 + par:3 + par], in0=ss[:, 2 + par:3 + par],
                                                                  scalar1=1.0 / D, scalar2=EPS, op0=ALU.mult, op1=ALU.add),
                          reads=[("ss", 2 + par)], writes=[("rs", 2 + par)])
                    sc.op("pool", lambda e, par=par: e.tensor_tensor(out=rstd[:, 2 + par:3 + par], in0=rs[:, 2 + par:3 + par],
                                                                   in1=neghalf[:, 0:1], op=ALU.pow),
                          reads=[("rs", 2 + par), ("neghalf",)], writes=[("rstd", 2 + par)])
                    sc.op("dve", lambda e, bA=bA, bB=bB: e.tensor_tensor(out=bB, in0=bB, in1=bA, op=ALU.mult),
                          reads=[("bufA", par)], writes=[("bufB", par)])
                    sc.op("dve", lambda e, t=t, bB=bB, par=par: e.scalar_tensor_tensor(
                        out=x_sb[:, t, :], in0=bB, scalar=rstd[:, 2 + par:3 + par], in1=x_sb[:, t, :], op0=ALU.mult, op1=ALU.add),
                        reads=[("bufB", par), ("rstd", 2 + par)], writes=[("x", t)])
                    if last:
                        sc.dma("sp", lambda e, t=t, si=si: e.dma_start(out=out_d[si, 128 * t:128 * t + 128, :], in_=x_sb[:, t, :]),
                               "o%d" % t, reads=[("x", t)])
                sc.barrier()
                sc.op("pool", lambda e: e.memset(Vv[:, :, :, 128:130], 1.0), writes=[("V", t) for t in range(16)])

        except _Stop:
            pass
        sc.emit()
    return nc


_CACHE = {}


def _consts():
    ident = np.eye(128, dtype=np.float32).astype(ml_dtypes.bfloat16)
    rs = np.arange(128)[:, None].astype(np.float64)
    rt = np.arange(128)[None, :].astype(np.float64)
    dcorr = np.zeros((128, 5, 128), np.float32)
    bcol = np.zeros((128, 4, 17), np.float32)
    for h in range(4):
        m = SLOPES[h]
        db = np.where(rs > rt, -2.0 * m * (rs - rt), 0.0)
        masked = (rs // 64) > (rt // 64)
        db = np.where(masked, MASKV, db)
        dcorr[:, h, :] = db * 8.0
        for dl in range(-16, 1):
            bcol[:, h, 16 + dl] = m * (128.0 * dl + np.arange(128))
    dcorr[:, 4, :] = MASKV * 8.0
    return ident, dcorr.reshape(128, 640).astype(ml_dtypes.bfloat16), bcol.reshape(128, 68)


def prep_weights(inp):
    f = lambda a: np.ascontiguousarray(np.asarray(a, dtype=np.float32))
    w_in = f(inp["w_in"])
    parts = [w_in[:, :, 512 * i:512 * (i + 1)] for i in range(8)]
    cols = [parts[0], parts[1], parts[2], parts[3]]
    for c in range(4):
        cols.append(np.concatenate([parts[4][:, :, 128 * c:128 * c + 128], parts[5][:, :, 128 * c:128 * c + 128],
                                    parts[6][:, :, 128 * c:128 * c + 128], parts[7][:, :, 128 * c:128 * c + 128]], axis=2))
    wp = np.stack(cols, axis=1)
    wp = wp.reshape(NL, 8, 8, 128, 512).transpose(0, 1, 3, 2, 4)
    ident, dbase, bcol = _consts()
    sub = f(inp["subln_gain"])
    d = {
        "w_in": np.ascontiguousarray(wp),
        "w_out": np.ascontiguousarray(f(inp["w_out"]).reshape(NL, 8, 128, 1024).transpose(0, 2, 1, 3)),
        "w_gate": np.ascontiguousarray(f(inp["w_ple_gate"]).reshape(NL, 8, 128, 1024).transpose(0, 2, 1, 3)),
        "w_ple": np.ascontiguousarray(f(inp["w_ple_proj"]).reshape(NL, 2, 128, 1024).transpose(0, 2, 1, 3)),
        "pre_g": np.ascontiguousarray(f(inp["pre_norm_gain"]).reshape(NL, 8, 128).transpose(2, 0, 1).reshape(128, NL * 8)),
        "post_g": f(inp["post_norm_gain"]),
        "ple_g": f(inp["ple_norm_gain"]),
        "subln_g": np.ascontiguousarray(np.tile(sub, (1, 4))),
        "conv_w": np.ascontiguousarray(f(inp["conv_w"]).reshape(NL, 3, 4, 128).transpose(3, 0, 2, 1).reshape(128, NL * 12)),
        "lam": np.ascontiguousarray(np.concatenate([f(inp["lambda_q1"]), f(inp["lambda_k1"]),
                                                    f(inp["lambda_q2"]), f(inp["lambda_k2"])], axis=1)),
        "ident": ident, "dbase": dbase, "bcol": bcol,
    }
    return d


def kernel(**inp):
    x = np.asarray(inp["x"], dtype=np.float32)
    p = np.asarray(inp["p"], dtype=np.float32)
    wd = prep_weights(inp)
    nseq = x.shape[0] // NCORES
    if "nc" not in _CACHE:
        _CACHE["nc"] = build(nseq=nseq, layers=(0, 1))
    nc = _CACHE["nc"]
    in_maps = []
    for c in range(NCORES):
        m = dict(wd)
        m["x"] = np.ascontiguousarray(x[c * nseq:(c + 1) * nseq])
        m["p"] = np.ascontiguousarray(p[:, c * nseq:(c + 1) * nseq])
        in_maps.append(m)
    res = run_bass_kernel_spmd(nc, in_maps, core_ids=list(range(NCORES)))
    return np.concatenate([r["out"] for r in res.results], axis=0)
```

```python
import math
from contextlib import ExitStack

import numpy as np
import ml_dtypes

import concourse.bass as bass
import concourse.mybir as mybir
from concourse.bass_utils import run_bass_kernel_spmd

F32 = mybir.dt.float32
BF16 = mybir.dt.bfloat16
AF = mybir.ActivationFunctionType
ALU = mybir.AluOpType
AX = mybir.AxisListType

S = 2048
D = 1024
NT = S // 128
NL = 2
NCORES = 8
EPS = 1e-6
SUBLN_EPS = 1e-5
SLOPES = [2.0 ** (-8.0 * (h + 1) / 4) for h in range(4)]
MASKV = -30000.0


class Sched:
    ENG = ["pe", "act", "dve", "pool", "sp"]

    def __init__(self, nc):
        self.nc = nc
        self.q = {e: [] for e in self.ENG}
        self.cnt = {e: 0 for e in self.ENG}
        self.waited = {e: {} for e in self.ENG}
        self.lastw = {}
        self.readers = {}
        self.dmacnt = {}
        self.pending = {e: [] for e in self.ENG}

    def _deps(self, eng, reads, writes):
        toks = list(self.pending[eng])
        self.pending[eng] = []
        for k in reads:
            if k in self.lastw:
                toks.append(self.lastw[k])
        for k in writes:
            if k in self.lastw:
                toks.append(self.lastw[k])
            toks.extend(self.readers.get(k, ()))
        need = {}
        for (s, v) in toks:
            if eng == "pe" and s == ("e", "pe"):
                continue
            if v > need.get(s, 0):
                need[s] = v
        out = []
        for s, v in need.items():
            if self.waited[eng].get(s, 0) < v:
                self.waited[eng][s] = v
                out.append((s, v))
        return out

    def _commit(self, tok, reads, writes):
        for k in writes:
            self.lastw[k] = tok
            self.readers[k] = []
        for k in reads:
            if k in writes:
                continue
            self.readers.setdefault(k, []).append(tok)

    @staticmethod
    def _excl(reads, writes):
        r = [k for k in reads if k[0] != "ps"]
        w = list(writes) + [k for k in reads if k[0] == "ps"]
        return r, w

    def op(self, eng, fn, reads=(), writes=()):
        reads, writes = self._excl(reads, writes)
        waits = self._deps(eng, reads, writes)
        self.cnt[eng] += 1
        tok = (("e", eng), self.cnt[eng])
        self.q[eng].append((waits, fn, tok))
        self._commit(tok, reads, writes)

    def dma(self, eng, fn, sem, reads=(), writes=()):
        waits = self._deps(eng, reads, writes)
        self.dmacnt[sem] = self.dmacnt.get(sem, 0) + 16
        tok = (("d", sem), self.dmacnt[sem])
        self.q[eng].append((waits, fn, tok))
        self._commit(tok, reads, writes)

    def barrier(self):
        toks = [(("e", e), self.cnt[e]) for e in self.ENG if self.cnt[e] > 0]
        toks += [(("d", s), v) for s, v in self.dmacnt.items()]
        for e in self.ENG:
            self.pending[e].extend(toks)

    def emit(self):
        nc = self.nc
        sems = {}
        with ExitStack() as st:
            for e in self.ENG:
                sems[("e", e)] = st.enter_context(nc.semaphore("s_" + e))
            for i, s in enumerate(self.dmacnt):
                sems[("d", s)] = st.enter_context(nc.semaphore("d_%d" % i))
            block = st.enter_context(nc.Block())
            final = [(("e", e), self.cnt[e]) for e in self.ENG if self.cnt[e] > 0]
            final += [(("d", s), v) for s, v in self.dmacnt.items()]

            def mk(e):
                def body(engobj):
                    for waits, fn, tok in self.q[e]:
                        for s, v in waits:
                            engobj.wait_ge(sems[s], v)
                        inst = fn(engobj)
                        inst.then_inc(sems[tok[0]], 1 if tok[0][0] == "e" else 16)
                    if e == "sp":
                        for s, v in final:
                            engobj.wait_ge(sems[s], v)
                return body

            block.tensor(mk("pe"))
            block.scalar(mk("act"))
            block.vector(mk("dve"))
            block.gpsimd(mk("pool"))
            block.sync(mk("sp"))


class _Stop(Exception):
    pass


def build(nseq=2, layers=(0, 1), debug=False, upto=None, dbg_li=0):
    nc = bass.Bass("TRN2", target_bir_lowering=False)
    NLK = len(layers)

    def din(name, shape, dt=F32):
        return nc.dram_tensor(name, list(shape), dt, kind="ExternalInput").ap()

    x_d = din("x", [nseq, S, D])
    p_d = din("p", [NL, nseq, S, 256])
    win_d = din("w_in", [NL, 8, 128, 8, 512])
    wout_d = din("w_out", [NL, 128, 8, 1024])
    wgate_d = din("w_gate", [NL, 128, 8, 1024])
    wple_d = din("w_ple", [NL, 128, 2, 1024])
    preg_d = din("pre_g", [128, NL * 8])
    postg_d = din("post_g", [NL, 1024])
    pleg_d = din("ple_g", [NL, 1024])
    subln_d = din("subln_g", [NL, 512])
    convw_d = din("conv_w", [128, NL * 12])
    lam_d = din("lam", [NL, 256])
    ident_d = din("ident", [128, 128], BF16)
    dbase_d = din("dbase", [128, 640], BF16)
    bcol_d = din("bcol", [128, 68])
    out_d = nc.dram_tensor("out", [nseq, S, D], F32, kind="ExternalOutput").ap()
    dbg_d = {}
    if debug:
        for nm, shp, dt in [("dQ", [128, 4 * S], BF16), ("dK", [128, 4 * S], BF16),
                            ("dV", [128, 16 * 4 * 130], BF16), ("dza", [128, 16 * 512], BF16),
                            ("dcv", [128, 4 * S], BF16), ("dmx", [128, 4 * S], BF16),
                            ("dhT", [128, 8 * 1024], BF16)]:
            dbg_d[nm] = nc.dram_tensor(nm, shp, dt, kind="ExternalOutput").ap()

    sc = Sched(nc)
    st = ExitStack()

    def sb(name, shape, dt):
        return st.enter_context(nc.sbuf_tensor("sb_" + name, list(shape), dt))

    with st:
        x_sb = sb("x_sb", [128, NT, D], F32)
        hT_r = sb("hT_r", [128, 8 * 1024], BF16)
        wbuf = sb("wbuf", [128, 2, 8, 512], BF16)
        Q_r = sb("Q_r", [128, 4 * S], BF16)
        K_r = sb("K_r", [128, 4 * S], BF16)
        V_r = sb("V_r", [128, 16 * 4 * 130], BF16)
        za_r = sb("za_r", [128, 16 * 512], BF16)
        cv_r = sb("cv_r", [128, 4 * S], BF16)
        hn = sb("hn", [128, 2, D], BF16)
        tmp_r = sb("tmp_r", [128, 4480], F32)
        pb = sb("pb", [128, 2, 256], BF16)
        preg = sb("preg", [128, NL * 8], F32)
        convw = sb("convw", [128, NL * 12], F32)
        sg4 = sb("sg4", [128, 512], F32)
        lamv = sb("lamv", [128, 256], F32)
        ident = sb("ident", [128, 128], BF16)
        dbase = sb("dbase", [128, 640], BF16)
        bcol = sb("bcol", [128, 68], F32)
        small = sb("small", [128, 64], F32)
        lamtmp = sb("lamtmp", [128, 128], F32)
        agb_t = sb("agb_t", [128, 2, 512], BF16)
        ps = st.enter_context(nc.psum_tensor("ps", [128, 8, 512], F32))
        psbf = ps.bitcast(BF16)

        hT = hT_r[:, :].rearrange("p (c t) -> p c t", c=8)
        mixT = hT_r[:, :].rearrange("p (c t) -> p c t", c=4)
        Qv = Q_r[:, :].rearrange("p (h t) -> p h t", h=4)
        Kv = K_r[:, :].rearrange("p (h t) -> p h t", h=4)
        woutv = wbuf[:, :, :, :].rearrange("p s c n -> p (s c n)").rearrange("p (c n) -> p c n", c=8)
        wgatev = K_r[:, :].rearrange("p (c n) -> p c n", c=8)
        Vv = V_r[:, :].rearrange("p (t h e) -> p t h e", t=16, h=4)
        wplev = V_r[:, 0:2048].rearrange("p (c n) -> p c n", c=2)
        postg = V_r[:, 2048:4096].bitcast(F32)
        pleg = V_r[:, 4096:6144].bitcast(F32)
        zav = za_r[:, :].rearrange("p (t n) -> p t n", t=16)
        za_f = za_r[:, :].bitcast(F32)
        bufA = za_f[:, 0:2048].rearrange("p (a b) -> p a b", a=2)
        bufB = za_f[:, 2048:4096].rearrange("p (a b) -> p a b", a=2)
        xb2 = tmp_r[:, 0:1024].bitcast(BF16).rearrange("p (a b) -> p a b", a=2)
        pT2 = tmp_r[:, 1024:1280].bitcast(BF16).rearrange("p (a c t) -> p a c t", a=2, c=2)
        cvv = cv_r[:, :].rearrange("p (c t) -> p c t", c=4)
        cs = tmp_r[:, 0:512]
        gb = tmp_r[:, 512:1540].rearrange("p (a b) -> p a b", a=2)
        cva = tmp_r[:, 1540:2052]
        szb = tmp_r[:, 2052:2564]
        bzb = tmp_r[:, 2564:3076]
        junk = tmp_r[:, 3456:3968].bitcast(BF16)
        PTr = tmp_r[:, 0:768].bitcast(BF16).rearrange("p (s q) -> p s q", s=3)
        a1 = tmp_r[:, 768:896]
        araw_sets = [tmp_r[:, 896:1920].rearrange("p (a b) -> p a b", a=2),
                     tmp_r[:, 2432:3456].rearrange("p (a b) -> p a b", a=2)]
        sqb = tmp_r[:, 1920:2432]
        agb2 = agb_t[:, :, :]
        zf2 = tmp_r[:, 2432:3456].rearrange("p (a b) -> p a b", a=2)
        Qz = tmp_r[:, 3456:4480].bitcast(BF16).rearrange("p (z m q) -> p z m q", z=4, m=2)
        ss = small[:, 0:4]
        rs = small[:, 4:8]
        rstd = small[:, 8:12]
        neghalf = small[:, 12:16]
        lam_s = small[:, 16:20]
        rl = small[:, 24:26]
        rl2 = small[:, 26:27]
        ssh = small[:, 28:32]
        rsh = small[:, 32:36]
        rstdh = small[:, 36:40]
        crr = small[:, 40:48].rearrange("p (c k) -> p c k", c=4)

        def ld(eng, out_ap, in_ap, sem, writes):
            sc.dma(eng, lambda e: e.dma_start(out=out_ap, in_=in_ap), sem, writes=writes)

        ld("sp", ident[:, :], ident_d, "c0", [("ident",)])
        ld("sp", dbase[:, :], dbase_d, "c0", [("dbase",)])
        ld("sp", bcol[:, :], bcol_d, "c0", [("bcol",)])
        ld("sp", preg[:, :], preg_d, "c0", [("preg",)])
        ld("sp", convw[:, :], convw_d, "c0", [("convw",)])
        sc.barrier()
        sc.op("dve", lambda e: e.memset(neghalf, -0.5), writes=[("neghalf",)])
        sc.op("pool", lambda e: e.memset(V_r[:, :], 1.0), writes=[("V", t) for t in range(16)])

        bank_ctr = [0]

        def nextbank():
            b = bank_ctr[0] % 8
            bank_ctr[0] += 1
            return b

        alt = [0]

        def alt_eng():
            alt[0] += 1
            return "act" if alt[0] % 2 else "dve"

        wq = []
        for si in range(nseq):
            for l in layers:
                for hf in range(2):
                    for j in range(8):
                        wq.append((l, j))
        wissued = [0]

        def issue_w():
            n = wissued[0]
            if n >= len(wq):
                return
            l, j = wq[n]
            slot = n % 2
            sc.dma("pool", lambda e: e.dma_start(out=wbuf[:, slot, :, :], in_=win_d[l, j],
                                                  max_dma_last_dim=4096),
                   "w%d" % slot, writes=[("w", slot)])
            wissued[0] += 1

        issue_w()
        wused = [0]

        for li0, l0 in enumerate(layers):
            linit = 0.8 - 0.6 * math.exp(-0.3 * l0)
            sc.dma("sp", lambda e, l0=l0: e.dma_start(out=lamv[:, :].unsqueeze(1),
                                                  in_=lam_d[l0:l0 + 1, :].partition_broadcast(128)),
                   "lamv", writes=[("lamv",)])
            lv = lamv[:, :].rearrange("p (a b d) -> p a b d", a=2, b=2)
            prv = lamtmp[:, :].rearrange("p (a d) -> p a d", a=2)
            sc.op("dve", lambda e, lv=lv, prv=prv: e.tensor_tensor(out=prv, in0=lv[:, :, 0, :], in1=lv[:, :, 1, :],
                                                               op=ALU.mult),
                  reads=[("lamv",)], writes=[("lamtmp",)])
            sc.op("dve", lambda e, prv=prv: e.tensor_reduce(out=lam_s[:, 0:2], in_=prv, axis=AX.X, op=ALU.add),
                  reads=[("lamtmp",)], writes=[("lam_s",)])
            sc.op("act", lambda e: e.activation(out=lam_s[:, 2:4], in_=lam_s[:, 0:2], func=AF.Exp),
                  reads=[("lam_s",)], writes=[("lam_e",)])
            sc.op("dve", lambda e, linit=linit, li0=li0: e.tensor_scalar(
                out=small[:, 20 + li0:21 + li0], in0=lam_s[:, 3:4], scalar1=lam_s[:, 2:3],
                scalar2=-linit, op0=ALU.subtract, op1=ALU.add),
                reads=[("lam_e",)], writes=[("neglam", li0)])
        try:
          if upto == "consts":
              raise _Stop()
          for si in range(nseq):
            for li, l in enumerate(layers):
                lambda_init = 0.8 - 0.6 * math.exp(-0.3 * l)
                first = (li == 0)
                last = (li == NLK - 1)
                neglam = small[:, 20 + li:21 + li]
                sc.dma("sp", lambda e, l=l: e.dma_start(out=sg4[:, :].unsqueeze(1),
                                                   in_=subln_d[l:l + 1, :].partition_broadcast(128)),
                       "sg4", writes=[("sg4",)])
                sc.op("dve", lambda e, lambda_init=lambda_init: e.tensor_scalar(out=sg4[:, :], in0=sg4[:, :],
                                                       scalar1=(1.0 - lambda_init), scalar2=None, op0=ALU.mult),
                      reads=[("sg4",)], writes=[("sg4",)])

                if upto == "lam":
                    raise _Stop()
                for hf in range(2):
                    def n_stat(t):
                        sl = t % 4
                        if first:
                            sc.dma("sp", lambda e, t=t, si=si: e.dma_start(out=x_sb[:, t, :],
                                                                    in_=x_d[si, 128 * t:128 * t + 128, :]),
                                   "x%d" % t, writes=[("x", t)])
                        sc.op("act", lambda e, t=t, sl=sl: e.activation(out=junk[:, :], in_=x_sb[:, t, :],
                                                                      func=AF.Square,
                                                                      accum_out=ss[:, sl:sl + 1]),
                              reads=[("x", t)], writes=[("ss", sl), ("junk",)])
                        sc.op("dve", lambda e, sl=sl: e.tensor_scalar(out=rs[:, sl:sl + 1], in0=ss[:, sl:sl + 1],
                                                                    scalar1=1.0 / D, scalar2=EPS,
                                                                    op0=ALU.mult, op1=ALU.add),
                              reads=[("ss", sl)], writes=[("rs", sl)])
                        sc.op("pool", lambda e, sl=sl: e.tensor_tensor(out=rstd[:, sl:sl + 1], in0=rs[:, sl:sl + 1],
                                                                     in1=neghalf[:, 0:1], op=ALU.pow),
                              reads=[("rs", sl), ("neghalf",)], writes=[("rstd", sl)])

                    def n_norm(t):
                        sl = t % 4
                        hb = t % 2
                        tt = t % 4
                        par = (t // 4) % 2
                        sc.op("dve", lambda e, t=t, sl=sl, hb=hb: e.tensor_scalar(
                            out=hn[:, hb, :], in0=x_sb[:, t, :], scalar1=rstd[:, sl:sl + 1], scalar2=None,
                            op0=ALU.mult),
                            reads=[("x", t), ("rstd", sl)], writes=[("hn", hb)])

                        def tr(e, tt=tt, hb=hb, par=par):
                            ins = None
                            for c in range(8):
                                bk = 4 * par + c // 2
                                off = (c % 2) * 512 + 128 * tt
                                ins = e.transpose(out=psbf[:, bk, off:off + 128],
                                                  in_=hn[:, hb, 128 * c:128 * c + 128], identity=ident[:, :])
                            return ins
                        sc.op("pe", tr, reads=[("hn", hb), ("ident",)],
                              writes=[("ps", 4 * par + b) for b in range(4)])

                    def n_evac(G):
                        g = G % 2
                        par = G % 2
                        for c in range(8):
                            bk = 4 * par + c // 2
                            off = (c % 2) * 512
                            eng = "act" if (c // 2) % 2 == 0 else "dve"
                            gcol = preg[:, l * 8 + c:l * 8 + c + 1]
                            dst = hT[:, c, 512 * g:512 * g + 512]
                            src = psbf[:, bk, off:off + 512]
                            if eng == "act":
                                sc.op("act", lambda e, dst=dst, src=src, gcol=gcol: e.mul(out=dst, in_=src, mul=gcol),
                                      reads=[("ps", bk), ("preg",)], writes=[("hT", g, c)])
                            else:
                                sc.op("dve", lambda e, dst=dst, src=src, gcol=gcol: e.tensor_scalar(
                                    out=dst, in0=src, scalar1=gcol, scalar2=None, op0=ALU.mult),
                                    reads=[("ps", bk), ("preg",)], writes=[("hT", g, c)])

                    tbase = 8 * hf
                    n_stat(tbase)
                    n_stat(tbase + 1)
                    for k in range(8):
                        if k + 2 < 8:
                            n_stat(tbase + k + 2)
                        n_norm(tbase + k)
                        if k % 4 == 3:
                            n_evac(2 * hf + k // 4)
                    if upto == "N":
                        raise _Stop()

                    if debug and si == 0 and li == dbg_li and hf == 0:
                        sc.dma("sp", lambda e: e.dma_start(out=dbg_d["dhT"], in_=hT_r[:, :]), "dbg",
                               reads=[("hT", g, c) for g in range(2) for c in range(8)])

                    for j in range(8):
                        slot = wused[0] % 2
                        if not (hf == 1 and j == 7) and wissued[0] <= wused[0] + 1:
                            issue_w()
                        wused[0] += 1
                        wk = ("w", slot)
                        if j < 2:
                            dstv = Qv if j == 0 else Kv
                            dk = "Q" if j == 0 else "K"
                            for m in range(4):
                                for g in range(2):
                                    G = 2 * hf + g
                                    bk = nextbank()

                                    def mm(e, m=m, g=g, bk=bk, slot=slot):
                                        ins = None
                                        for kc in range(8):
                                            ins = e.matmul(ps[:, bk, :], wbuf[:, slot, kc, 128 * m:128 * m + 128],
                                                           hT[:, kc, 512 * g:512 * g + 512],
                                                           start=(kc == 0), stop=(kc == 7))
                                        return ins
                                    sc.op("pe", mm, reads=[wk] + [("hT", g, c) for c in range(8)],
                                          writes=[("ps", bk)])
                                    dst = dstv[:, m, 512 * G:512 * G + 512]
                                    eng = alt_eng()
                                    if eng == "act":
                                        sc.op("act", lambda e, dst=dst, bk=bk: e.copy(out=dst, in_=ps[:, bk, :]),
                                              reads=[("ps", bk)], writes=[(dk, m, G)])
                                    else:
                                        sc.op("dve", lambda e, dst=dst, bk=bk: e.tensor_copy(out=dst, in_=ps[:, bk, :]),
                                              reads=[("ps", bk)], writes=[(dk, m, G)])
                        elif j < 4:
                            for tt in range(8):
                                t = 8 * hf + tt
                                g = tt // 4
                                bk = nextbank()

                                def mm(e, tt=tt, bk=bk, slot=slot):
                                    ins = None
                                    for kc in range(8):
                                        ins = e.matmul(ps[:, bk, :], hT[:, kc, 128 * tt:128 * tt + 128],
                                                       wbuf[:, slot, kc, :], start=(kc == 0), stop=(kc == 7))
                                    return ins
                                sc.op("pe", mm, reads=[wk] + [("hT", g, c) for c in range(8)], writes=[("ps", bk)])
                                if j == 2:
                                    eng = alt_eng()
                                    dst = Vv[:, t, :, 0:128]
                                    src = ps[:, bk, :].rearrange("p (h e) -> p h e", h=4)
                                    if eng == "act":
                                        sc.op("act", lambda e, dst=dst, src=src: e.copy(out=dst, in_=src),
                                              reads=[("ps", bk)], writes=[("V", t)])
                                    else:
                                        sc.op("dve", lambda e, dst=dst, src=src: e.tensor_copy(out=dst, in_=src),
                                              reads=[("ps", bk)], writes=[("V", t)])
                                else:
                                    sc.op("act", lambda e, bk=bk: e.activation(out=szb, in_=ps[:, bk, :], func=AF.Silu),
                                          reads=[("ps", bk)], writes=[("szb",)])
                                    sc.op("dve", lambda e, t=t: e.tensor_tensor(out=zav[:, t, :], in0=szb, in1=sg4[:, :],
                                                                              op=ALU.mult),
                                          reads=[("szb",), ("sg4",)], writes=[("za", t)])
                        else:
                            c = j - 4
                            for g in range(2):
                                G = 2 * hf + g
                                gp = G % 2
                                bks = [nextbank() for _ in range(4)]
                                for part in range(4):
                                    bk = bks[part]

                                    def mm(e, part=part, g=g, bk=bk, slot=slot):
                                        ins = None
                                        for kc in range(8):
                                            ins = e.matmul(ps[:, bk, :],
                                                           wbuf[:, slot, kc, 128 * part:128 * part + 128],
                                                           hT[:, kc, 512 * g:512 * g + 512],
                                                           start=(kc == 0), stop=(kc == 7))
                                        return ins
                                    sc.op("pe", mm, reads=[wk] + [("hT", g, cc) for cc in range(8)],
                                          writes=[("ps", bk)])
                                bb, bc, bu, bz = bks
                                if G == 0:
                                    sc.op("pool", lambda e, gp=gp: e.memset(gb[:, gp, 0:2], 0.0), writes=[("gbc", gp)])
                                else:
                                    sc.op("pool", lambda e, gp=gp, c=c: e.tensor_copy(out=gb[:, gp, 0:2], in_=crr[:, c, :]),
                                          reads=[("crr", c)], writes=[("gbc", gp)])
                                sc.op("act", lambda e, bc=bc: e.copy(out=cs, in_=ps[:, bc, :]),
                                      reads=[("ps", bc)], writes=[("cs",)])
                                sc.op("dve", lambda e, bu=bu, gp=gp: e.tensor_tensor(out=gb[:, gp, 2:514], in0=ps[:, bu, :],
                                                                                   in1=cs, op=ALU.mult),
                                      reads=[("ps", bu), ("cs",)], writes=[("gb", gp)])
                                w0 = convw[:, l * 12 + c * 3 + 0:l * 12 + c * 3 + 1]
                                w1 = convw[:, l * 12 + c * 3 + 1:l * 12 + c * 3 + 2]
                                w2 = convw[:, l * 12 + c * 3 + 2:l * 12 + c * 3 + 3]
                                sc.op("dve", lambda e, gp=gp, w0=w0: e.tensor_scalar(out=cva, in0=gb[:, gp, 0:512], scalar1=w0,
                                                                                   scalar2=None, op0=ALU.mult),
                                      reads=[("gb", gp), ("gbc", gp), ("convw",)], writes=[("cva",)])
                                sc.op("dve", lambda e, gp=gp, w1=w1: e.scalar_tensor_tensor(
                                    out=cva, in0=gb[:, gp, 1:513], scalar=w1, in1=cva, op0=ALU.mult, op1=ALU.add),
                                    reads=[("gb", gp), ("gbc", gp), ("cva",)], writes=[("cva",)])
                                sc.op("dve", lambda e, gp=gp, w2=w2: e.scalar_tensor_tensor(
                                    out=cva, in0=gb[:, gp, 2:514], scalar=w2, in1=cva, op0=ALU.mult, op1=ALU.add),
                                    reads=[("gb", gp), ("cva",)], writes=[("cva",)])
                                sc.op("pool", lambda e, gp=gp, c=c: e.tensor_copy(out=crr[:, c, :], in_=gb[:, gp, 512:514]),
                                      reads=[("gb", gp)], writes=[("crr", c)])
                                sc.op("act", lambda e, bz=bz: e.activation(out=szb, in_=ps[:, bz, :], func=AF.Silu),
                                      reads=[("ps", bz)], writes=[("szb",)])
                                sc.op("dve", lambda e, bb=bb: e.tensor_tensor(out=bzb, in0=ps[:, bb, :], in1=szb, op=ALU.mult),
                                      reads=[("ps", bb), ("szb",)], writes=[("bzb",)])
                                sc.op("dve", lambda e, c=c, G=G: e.tensor_tensor(out=cvv[:, c, 512 * G:512 * G + 512],
                                                                                 in0=cva, in1=bzb, op=ALU.mult),
                                      reads=[("cva",), ("bzb",)], writes=[("cv", c, 4 * G + k) for k in range(4)])

                sc.barrier()
                if upto == "A":
                    raise _Stop()
                sc.dma("pool", lambda e, l=l: e.dma_start(out=woutv, in_=wout_d[l], max_dma_last_dim=4096), "wout",
                       writes=[("wout",), ("w", 0), ("w", 1)])
                steps = [(I, h, j) for I in range(8) for h in range(4) for j in range(2 * I + 2)]
                LOOK = 2
                nsteps = len(steps)
                sc.op("pool", lambda e: e.memset(Qz[:, :, :, :], 0.0), writes=[("Qz", z) for z in range(4)])

                def emit_qz(I, h):
                    zb = (4 * I + h) % 4
                    for m in range(2):
                        if I < 4:
                            sc.op("act", lambda e, zb=zb, m=m, I=I, h=h: e.copy(
                                out=Qz[64 * m:64 * m + 64, zb, m, :], in_=Qv[64 * m:64 * m + 64, h, 256 * I:256 * I + 256]),
                                reads=[("Q", h, I // 2)], writes=[("Qz", zb)])
                        else:
                            sc.op("pool", lambda e, zb=zb, m=m, I=I, h=h: e.tensor_copy(
                                out=Qz[64 * m:64 * m + 64, zb, m, :], in_=Qv[64 * m:64 * m + 64, h, 256 * I:256 * I + 256]),
                                reads=[("Q", h, I // 2)], writes=[("Qz", zb)])

                def emit_qk(n):
                    I, h, j = steps[n]
                    i0 = 2 * I
                    sbi = n % 3
                    zb = (4 * I + h) % 4
                    if j == 0:
                        nh = 4 * I + h + 1
                        if nh < 32:
                            emit_qz(nh // 4, nh % 4)

                    def qk(e, I=I, h=h, j=j, sbi=sbi, zb=zb, i0=i0):
                        ins = None
                        for m in range(2):
                            ins = e.matmul(ps[:, sbi, 256 * m:256 * m + 256],
                                           Kv[:, h, 128 * j:128 * j + 128], Qz[:, zb, m, :],
                                           start=(m == 0), stop=True, skip_group_check=True)
                        dc = dbase[:, 128 * h:128 * h + 128]
                        if j == i0:
                            for m in range(2):
                                ins = e.matmul(ps[:, sbi, 256 * m:256 * m + 128], ident[:, :], dc,
                                               start=False, stop=True, skip_group_check=True)
                        if j == i0 + 1:
                            for m in range(2):
                                ins = e.matmul(ps[:, sbi, 256 * m + 128:256 * m + 256], ident[:, :], dc,
                                               start=False, stop=True, skip_group_check=True)
                                ins = e.matmul(ps[:, sbi, 256 * m:256 * m + 128], ident[:, :], dbase[:, 512:640],
                                               start=False, stop=True, skip_group_check=True)
                        return ins
                    sc.op("pe", qk, reads=[("K", h, j // 4), ("Qz", zb), ("ident",), ("dbase",)],
                          writes=[("ps", sbi)])

                gcount = 0
                hits = {}

                def chk(nm):
                    hits[nm] = hits.get(nm, 0) + 1
                    if upto == nm or upto == "%s#%d" % (nm, hits[nm]):
                        raise _Stop()
                emit_qz(0, 0)
                for n in range(min(LOOK, nsteps)):
                    emit_qk(n)
                deferred = []
                for n in range(nsteps):
                    I, h, j = steps[n]
                    i0 = 2 * I
                    sbi = n % 3
                    obase = 3 + 2 * (gcount % 2)
                    while deferred and deferred[0][0] <= n:
                        deferred.pop(0)[1]()
                    bidx = h * 17 + 16 + (j - i0 - 1)
                    bias = bcol[:, bidx:bidx + 1]
                    sc.op("act", lambda e, sbi=sbi, bias=bias: e.activation(
                        out=PTr[:, sbi, :], in_=ps[:, sbi, :], func=AF.Exp, bias=bias, scale=0.125),
                        reads=[("bcol",)], writes=[("ps", sbi), ("PT", sbi)])
                    if n + LOOK < nsteps:
                        emit_qk(n + LOOK)

                    def pv(e, sbi=sbi, j=j, h=h, i0=i0, obase=obase):
                        ins = None
                        for a in range(2):
                            if a == 0 and j == i0 + 1:
                                continue
                            for m in range(2):
                                ins = e.matmul(ps[:, obase + a, 130 * m:130 * m + 130],
                                               PTr[:, sbi, 256 * m + 128 * a:256 * m + 128 * a + 128], Vv[:, j, h, :],
                                               start=(j == 0 and m == 0), stop=True, skip_group_check=True)
                        return ins
                    sc.op("pe", pv, reads=[("PT", sbi), ("V", j)], writes=[("ps", obase), ("ps", obase + 1)])
                    chk("B3")
                    if j == i0 + 1:
                        gcount += 1
                        for a in range(2):
                            ob = obase + a
                            o3 = ps[:, ob, 0:260].rearrange("p (a b) -> p a b", a=2)
                            sc.op("dve", lambda e, o3=o3: e.reciprocal(out=rl, in_=o3[:, :, 128]),
                                  writes=[("ps", ob), ("rl",)])
                            sc.op("dve", lambda e, neglam=neglam: e.tensor_scalar(out=rl2, in0=rl[:, 1:2], scalar1=neglam,
                                                                                  scalar2=None, op0=ALU.mult),
                                  reads=[("rl",), ("neglam", li)], writes=[("rl2",)])
                            sc.op("dve", lambda e, ob=ob: e.tensor_scalar(out=a1, in0=ps[:, ob, 0:128], scalar1=rl[:, 0:1],
                                                                         scalar2=None, op0=ALU.mult),
                                  reads=[("rl",)], writes=[("ps", ob), ("a1",)])
                            sc.op("dve", lambda e, ob=ob, a=a, h=h, I=I: e.scalar_tensor_tensor(
                                out=araw_sets[I % 2][:, a, 128 * h:128 * h + 128], in0=ps[:, ob, 130:258], scalar=rl2, in1=a1,
                                op0=ALU.mult, op1=ALU.add),
                                reads=[("rl2",), ("a1",)], writes=[("ps", ob), ("araw", I % 2, a, h)])
                        chk("B4d")
                        if h == 3:
                            for a in range(2):
                                i = i0 + a

                                def chain(a=a, i=i, I=I):
                                    ar = araw_sets[I % 2][:, a, :]
                                    ar3 = ar.rearrange("p (h d) -> p h d", h=4)
                                    sq3 = sqb.rearrange("p (h d) -> p h d", h=4)
                                    sc.op("dve", lambda e, ar=ar: e.tensor_tensor(out=sqb, in0=ar, in1=ar, op=ALU.mult),
                                          reads=[("araw", I % 2, a, hh) for hh in range(4)], writes=[("sqb",)])
                                    sc.op("dve", lambda e, sq3=sq3: e.tensor_reduce(out=ssh, in_=sq3, axis=AX.X, op=ALU.add),
                                          reads=[("sqb",)], writes=[("ssh",)])
                                    sc.op("dve", lambda e: e.tensor_scalar(out=rsh, in0=ssh, scalar1=1.0 / 128, scalar2=SUBLN_EPS,
                                                                           op0=ALU.mult, op1=ALU.add),
                                          reads=[("ssh",)], writes=[("rsh",)])
                                    sc.op("pool", lambda e: e.tensor_tensor(out=rstdh, in0=rsh, in1=neghalf, op=ALU.pow),
                                          reads=[("rsh",), ("neghalf",)], writes=[("rstdh",)])
                                    for hh in range(4):
                                        sc.op("dve", lambda e, ar3=ar3, hh=hh, a=a, i=i: e.scalar_tensor_tensor(
                                            out=agb2[:, a, 128 * hh:128 * hh + 128], in0=ar3[:, hh, :], scalar=rstdh[:, hh:hh + 1],
                                            in1=zav[:, i, 128 * hh:128 * hh + 128], op0=ALU.mult, op1=ALU.mult),
                                            reads=[("araw", I % 2, a, hh), ("rstdh",), ("za", i)], writes=[("agb", a)])

                                def tr(e, a=a):
                                    ins = None
                                    for c in range(4):
                                        ins = e.transpose(out=psbf[:, 7, 128 * c:128 * c + 128],
                                                          in_=agb2[:, a, 128 * c:128 * c + 128], identity=ident[:, :])
                                    return ins

                                def fin(tr=tr, i=i, a=a):
                                    sc.op("pe", tr, reads=[("agb", a), ("ident",)], writes=[("ps", 7)])
                                    sc.op("act", lambda e, i=i: e.copy(
                                        out=mixT[:, :, 128 * i:128 * i + 128],
                                        in_=psbf[:, 7, 0:512].rearrange("p (c t) -> p c t", c=4)),
                                        writes=[("ps", 7), ("mx", i)])
                                nk = 2 * I + 4
                                cdue = n + 1 + (a + 1) * nk
                                deferred.append((cdue, chain))
                                deferred.append((cdue + min(20, 8 * I + 6), fin))
                                deferred.sort(key=lambda x: x[0])
                            chk("B6")

                while deferred:
                    deferred.pop(0)[1]()
                if debug and si == 0 and li == dbg_li:
                    sc.barrier()
                    for nm, r in [("dQ", Q_r), ("dK", K_r), ("dV", V_r), ("dza", za_r), ("dcv", cv_r), ("dmx", hT_r)]:
                        sc.dma("sp", lambda e, nm=nm, r=r: e.dma_start(out=dbg_d[nm], in_=r[:, :]), "dbg")

                if upto == "B":
                    raise _Stop()
                sc.barrier()
                sc.dma("sp", lambda e, l=l: e.dma_start(out=postg.unsqueeze(1),
                                                   in_=postg_d[l:l + 1, :].partition_broadcast(128)),
                       "postg", writes=[("postg",)])
                sc.dma("sp", lambda e, l=l: e.dma_start(out=pleg.unsqueeze(1),
                                                   in_=pleg_d[l:l + 1, :].partition_broadcast(128)),
                       "pleg", writes=[("pleg",)])

                def issue_p(t):
                    sc.dma("pool", lambda e, t=t, l=l, si=si: e.dma_start(out=pb[:, t % 2, :], in_=p_d[l, si, 128 * t:128 * t + 128, :]),
                           "p%d" % (t % 2), writes=[("pb", t % 2)])
                issue_p(0)
                issue_p(1)
                sc.dma("pool", lambda e, l=l: e.dma_start(out=wgatev, in_=wgate_d[l], max_dma_last_dim=4096), "wgate",
                       writes=[("wgate",)])
                sc.dma("pool", lambda e, l=l: e.dma_start(out=wplev, in_=wple_d[l], max_dma_last_dim=4096), "wple",
                       writes=[("wple",)])
                j2 = junk[:, :].rearrange("p (a b) -> p a b", a=2)
                pgv = postg.rearrange("p (a b) -> p a b", a=2)
                plv = pleg.rearrange("p (a b) -> p a b", a=2)
                def c1_A(t):
                    b0 = 2 * (t % 3)

                    def mmix(e, t=t, b0=b0):
                        ins = None
                        for n in range(2):
                            for kc in range(8):
                                lhs = mixT[:, kc, 128 * t:128 * t + 128] if kc < 4 else cvv[:, kc - 4, 128 * t:128 * t + 128]
                                ins = e.matmul(ps[:, b0 + n, :], lhs, woutv[:, kc, 512 * n:512 * n + 512],
                                               start=(kc == 0), stop=(kc == 7))
                        return ins
                    sc.op("pe", mmix, reads=[("wout",), ("w", 0), ("w", 1), ("mx", t)] + [("cv", c, t) for c in range(4)],
                          writes=[("ps", b0), ("ps", b0 + 1)])

                def c1_Bsq(t):
                    par = t % 2
                    b0 = 2 * (t % 3)
                    sc.op("act", lambda e, b0=b0, par=par: e.activation(out=j2, in_=ps[:, b0:b0 + 2, :], func=AF.Square,
                                                                      accum_out=ss[:, par:par + 1]),
                          writes=[("ps", b0), ("ps", b0 + 1), ("ss", par), ("junk",)])

                def c1_Btmp(t):
                    par = t % 2
                    b0 = 2 * (t % 3)
                    bA = bufA[:, par, :]
                    bAv = bA.rearrange("p (a b) -> p a b", a=2)
                    sc.op("dve", lambda e, bAv=bAv, b0=b0: e.tensor_tensor(out=bAv, in0=ps[:, b0:b0 + 2, :], in1=pgv, op=ALU.mult),
                          reads=[("postg",)], writes=[("ps", b0), ("ps", b0 + 1), ("bufA", par)])
                    sc.op("dve", lambda e, par=par: e.tensor_scalar(out=rs[:, par:par + 1], in0=ss[:, par:par + 1],
                                                                  scalar1=1.0 / D, scalar2=EPS, op0=ALU.mult, op1=ALU.add),
                          reads=[("ss", par)], writes=[("rs", par)])
                    sc.op("pool", lambda e, par=par: e.tensor_tensor(out=rstd[:, par:par + 1], in0=rs[:, par:par + 1],
                                                                   in1=neghalf[:, 0:1], op=ALU.pow),
                          reads=[("rs", par), ("neghalf",)], writes=[("rstd", par)])

                def c1_Bstt(t):
                    par = t % 2
                    bA = bufA[:, par, :]
                    sc.op("dve", lambda e, t=t, bA=bA, par=par: e.scalar_tensor_tensor(
                        out=x_sb[:, t, :], in0=bA, scalar=rstd[:, par:par + 1], in1=x_sb[:, t, :], op0=ALU.mult, op1=ALU.add),
                        reads=[("bufA", par), ("rstd", par)], writes=[("x", t)])

                def c1_C(t):
                    par = t % 2
                    sc.op("act", lambda e, t=t, par=par: e.copy(out=xb2[:, par, :], in_=x_sb[:, t, :]),
                          reads=[("x", t)], writes=[("xb", par)])

                def c1_Dtrx(t):
                    par = t % 2

                    def trx(e, par=par):
                        ins = None
                        for c in range(8):
                            ins = e.transpose(out=psbf[:, 6, 128 * c:128 * c + 128], in_=xb2[:, par, 128 * c:128 * c + 128],
                                              identity=ident[:, :])
                        return ins
                    sc.op("pe", trx, reads=[("xb", par), ("ident",)], writes=[("ps", 6)])

                def c1_Devac(t):
                    sc.op("dve", lambda e, t=t: e.tensor_copy(
                        out=mixT[:, :, 128 * t:128 * t + 128],
                        in_=psbf[:, 6, 0:512].rearrange("p (c t) -> p c t", c=4)),
                        writes=[("ps", 6), ("mx", t)])
                    sc.op("dve", lambda e, t=t: e.tensor_copy(
                        out=cvv[:, :, 128 * t:128 * t + 128],
                        in_=psbf[:, 6, 512:1024].rearrange("p (c t) -> p c t", c=4)),
                        writes=[("ps", 6)] + [("cv", c, t) for c in range(4)])

                c1_A(0)
                c1_A(1)
                c1_A(2)
                c1_Bsq(0)
                c1_Btmp(0)
                c1_Bstt(0)
                c1_Bsq(1)
                c1_Btmp(1)
                for k in range(16):
                    if k + 3 < 16:
                        c1_A(k + 3)
                    c1_C(k)
                    c1_Dtrx(k)
                    if k + 2 < 16:
                        c1_Bsq(k + 2)
                    c1_Devac(k)
                    if k + 1 < 16:
                        c1_Bstt(k + 1)
                    if k + 2 < 16:
                        c1_Btmp(k + 2)
                issue_w()
                issue_w()
                def c2_p(t):
                    par = t % 2

                    def trp(e, par=par):
                        ins = None
                        for kc in range(2):
                            ins = e.transpose(out=psbf[:, 6, 128 * kc:128 * kc + 128],
                                              in_=pb[:, par, 128 * kc:128 * kc + 128], identity=ident[:, :])
                        return ins
                    sc.op("pe", trp, reads=[("pb", par), ("ident",)], writes=[("ps", 6)])
                    sc.op("act", lambda e, par=par: e.copy(
                        out=pT2[:, par, :, :], in_=psbf[:, 6, 0:256].rearrange("p (c t) -> p c t", c=2)),
                        writes=[("ps", 6), ("pT", par)])
                    if t + 2 < 16:
                        issue_p(t + 2)

                c2_p(0)
                for t in range(16):
                    par = t % 2
                    g0 = 2 * par
                    e0 = 4
                    if t + 1 < 16:
                        c2_p(t + 1)

                    def mgate(e, t=t, g0=g0):
                        ins = None
                        for n in range(2):
                            for kc in range(8):
                                lhs = mixT[:, kc, 128 * t:128 * t + 128] if kc < 4 else cvv[:, kc - 4, 128 * t:128 * t + 128]
                                ins = e.matmul(ps[:, g0 + n, :], lhs, wgatev[:, kc, 512 * n:512 * n + 512],
                                               start=(kc == 0), stop=(kc == 7))
                        return ins
                    sc.op("pe", mgate, reads=[("wgate",), ("mx", t)] + [("cv", c, t) for c in range(4)],
                          writes=[("ps", g0), ("ps", g0 + 1)])

                    def mple(e, t=t, e0=e0):
                        ins = None
                        for n in range(2):
                            for kc in range(2):
                                ins = e.matmul(ps[:, e0 + n, :], pT2[:, t % 2, kc, :],
                                               wplev[:, kc, 512 * n:512 * n + 512], start=(kc == 0), stop=(kc == 1))
                        return ins
                    sc.op("pe", mple, reads=[("wple",), ("pT", par)], writes=[("ps", e0), ("ps", e0 + 1)])
                    bA = bufA[:, par, :]
                    bB = bufB[:, par, :]
                    bAv = bA.rearrange("p (a b) -> p a b", a=2)
                    bBv = bB.rearrange("p (a b) -> p a b", a=2)
                    sc.op("act", lambda e, e0=e0, par=par: e.activation(out=j2, in_=ps[:, e0:e0 + 2, :], func=AF.Square,
                                                                      accum_out=ss[:, 2 + par:3 + par]),
                          writes=[("ps", e0), ("ps", e0 + 1), ("ss", 2 + par), ("junk",)])
                    sc.op("act", lambda e, bAv=bAv, g0=g0: e.activation(out=bAv, in_=ps[:, g0:g0 + 2, :], func=AF.Sigmoid),
                          writes=[("ps", g0), ("ps", g0 + 1), ("bufA", par)])
                    sc.op("dve", lambda e, bBv=bBv, e0=e0: e.tensor_tensor(out=bBv, in0=ps[:, e0:e0 + 2, :], in1=plv, op=ALU.mult),
                          reads=[("pleg",)], writes=[("ps", e0), ("ps", e0 + 1), ("bufB", par)])
                    sc.op("dve", lambda e, par=par: e.tensor_scalar(out=rs[:, 2 + par:3 + par], in0=ss[:, 2 + par:3 + par],
                                                                  scalar1=1.0 / D, scalar2=EPS, op0=ALU.mult, op1=ALU.add),
                          reads=[("ss", 2 + par)], writes=[("rs", 2 + par)])
                    sc.op("pool", lambda e, par=par: e.tensor_tensor(out=rstd[:, 2 + par:3 + par], in0=rs[:, 2 + par:3 + par],
                                                                   in1=neghalf[:, 0:1], op=ALU.pow),
                          reads=[("rs", 2 + par), ("neghalf",)], writes=[("rstd", 2 + par)])
                    sc.op("dve", lambda e, bA=bA, bB=bB: e.tensor_tensor(out=bB, in0=bB, in1=bA, op=ALU.mult),
                          reads=[("bufA", par)], writes=[("bufB", par)])
                    sc.op("dve", lambda e, t=t, bB=bB, par=par: e.scalar_tensor_tensor(
                        out=x_sb[:, t, :], in0=bB, scalar=rstd[:, 2 + par:3 + par], in1=x_sb[:, t, :], op0=ALU.mult, op1=ALU.add),
                        reads=[("bufB", par), ("rstd", 2 + par)], writes=[("x", t)])
                    if last:
                        sc.dma("sp", lambda e, t=t, si=si: e.dma_start(out=out_d[si, 128 * t:128 * t + 128, :], in_=x_sb[:, t, :]),
                               "o%d" % t, reads=[("x", t)])
                sc.barrier()
                sc.op("pool", lambda e: e.memset(Vv[:, :, :, 128:130], 1.0), writes=[("V", t) for t in range(16)])

        except _Stop:
            pass
        sc.emit()
    return nc


_CACHE = {}


def _consts():
    ident = np.eye(128, dtype=np.float32).astype(ml_dtypes.bfloat16)
    rs = np.arange(128)[:, None].astype(np.float64)
    rt = np.arange(128)[None, :].astype(np.float64)
    dcorr = np.zeros((128, 5, 128), np.float32)
    bcol = np.zeros((128, 4, 17), np.float32)
    for h in range(4):
        m = SLOPES[h]
        db = np.where(rs > rt, -2.0 * m * (rs - rt), 0.0)
        masked = (rs // 64) > (rt // 64)
        db = np.where(masked, MASKV, db)
        dcorr[:, h, :] = db * 8.0
        for dl in range(-16, 1):
            bcol[:, h, 16 + dl] = m * (128.0 * dl + np.arange(128))
    dcorr[:, 4, :] = MASKV * 8.0
    return ident, dcorr.reshape(128, 640).astype(ml_dtypes.bfloat16), bcol.reshape(128, 68)


def prep_weights(inp):
    f = lambda a: np.ascontiguousarray(np.asarray(a, dtype=np.float32))
    w_in = f(inp["w_in"])
    parts = [w_in[:, :, 512 * i:512 * (i + 1)] for i in range(8)]
    cols = [parts[0], parts[1], parts[2], parts[3]]
    for c in range(4):
        cols.append(np.concatenate([parts[4][:, :, 128 * c:128 * c + 128], parts[5][:, :, 128 * c:128 * c + 128],
                                    parts[6][:, :, 128 * c:128 * c + 128], parts[7][:, :, 128 * c:128 * c + 128]], axis=2))
    wp = np.stack(cols, axis=1)
    wp = wp.reshape(NL, 8, 8, 128, 512).transpose(0, 1, 3, 2, 4)
    ident, dbase, bcol = _consts()
    sub = f(inp["subln_gain"])
    d = {
        "w_in": np.ascontiguousarray(wp),
        "w_out": np.ascontiguousarray(f(inp["w_out"]).reshape(NL, 8, 128, 1024).transpose(0, 2, 1, 3)),
        "w_gate": np.ascontiguousarray(f(inp["w_ple_gate"]).reshape(NL, 8, 128, 1024).transpose(0, 2, 1, 3)),
        "w_ple": np.ascontiguousarray(f(inp["w_ple_proj"]).reshape(NL, 2, 128, 1024).transpose(0, 2, 1, 3)),
        "pre_g": np.ascontiguousarray(f(inp["pre_norm_gain"]).reshape(NL, 8, 128).transpose(2, 0, 1).reshape(128, NL * 8)),
        "post_g": f(inp["post_norm_gain"]),
        "ple_g": f(inp["ple_norm_gain"]),
        "subln_g": np.ascontiguousarray(np.tile(sub, (1, 4))),
        "conv_w": np.ascontiguousarray(f(inp["conv_w"]).reshape(NL, 3, 4, 128).transpose(3, 0, 2, 1).reshape(128, NL * 12)),
        "lam": np.ascontiguousarray(np.concatenate([f(inp["lambda_q1"]), f(inp["lambda_k1"]),
                                                    f(inp["lambda_q2"]), f(inp["lambda_k2"])], axis=1)),
        "ident": ident, "dbase": dbase, "bcol": bcol,
    }
    return d


def kernel(**inp):
    x = np.asarray(inp["x"], dtype=np.float32)
    p = np.asarray(inp["p"], dtype=np.float32)
    wd = prep_weights(inp)
    nseq = x.shape[0] // NCORES
    if "nc" not in _CACHE:
        _CACHE["nc"] = build(nseq=nseq, layers=(0, 1))
    nc = _CACHE["nc"]
    in_maps = []
    for c in range(NCORES):
        m = dict(wd)
        m["x"] = np.ascontiguousarray(x[c * nseq:(c + 1) * nseq])
        m["p"] = np.ascontiguousarray(p[:, c * nseq:(c + 1) * nseq])
        in_maps.append(m)
    res = run_bass_kernel_spmd(nc, in_maps, core_ids=list(range(NCORES)))
    return np.concatenate([r["out"] for r in res.results], axis=0)
```

```python
import math
from contextlib import ExitStack

import numpy as np
import ml_dtypes

import concourse.bass as bass
import concourse.mybir as mybir
from concourse.bass_utils import run_bass_kernel_spmd

F32 = mybir.dt.float32
BF16 = mybir.dt.bfloat16
AF = mybir.ActivationFunctionType
ALU = mybir.AluOpType
AX = mybir.AxisListType

S = 2048
D = 1024
NT = S // 128
NL = 2
NCORES = 8
EPS = 1e-6
SUBLN_EPS = 1e-5
SLOPES = [2.0 ** (-8.0 * (h + 1) / 4) for h in range(4)]
MASKV = -30000.0


class Sched:
    ENG = ["pe", "act", "dve", "pool", "sp"]

    def __init__(self, nc):
        self.nc = nc
        self.q = {e: [] for e in self.ENG}
        self.cnt = {e: 0 for e in self.ENG}
        self.waited = {e: {} for e in self.ENG}
        self.lastw = {}
        self.readers = {}
        self.dmacnt = {}
        self.pending = {e: [] for e in self.ENG}

    def _deps(self, eng, reads, writes):
        toks = list(self.pending[eng])
        self.pending[eng] = []
        for k in reads:
            if k in self.lastw:
                toks.append(self.lastw[k])
        for k in writes:
            if k in self.lastw:
                toks.append(self.lastw[k])
            toks.extend(self.readers.get(k, ()))
        need = {}
        for (s, v) in toks:
            if eng == "pe" and s == ("e", "pe"):
                continue
            if v > need.get(s, 0):
                need[s] = v
        out = []
        for s, v in need.items():
            if self.waited[eng].get(s, 0) < v:
                self.waited[eng][s] = v
                out.append((s, v))
        return out

    def _commit(self, tok, reads, writes):
        for k in writes:
            self.lastw[k] = tok
            self.readers[k] = []
        for k in reads:
            if k in writes:
                continue
            self.readers.setdefault(k, []).append(tok)

    @staticmethod
    def _excl(reads, writes):
        r = [k for k in reads if k[0] != "ps"]
        w = list(writes) + [k for k in reads if k[0] == "ps"]
        return r, w

    def op(self, eng, fn, reads=(), writes=()):
        reads, writes = self._excl(reads, writes)
        waits = self._deps(eng, reads, writes)
        self.cnt[eng] += 1
        tok = (("e", eng), self.cnt[eng])
        self.q[eng].append((waits, fn, tok))
        self._commit(tok, reads, writes)

    def dma(self, eng, fn, sem, reads=(), writes=()):
        waits = self._deps(eng, reads, writes)
        self.dmacnt[sem] = self.dmacnt.get(sem, 0) + 16
        tok = (("d", sem), self.dmacnt[sem])
        self.q[eng].append((waits, fn, tok))
        self._commit(tok, reads, writes)

    def barrier(self):
        toks = [(("e", e), self.cnt[e]) for e in self.ENG if self.cnt[e] > 0]
        toks += [(("d", s), v) for s, v in self.dmacnt.items()]
        for e in self.ENG:
            self.pending[e].extend(toks)

    def emit(self):
        nc = self.nc
        sems = {}
        with ExitStack() as st:
            for e in self.ENG:
                sems[("e", e)] = st.enter_context(nc.semaphore("s_" + e))
            for i, s in enumerate(self.dmacnt):
                sems[("d", s)] = st.enter_context(nc.semaphore("d_%d" % i))
            block = st.enter_context(nc.Block())
            final = [(("e", e), self.cnt[e]) for e in self.ENG if self.cnt[e] > 0]
            final += [(("d", s), v) for s, v in self.dmacnt.items()]

            def mk(e):
                def body(engobj):
                    for waits, fn, tok in self.q[e]:
                        for s, v in waits:
                            engobj.wait_ge(sems[s], v)
                        inst = fn(engobj)
                        inst.then_inc(sems[tok[0]], 1 if tok[0][0] == "e" else 16)
                    if e == "sp":
                        for s, v in final:
                            engobj.wait_ge(sems[s], v)
                return body

            block.tensor(mk("pe"))
            block.scalar(mk("act"))
            block.vector(mk("dve"))
            block.gpsimd(mk("pool"))
            block.sync(mk("sp"))


class _Stop(Exception):
    pass


def build(nseq=2, layers=(0, 1), debug=False, upto=None, dbg_li=0):
    nc = bass.Bass("TRN2", target_bir_lowering=False)
    NLK = len(layers)

    def din(name, shape, dt=F32):
        return nc.dram_tensor(name, list(shape), dt, kind="ExternalInput").ap()

    x_d = din("x", [nseq, S, D])
    p_d = din("p", [NL, nseq, S, 256])
    win_d = din("w_in", [NL, 8, 128, 8, 512])
    wout_d = din("w_out", [NL, 128, 8, 1024])
    wgate_d = din("w_gate", [NL, 128, 8, 1024])
    wple_d = din("w_ple", [NL, 128, 2, 1024])
    preg_d = din("pre_g", [128, NL * 8])
    postg_d = din("post_g", [NL, 1024])
    pleg_d = din("ple_g", [NL, 1024])
    subln_d = din("subln_g", [NL, 512])
    convw_d = din("conv_w", [128, NL * 12])
    lam_d = din("lam", [NL, 256])
    ident_d = din("ident", [128, 128], BF16)
    dbase_d = din("dbase", [128, 640], BF16)
    bcol_d = din("bcol", [128, 68])
    out_d = nc.dram_tensor("out", [nseq, S, D], F32, kind="ExternalOutput").ap()
    dbg_d = {}
    if debug:
        for nm, shp, dt in [("dQ", [128, 4 * S], BF16), ("dK", [128, 4 * S], BF16),
                            ("dV", [128, 16 * 4 * 130], BF16), ("dza", [128, 16 * 512], BF16),
                            ("dcv", [128, 4 * S], BF16), ("dmx", [128, 4 * S], BF16),
                            ("dhT", [128, 8 * 1024], BF16)]:
            dbg_d[nm] = nc.dram_tensor(nm, shp, dt, kind="ExternalOutput").ap()

    sc = Sched(nc)
    st = ExitStack()

    def sb(name, shape, dt):
        return st.enter_context(nc.sbuf_tensor("sb_" + name, list(shape), dt))

    with st:
        x_sb = sb("x_sb", [128, NT, D], F32)
        hT_r = sb("hT_r", [128, 8 * 1024], BF16)
        wbuf = sb("wbuf", [128, 2, 8, 512], BF16)
        Q_r = sb("Q_r", [128, 4 * S], BF16)
        K_r = sb("K_r", [128, 4 * S], BF16)
        V_r = sb("V_r", [128, 16 * 4 * 130], BF16)
        za_r = sb("za_r", [128, 16 * 512], BF16)
        cv_r = sb("cv_r", [128, 4 * S], BF16)
        hn = sb("hn", [128, 2, D], BF16)
        tmp_r = sb("tmp_r", [128, 4480], F32)
        pb = sb("pb", [128, 2, 256], BF16)
        preg = sb("preg", [128, NL * 8], F32)
        convw = sb("convw", [128, NL * 12], F32)
        sg4 = sb("sg4", [128, 512], F32)
        lamv = sb("lamv", [128, 256], F32)
        ident = sb("ident", [128, 128], BF16)
        dbase = sb("dbase", [128, 640], BF16)
        bcol = sb("bcol", [128, 68], F32)
        small = sb("small", [128, 64], F32)
        lamtmp = sb("lamtmp", [128, 128], F32)
        agb_t = sb("agb_t", [128, 2, 512], BF16)
        ps = st.enter_context(nc.psum_tensor("ps", [128, 8, 512], F32))
        psbf = ps.bitcast(BF16)

        hT = hT_r[:, :].rearrange("p (c t) -> p c t", c=8)
        mixT = hT_r[:, :].rearrange("p (c t) -> p c t", c=4)
        Qv = Q_r[:, :].rearrange("p (h t) -> p h t", h=4)
        Kv = K_r[:, :].rearrange("p (h t) -> p h t", h=4)
        woutv = wbuf[:, :, :, :].rearrange("p s c n -> p (s c n)").rearrange("p (c n) -> p c n", c=8)
        wgatev = K_r[:, :].rearrange("p (c n) -> p c n", c=8)
        Vv = V_r[:, :].rearrange("p (t h e) -> p t h e", t=16, h=4)
        wplev = V_r[:, 0:2048].rearrange("p (c n) -> p c n", c=2)
        postg = V_r[:, 2048:4096].bitcast(F32)
        pleg = V_r[:, 4096:6144].bitcast(F32)
        zav = za_r[:, :].rearrange("p (t n) -> p t n", t=16)
        za_f = za_r[:, :].bitcast(F32)
        bufA = za_f[:, 0:2048].rearrange("p (a b) -> p a b", a=2)
        bufB = za_f[:, 2048:4096].rearrange("p (a b) -> p a b", a=2)
        xb2 = tmp_r[:, 0:1024].bitcast(BF16).rearrange("p (a b) -> p a b", a=2)
        pT2 = tmp_r[:, 1024:1280].bitcast(BF16).rearrange("p (a c t) -> p a c t", a=2, c=2)
        cvv = cv_r[:, :].rearrange("p (c t) -> p c t", c=4)
        cs = tmp_r[:, 0:512]
        gb = tmp_r[:, 512:1540].rearrange("p (a b) -> p a b", a=2)
        cva = tmp_r[:, 1540:2052]
        szb = tmp_r[:, 2052:2564]
        bzb = tmp_r[:, 2564:3076]
        junk = tmp_r[:, 3456:3968].bitcast(BF16)
        PTr = tmp_r[:, 0:768].bitcast(BF16).rearrange("p (s q) -> p s q", s=3)
        a1 = tmp_r[:, 768:896]
        araw_sets = [tmp_r[:, 896:1920].rearrange("p (a b) -> p a b", a=2),
                     tmp_r[:, 2432:3456].rearrange("p (a b) -> p a b", a=2)]
        sqb = tmp_r[:, 1920:2432]
        agb2 = agb_t[:, :, :]
        zf2 = tmp_r[:, 2432:3456].rearrange("p (a b) -> p a b", a=2)
        Qz = tmp_r[:, 3456:4480].bitcast(BF16).rearrange("p (z m q) -> p z m q", z=4, m=2)
        ss = small[:, 0:4]
        rs = small[:, 4:8]
        rstd = small[:, 8:12]
        neghalf = small[:, 12:16]
        lam_s = small[:, 16:20]
        rl = small[:, 24:26]
        rl2 = small[:, 26:27]
        ssh = small[:, 28:32]
        rsh = small[:, 32:36]
        rstdh = small[:, 36:40]
        crr = small[:, 40:48].rearrange("p (c k) -> p c k", c=4)

        def ld(eng, out_ap, in_ap, sem, writes):
            sc.dma(eng, lambda e: e.dma_start(out=out_ap, in_=in_ap), sem, writes=writes)

        ld("sp", ident[:, :], ident_d, "c0", [("ident",)])
        ld("sp", dbase[:, :], dbase_d, "c0", [("dbase",)])
        ld("sp", bcol[:, :], bcol_d, "c0", [("bcol",)])
        ld("sp", preg[:, :], preg_d, "c0", [("preg",)])
        ld("sp", convw[:, :], convw_d, "c0", [("convw",)])
        sc.barrier()
        sc.op("dve", lambda e: e.memset(neghalf, -0.5), writes=[("neghalf",)])
        sc.op("pool", lambda e: e.memset(V_r[:, :], 1.0), writes=[("V", t) for t in range(16)])

        bank_ctr = [0]

        def nextbank():
            b = bank_ctr[0] % 8
            bank_ctr[0] += 1
            return b

        alt = [0]

        def alt_eng():
            alt[0] += 1
            return "act" if alt[0] % 2 else "dve"

        wq = []
        for si in range(nseq):
            for l in layers:
                for hf in range(2):
                    for j in range(8):
                        wq.append((l, j))
        wissued = [0]

        def issue_w():
            n = wissued[0]
            if n >= len(wq):
                return
            l, j = wq[n]
            slot = n % 2
            sc.dma("pool", lambda e: e.dma_start(out=wbuf[:, slot, :, :], in_=win_d[l, j],
                                                  max_dma_last_dim=4096),
                   "w%d" % slot, writes=[("w", slot)])
            wissued[0] += 1

        issue_w()
        wused = [0]

        for li0, l0 in enumerate(layers):
            linit = 0.8 - 0.6 * math.exp(-0.3 * l0)
            sc.dma("sp", lambda e, l0=l0: e.dma_start(out=lamv[:, :].unsqueeze(1),
                                                  in_=lam_d[l0:l0 + 1, :].partition_broadcast(128)),
                   "lamv", writes=[("lamv",)])
            lv = lamv[:, :].rearrange("p (a b d) -> p a b d", a=2, b=2)
            prv = lamtmp[:, :].rearrange("p (a d) -> p a d", a=2)
            sc.op("dve", lambda e, lv=lv, prv=prv: e.tensor_tensor(out=prv, in0=lv[:, :, 0, :], in1=lv[:, :, 1, :],
                                                               op=ALU.mult),
                  reads=[("lamv",)], writes=[("lamtmp",)])
            sc.op("dve", lambda e, prv=prv: e.tensor_reduce(out=lam_s[:, 0:2], in_=prv, axis=AX.X, op=ALU.add),
                  reads=[("lamtmp",)], writes=[("lam_s",)])
            sc.op("act", lambda e: e.activation(out=lam_s[:, 2:4], in_=lam_s[:, 0:2], func=AF.Exp),
                  reads=[("lam_s",)], writes=[("lam_e",)])
            sc.op("dve", lambda e, linit=linit, li0=li0: e.tensor_scalar(
                out=small[:, 20 + li0:21 + li0], in0=lam_s[:, 3:4], scalar1=lam_s[:, 2:3],
                scalar2=-linit, op0=ALU.subtract, op1=ALU.add),
                reads=[("lam_e",)], writes=[("neglam", li0)])
        try:
          if upto == "consts":
              raise _Stop()
          for si in range(nseq):
            for li, l in enumerate(layers):
                lambda_init = 0.8 - 0.6 * math.exp(-0.3 * l)
                first = (li == 0)
                last = (li == NLK - 1)
                neglam = small[:, 20 + li:21 + li]
                sc.dma("sp", lambda e, l=l: e.dma_start(out=sg4[:, :].unsqueeze(1),
                                                   in_=subln_d[l:l + 1, :].partition_broadcast(128)),
                       "sg4", writes=[("sg4",)])
                sc.op("dve", lambda e, lambda_init=lambda_init: e.tensor_scalar(out=sg4[:, :], in0=sg4[:, :],
                                                       scalar1=(1.0 - lambda_init), scalar2=None, op0=ALU.mult),
                      reads=[("sg4",)], writes=[("sg4",)])

                if upto == "lam":
                    raise _Stop()
                for hf in range(2):
                    def n_stat(t):
                        sl = t % 4
                        if first:
                            sc.dma("sp", lambda e, t=t, si=si: e.dma_start(out=x_sb[:, t, :],
                                                                    in_=x_d[si, 128 * t:128 * t + 128, :]),
                                   "x%d" % t, writes=[("x", t)])
                        sc.op("act", lambda e, t=t, sl=sl: e.activation(out=junk[:, :], in_=x_sb[:, t, :],
                                                                      func=AF.Square,
                                                                      accum_out=ss[:, sl:sl + 1]),
                              reads=[("x", t)], writes=[("ss", sl), ("junk",)])
                        sc.op("dve", lambda e, sl=sl: e.tensor_scalar(out=rs[:, sl:sl + 1], in0=ss[:, sl:sl + 1],
                                                                    scalar1=1.0 / D, scalar2=EPS,
                                                                    op0=ALU.mult, op1=ALU.add),
                              reads=[("ss", sl)], writes=[("rs", sl)])
                        sc.op("pool", lambda e, sl=sl: e.tensor_tensor(out=rstd[:, sl:sl + 1], in0=rs[:, sl:sl + 1],
                                                                     in1=neghalf[:, 0:1], op=ALU.pow),
                              reads=[("rs", sl), ("neghalf",)], writes=[("rstd", sl)])

                    def n_norm(t):
                        sl = t % 4
                        hb = t % 2
                        tt = t % 4
                        par = (t // 4) % 2
                        sc.op("dve", lambda e, t=t, sl=sl, hb=hb: e.tensor_scalar(
                            out=hn[:, hb, :], in0=x_sb[:, t, :], scalar1=rstd[:, sl:sl + 1], scalar2=None,
                            op0=ALU.mult),
                            reads=[("x", t), ("rstd", sl)], writes=[("hn", hb)])

                        def tr(e, tt=tt, hb=hb, par=par):
                            ins = None
                            for c in range(8):
                                bk = 4 * par + c // 2
                                off = (c % 2) * 512 + 128 * tt
                                ins = e.transpose(out=psbf[:, bk, off:off + 128],
                                                  in_=hn[:, hb, 128 * c:128 * c + 128], identity=ident[:, :])
                            return ins
                        sc.op("pe", tr, reads=[("hn", hb), ("ident",)],
                              writes=[("ps", 4 * par + b) for b in range(4)])

                    def n_evac(G):
                        g = G % 2
                        par = G % 2
                        for c in range(8):
                            bk = 4 * par + c // 2
                            off = (c % 2) * 512
                            eng = "act" if (c // 2) % 2 == 0 else "dve"
                            gcol = preg[:, l * 8 + c:l * 8 + c + 1]
                            dst = hT[:, c, 512 * g:512 * g + 512]
                            src = psbf[:, bk, off:off + 512]
                            if eng == "act":
                                sc.op("act", lambda e, dst=dst, src=src, gcol=gcol: e.mul(out=dst, in_=src, mul=gcol),
                                      reads=[("ps", bk), ("preg",)], writes=[("hT", g, c)])
                            else:
                                sc.op("dve", lambda e, dst=dst, src=src, gcol=gcol: e.tensor_scalar(
                                    out=dst, in0=src, scalar1=gcol, scalar2=None, op0=ALU.mult),
                                    reads=[("ps", bk), ("preg",)], writes=[("hT", g, c)])

                    tbase = 8 * hf
                    n_stat(tbase)
                    n_stat(tbase + 1)
                    for k in range(8):
                        if k + 2 < 8:
                            n_stat(tbase + k + 2)
                        n_norm(tbase + k)
                        if k % 4 == 3:
                            n_evac(2 * hf + k // 4)
                    if upto == "N":
                        raise _Stop()

                    if debug and si == 0 and li == dbg_li and hf == 0:
                        sc.dma("sp", lambda e: e.dma_start(out=dbg_d["dhT"], in_=hT_r[:, :]), "dbg",
                               reads=[("hT", g, c) for g in range(2) for c in range(8)])

                    for j in range(8):
                        slot = wused[0] % 2
                        if not (hf == 1 and j == 7) and wissued[0] <= wused[0] + 1:
                            issue_w()
                        wused[0] += 1
                        wk = ("w", slot)
                        if j < 2:
                            dstv = Qv if j == 0 else Kv
                            dk = "Q" if j == 0 else "K"
                            for m in range(4):
                                for g in range(2):
                                    G = 2 * hf + g
                                    bk = nextbank()

                                    def mm(e, m=m, g=g, bk=bk, slot=slot):
                                        ins = None
                                        for kc in range(8):
                                            ins = e.matmul(ps[:, bk, :], wbuf[:, slot, kc, 128 * m:128 * m + 128],
                                                           hT[:, kc, 512 * g:512 * g + 512],
                                                           start=(kc == 0), stop=(kc == 7))
                                        return ins
                                    sc.op("pe", mm, reads=[wk] + [("hT", g, c) for c in range(8)],
                                          writes=[("ps", bk)])
                                    dst = dstv[:, m, 512 * G:512 * G + 512]
                                    eng = alt_eng()
                                    if eng == "act":
                                        sc.op("act", lambda e, dst=dst, bk=bk: e.copy(out=dst, in_=ps[:, bk, :]),
                                              reads=[("ps", bk)], writes=[(dk, m, G)])
                                    else:
                                        sc.op("dve", lambda e, dst=dst, bk=bk: e.tensor_copy(out=dst, in_=ps[:, bk, :]),
                                              reads=[("ps", bk)], writes=[(dk, m, G)])
                        elif j < 4:
                            for tt in range(8):
                                t = 8 * hf + tt
                                g = tt // 4
                                bk = nextbank()

                                def mm(e, tt=tt, bk=bk, slot=slot):
                                    ins = None
                                    for kc in range(8):
                                        ins = e.matmul(ps[:, bk, :], hT[:, kc, 128 * tt:128 * tt + 128],
                                                       wbuf[:, slot, kc, :], start=(kc == 0), stop=(kc == 7))
                                    return ins
                                sc.op("pe", mm, reads=[wk] + [("hT", g, c) for c in range(8)], writes=[("ps", bk)])
                                if j == 2:
                                    eng = alt_eng()
                                    dst = Vv[:, t, :, 0:128]
                                    src = ps[:, bk, :].rearrange("p (h e) -> p h e", h=4)
                                    if eng == "act":
                                        sc.op("act", lambda e, dst=dst, src=src: e.copy(out=dst, in_=src),
                                              reads=[("ps", bk)], writes=[("V", t)])
                                    else:
                                        sc.op("dve", lambda e, dst=dst, src=src: e.tensor_copy(out=dst, in_=src),
                                              reads=[("ps", bk)], writes=[("V", t)])
                                else:
                                    sc.op("act", lambda e, bk=bk: e.activation(out=szb, in_=ps[:, bk, :], func=AF.Silu),
                                          reads=[("ps", bk)], writes=[("szb",)])
                                    sc.op("dve", lambda e, t=t: e.tensor_tensor(out=zav[:, t, :], in0=szb, in1=sg4[:, :],
                                                                              op=ALU.mult),
                                          reads=[("szb",), ("sg4",)], writes=[("za", t)])
                        else:
                            c = j - 4
                            for g in range(2):
                                G = 2 * hf + g
                                gp = G % 2
                                bks = [nextbank() for _ in range(4)]
                                for part in range(4):
                                    bk = bks[part]

                                    def mm(e, part=part, g=g, bk=bk, slot=slot):
                                        ins = None
                                        for kc in range(8):
                                            ins = e.matmul(ps[:, bk, :],
                                                           wbuf[:, slot, kc, 128 * part:128 * part + 128],
                                                           hT[:, kc, 512 * g:512 * g + 512],
                                                           start=(kc == 0), stop=(kc == 7))
                                        return ins
                                    sc.op("pe", mm, reads=[wk] + [("hT", g, cc) for cc in range(8)],
                                          writes=[("ps", bk)])
                                bb, bc, bu, bz = bks
                                if G == 0:
                                    sc.op("pool", lambda e, gp=gp: e.memset(gb[:, gp, 0:2], 0.0), writes=[("gbc", gp)])
                                else:
                                    sc.op("pool", lambda e, gp=gp, c=c: e.tensor_copy(out=gb[:, gp, 0:2], in_=crr[:, c, :]),
                                          reads=[("crr", c)], writes=[("gbc", gp)])
                                sc.op("act", lambda e, bc=bc: e.copy(out=cs, in_=ps[:, bc, :]),
                                      reads=[("ps", bc)], writes=[("cs",)])
                                sc.op("dve", lambda e, bu=bu, gp=gp: e.tensor_tensor(out=gb[:, gp, 2:514], in0=ps[:, bu, :],
                                                                                   in1=cs, op=ALU.mult),
                                      reads=[("ps", bu), ("cs",)], writes=[("gb", gp)])
                                w0 = convw[:, l * 12 + c * 3 + 0:l * 12 + c * 3 + 1]
                                w1 = convw[:, l * 12 + c * 3 + 1:l * 12 + c * 3 + 2]
                                w2 = convw[:, l * 12 + c * 3 + 2:l * 12 + c * 3 + 3]
                                sc.op("dve", lambda e, gp=gp, w0=w0: e.tensor_scalar(out=cva, in0=gb[:, gp, 0:512], scalar1=w0,
                                                                                   scalar2=None, op0=ALU.mult),
                                      reads=[("gb", gp), ("gbc", gp), ("convw",)], writes=[("cva",)])
                                sc.op("dve", lambda e, gp=gp, w1=w1: e.scalar_tensor_tensor(
                                    out=cva, in0=gb[:, gp, 1:513], scalar=w1, in1=cva, op0=ALU.mult, op1=ALU.add),
                                    reads=[("gb", gp), ("gbc", gp), ("cva",)], writes=[("cva",)])
                                sc.op("dve", lambda e, gp=gp, w2=w2: e.scalar_tensor_tensor(
                                    out=cva, in0=gb[:, gp, 2:514], scalar=w2, in1=cva, op0=ALU.mult, op1=ALU.add),
                                    reads=[("gb", gp), ("cva",)], writes=[("cva",)])
                                sc.op("pool", lambda e, gp=gp, c=c: e.tensor_copy(out=crr[:, c, :], in_=gb[:, gp, 512:514]),
                                      reads=[("gb", gp)], writes=[("crr", c)])
                                sc.op("act", lambda e, bz=bz: e.activation(out=szb, in_=ps[:, bz, :], func=AF.Silu),
                                      reads=[("ps", bz)], writes=[("szb",)])
                                sc.op("dve", lambda e, bb=bb: e.tensor_tensor(out=bzb, in0=ps[:, bb, :], in1=szb, op=ALU.mult),
                                      reads=[("ps", bb), ("szb",)], writes=[("bzb",)])
                                sc.op("dve", lambda e, c=c, G=G: e.tensor_tensor(out=cvv[:, c, 512 * G:512 * G + 512],
                                                                                 in0=cva, in1=bzb, op=ALU.mult),
                                      reads=[("cva",), ("bzb",)], writes=[("cv", c, 4 * G + k) for k in range(4)])

                sc.barrier()
                if upto == "A":
                    raise _Stop()
                sc.dma("pool", lambda e, l=l: e.dma_start(out=woutv, in_=wout_d[l], max_dma_last_dim=4096), "wout",
                       writes=[("wout",), ("w", 0), ("w", 1)])
                steps = [(I, h, j) for I in range(8) for h in range(4) for j in range(2 * I + 2)]
                LOOK = 2
                nsteps = len(steps)
                sc.op("pool", lambda e: e.memset(Qz[:, :, :, :], 0.0), writes=[("Qz", z) for z in range(4)])

                def emit_qz(I, h):
                    zb = (4 * I + h) % 4
                    for m in range(2):
                        if I < 4:
                            sc.op("act", lambda e, zb=zb, m=m, I=I, h=h: e.copy(
                                out=Qz[64 * m:64 * m + 64, zb, m, :], in_=Qv[64 * m:64 * m + 64, h, 256 * I:256 * I + 256]),
                                reads=[("Q", h, I // 2)], writes=[("Qz", zb)])
                        else:
                            sc.op("pool", lambda e, zb=zb, m=m, I=I, h=h: e.tensor_copy(
                                out=Qz[64 * m:64 * m + 64, zb, m, :], in_=Qv[64 * m:64 * m + 64, h, 256 * I:256 * I + 256]),
                                reads=[("Q", h, I // 2)], writes=[("Qz", zb)])

                def emit_qk(n):
                    I, h, j = steps[n]
                    i0 = 2 * I
                    sbi = n % 3
                    zb = (4 * I + h) % 4
                    if j == 0:
                        nh = 4 * I + h + 1
                        if nh < 32:
                            emit_qz(nh // 4, nh % 4)

                    def qk(e, I=I, h=h, j=j, sbi=sbi, zb=zb, i0=i0):
                        ins = None
                        for m in range(2):
                            ins = e.matmul(ps[:, sbi, 256 * m:256 * m + 256],
                                           Kv[:, h, 128 * j:128 * j + 128], Qz[:, zb, m, :],
                                           start=(m == 0), stop=True, skip_group_check=True)
                        dc = dbase[:, 128 * h:128 * h + 128]
                        if j == i0:
                            for m in range(2):
                                ins = e.matmul(ps[:, sbi, 256 * m:256 * m + 128], ident[:, :], dc,
                                               start=False, stop=True, skip_group_check=True)
                        if j == i0 + 1:
                            for m in range(2):
                                ins = e.matmul(ps[:, sbi, 256 * m + 128:256 * m + 256], ident[:, :], dc,
                                               start=False, stop=True, skip_group_check=True)
                                ins = e.matmul(ps[:, sbi, 256 * m:256 * m + 128], ident[:, :], dbase[:, 512:640],
                                               start=False, stop=True, skip_group_check=True)
                        return ins
                    sc.op("pe", qk, reads=[("K", h, j // 4), ("Qz", zb), ("ident",), ("dbase",)],
                          writes=[("ps", sbi)])

                gcount = 0
                hits = {}

                def chk(nm):
                    hits[nm] = hits.get(nm, 0) + 1
                    if upto == nm or upto == "%s#%d" % (nm, hits[nm]):
                        raise _Stop()
                emit_qz(0, 0)
                for n in range(min(LOOK, nsteps)):
                    emit_qk(n)
                deferred = []
                for n in range(nsteps):
                    I, h, j = steps[n]
                    i0 = 2 * I
                    sbi = n % 3
                    obase = 3 + 2 * (gcount % 2)
                    while deferred and deferred[0][0] <= n:
                        deferred.pop(0)[1]()
                    bidx = h * 17 + 16 + (j - i0 - 1)
                    bias = bcol[:, bidx:bidx + 1]
                    sc.op("act", lambda e, sbi=sbi, bias=bias: e.activation(
                        out=PTr[:, sbi, :], in_=ps[:, sbi, :], func=AF.Exp, bias=bias, scale=0.125),
                        reads=[("bcol",)], writes=[("ps", sbi), ("PT", sbi)])
                    if n + LOOK < nsteps:
                        emit_qk(n + LOOK)

                    def pv(e, sbi=sbi, j=j, h=h, i0=i0, obase=obase):
                        ins = None
                        for a in range(2):
                            if a == 0 and j == i0 + 1:
                                continue
                            for m in range(2):
                                ins = e.matmul(ps[:, obase + a, 130 * m:130 * m + 130],
                                               PTr[:, sbi, 256 * m + 128 * a:256 * m + 128 * a + 128], Vv[:, j, h, :],
                                               start=(j == 0 and m == 0), stop=True, skip_group_check=True)
                        return ins
                    sc.op("pe", pv, reads=[("PT", sbi), ("V", j)], writes=[("ps", obase), ("ps", obase + 1)])
                    chk("B3")
                    if j == i0 + 1:
                        gcount += 1
                        for a in range(2):
                            ob = obase + a
                            o3 = ps[:, ob, 0:260].rearrange("p (a b) -> p a b", a=2)
                            sc.op("dve", lambda e, o3=o3: e.reciprocal(out=rl, in_=o3[:, :, 128]),
                                  writes=[("ps", ob), ("rl",)])
                            sc.op("dve", lambda e, neglam=neglam: e.tensor_scalar(out=rl2, in0=rl[:, 1:2], scalar1=neglam,
                                                                                  scalar2=None, op0=ALU.mult),
                                  reads=[("rl",), ("neglam", li)], writes=[("rl2",)])
                            sc.op("dve", lambda e, ob=ob: e.tensor_scalar(out=a1, in0=ps[:, ob, 0:128], scalar1=rl[:, 0:1],
                                                                         scalar2=None, op0=ALU.mult),
                                  reads=[("rl",)], writes=[("ps", ob), ("a1",)])
                            sc.op("dve", lambda e, ob=ob, a=a, h=h, I=I: e.scalar_tensor_tensor(
                                out=araw_sets[I % 2][:, a, 128 * h:128 * h + 128], in0=ps[:, ob, 130:258], scalar=rl2, in1=a1,
                                op0=ALU.mult, op1=ALU.add),
                                reads=[("rl2",), ("a1",)], writes=[("ps", ob), ("araw", I % 2, a, h)])
                        chk("B4d")
                        if h == 3:
                            for a in range(2):
                                i = i0 + a

                                def chain(a=a, i=i, I=I):
                                    ar = araw_sets[I % 2][:, a, :]
                                    ar3 = ar.rearrange("p (h d) -> p h d", h=4)
                                    sq3 = sqb.rearrange("p (h d) -> p h d", h=4)
                                    sc.op("dve", lambda e, ar=ar: e.tensor_tensor(out=sqb, in0=ar, in1=ar, op=ALU.mult),
                                          reads=[("araw", I % 2, a, hh) for hh in range(4)], writes=[("sqb",)])
                                    sc.op("dve", lambda e, sq3=sq3: e.tensor_reduce(out=ssh, in_=sq3, axis=AX.X, op=ALU.add),
                                          reads=[("sqb",)], writes=[("ssh",)])
                                    sc.op("dve", lambda e: e.tensor_scalar(out=rsh, in0=ssh, scalar1=1.0 / 128, scalar2=SUBLN_EPS,
                                                                           op0=ALU.mult, op1=ALU.add),
                                          reads=[("ssh",)], writes=[("rsh",)])
                                    sc.op("pool", lambda e: e.tensor_tensor(out=rstdh, in0=rsh, in1=neghalf, op=ALU.pow),
                                          reads=[("rsh",), ("neghalf",)], writes=[("rstdh",)])
                                    for hh in range(4):
                                        sc.op("dve", lambda e, ar3=ar3, hh=hh, a=a, i=i: e.scalar_tensor_tensor(
                                            out=agb2[:, a, 128 * hh:128 * hh + 128], in0=ar3[:, hh, :], scalar=rstdh[:, hh:hh + 1],
                                            in1=zav[:, i, 128 * hh:128 * hh + 128], op0=ALU.mult, op1=ALU.mult),
                                            reads=[("araw", I % 2, a, hh), ("rstdh",), ("za", i)], writes=[("agb", a)])

                                def tr(e, a=a):
                                    ins = None
                                    for c in range(4):
                                        ins = e.transpose(out=psbf[:, 7, 128 * c:128 * c + 128],
                                                          in_=agb2[:, a, 128 * c:128 * c + 128], identity=ident[:, :])
                                    return ins

                                def fin(tr=tr, i=i, a=a):
                                    sc.op("pe", tr, reads=[("agb", a), ("ident",)], writes=[("ps", 7)])
                                    sc.op("dve", lambda e, i=i: e.tensor_copy(
                                        out=mixT[:, :, 128 * i:128 * i + 128],
                                        in_=psbf[:, 7, 0:512].rearrange("p (c t) -> p c t", c=4)),
                                        writes=[("ps", 7), ("mx", i)])
                                nk = 2 * I + 4
                                cdue = n + 1 + (a + 1) * nk
                                deferred.append((cdue, chain))
                                deferred.append((cdue + min(20, 8 * I + 6), fin))
                                deferred.sort(key=lambda x: x[0])
                            chk("B6")

                while deferred:
                    deferred.pop(0)[1]()
                if debug and si == 0 and li == dbg_li:
                    sc.barrier()
                    for nm, r in [("dQ", Q_r), ("dK", K_r), ("dV", V_r), ("dza", za_r), ("dcv", cv_r), ("dmx", hT_r)]:
                        sc.dma("sp", lambda e, nm=nm, r=r: e.dma_start(out=dbg_d[nm], in_=r[:, :]), "dbg")

                if upto == "B":
                    raise _Stop()
                sc.barrier()
                sc.dma("sp", lambda e, l=l: e.dma_start(out=postg.unsqueeze(1),
                                                   in_=postg_d[l:l + 1, :].partition_broadcast(128)),
                       "postg", writes=[("postg",)])
                sc.dma("sp", lambda e, l=l: e.dma_start(out=pleg.unsqueeze(1),
                                                   in_=pleg_d[l:l + 1, :].partition_broadcast(128)),
                       "pleg", writes=[("pleg",)])

                def issue_p(t):
                    sc.dma("pool", lambda e, t=t, l=l, si=si: e.dma_start(out=pb[:, t % 2, :], in_=p_d[l, si, 128 * t:128 * t + 128, :]),
                           "p%d" % (t % 2), writes=[("pb", t % 2)])
                issue_p(0)
                issue_p(1)
                sc.dma("pool", lambda e, l=l: e.dma_start(out=wgatev, in_=wgate_d[l], max_dma_last_dim=4096), "wgate",
                       writes=[("wgate",)])
                sc.dma("pool", lambda e, l=l: e.dma_start(out=wplev, in_=wple_d[l], max_dma_last_dim=4096), "wple",
                       writes=[("wple",)])
                j2 = junk[:, :].rearrange("p (a b) -> p a b", a=2)
                pgv = postg.rearrange("p (a b) -> p a b", a=2)
                plv = pleg.rearrange("p (a b) -> p a b", a=2)
                def c1_A(t):
                    b0 = 2 * (t % 3)

                    def mmix(e, t=t, b0=b0):
                        ins = None
                        for n in range(2):
                            for kc in range(8):
                                lhs = mixT[:, kc, 128 * t:128 * t + 128] if kc < 4 else cvv[:, kc - 4, 128 * t:128 * t + 128]
                                ins = e.matmul(ps[:, b0 + n, :], lhs, woutv[:, kc, 512 * n:512 * n + 512],
                                               start=(kc == 0), stop=(kc == 7))
                        return ins
                    sc.op("pe", mmix, reads=[("wout",), ("w", 0), ("w", 1), ("mx", t)] + [("cv", c, t) for c in range(4)],
                          writes=[("ps", b0), ("ps", b0 + 1)])

                def c1_Bsq(t):
                    par = t % 2
                    b0 = 2 * (t % 3)
                    sc.op("act", lambda e, b0=b0, par=par: e.activation(out=j2, in_=ps[:, b0:b0 + 2, :], func=AF.Square,
                                                                      accum_out=ss[:, par:par + 1]),
                          writes=[("ps", b0), ("ps", b0 + 1), ("ss", par), ("junk",)])

                def c1_Btmp(t):
                    par = t % 2
                    b0 = 2 * (t % 3)
                    bA = bufA[:, par, :]
                    bAv = bA.rearrange("p (a b) -> p a b", a=2)
                    sc.op("dve", lambda e, bAv=bAv, b0=b0: e.tensor_tensor(out=bAv, in0=ps[:, b0:b0 + 2, :], in1=pgv, op=ALU.mult),
                          reads=[("postg",)], writes=[("ps", b0), ("ps", b0 + 1), ("bufA", par)])
                    sc.op("dve", lambda e, par=par: e.tensor_scalar(out=rs[:, par:par + 1], in0=ss[:, par:par + 1],
                                                                  scalar1=1.0 / D, scalar2=EPS, op0=ALU.mult, op1=ALU.add),
                          reads=[("ss", par)], writes=[("rs", par)])
                    sc.op("pool", lambda e, par=par: e.tensor_tensor(out=rstd[:, par:par + 1], in0=rs[:, par:par + 1],
                                                                   in1=neghalf[:, 0:1], op=ALU.pow),
                          reads=[("rs", par), ("neghalf",)], writes=[("rstd", par)])

                def c1_Bstt(t):
                    par = t % 2
                    bA = bufA[:, par, :]
                    sc.op("dve", lambda e, t=t, bA=bA, par=par: e.scalar_tensor_tensor(
                        out=x_sb[:, t, :], in0=bA, scalar=rstd[:, par:par + 1], in1=x_sb[:, t, :], op0=ALU.mult, op1=ALU.add),
                        reads=[("bufA", par), ("rstd", par)], writes=[("x", t)])

                def c1_C(t):
                    par = t % 2
                    sc.op("act", lambda e, t=t, par=par: e.copy(out=xb2[:, par, :], in_=x_sb[:, t, :]),
                          reads=[("x", t)], writes=[("xb", par)])

                def c1_Dtrx(t):
                    par = t % 2

                    def trx(e, par=par):
                        ins = None
                        for c in range(8):
                            ins = e.transpose(out=psbf[:, 6, 128 * c:128 * c + 128], in_=xb2[:, par, 128 * c:128 * c + 128],
                                              identity=ident[:, :])
                        return ins
                    sc.op("pe", trx, reads=[("xb", par), ("ident",)], writes=[("ps", 6)])

                def c1_Devac(t):
                    sc.op("dve", lambda e, t=t: e.tensor_copy(
                        out=mixT[:, :, 128 * t:128 * t + 128],
                        in_=psbf[:, 6, 0:512].rearrange("p (c t) -> p c t", c=4)),
                        writes=[("ps", 6), ("mx", t)])
                    sc.op("dve", lambda e, t=t: e.tensor_copy(
                        out=cvv[:, :, 128 * t:128 * t + 128],
                        in_=psbf[:, 6, 512:1024].rearrange("p (c t) -> p c t", c=4)),
                        writes=[("ps", 6)] + [("cv", c, t) for c in range(4)])

                c1_A(0)
                c1_A(1)
                c1_A(2)
                c1_Bsq(0)
                c1_Btmp(0)
                c1_Bstt(0)
                c1_Bsq(1)
                c1_Btmp(1)
                for k in range(16):
                    if k + 3 < 16:
                        c1_A(k + 3)
                    c1_C(k)
                    c1_Dtrx(k)
                    if k + 2 < 16:
                        c1_Bsq(k + 2)
                    c1_Devac(k)
                    if k + 1 < 16:
                        c1_Bstt(k + 1)
                    if k + 2 < 16:
                        c1_Btmp(k + 2)
                issue_w()
                issue_w()
                def c2_p(t):
                    par = t % 2

                    def trp(e, par=par):
                        ins = None
                        for kc in range(2):
                            ins = e.transpose(out=psbf[:, 6, 128 * kc:128 * kc + 128],
                                              in_=pb[:, par, 128 * kc:128 * kc + 128], identity=ident[:, :])
                        return ins
                    sc.op("pe", trp, reads=[("pb", par), ("ident",)], writes=[("ps", 6)])
                    sc.op("act", lambda e, par=par: e.copy(
                        out=pT2[:, par, :, :], in_=psbf[:, 6, 0:256].rearrange("p (c t) -> p c t", c=2)),
                        writes=[("ps", 6), ("pT", par)])
                    if t + 2 < 16:
                        issue_p(t + 2)

                c2_p(0)
                for t in range(16):
                    par = t % 2
                    g0 = 2 * par
                    e0 = 4
                    if t + 1 < 16:
                        c2_p(t + 1)

                    def mgate(e, t=t, g0=g0):
                        ins = None
                        for n in range(2):
                            for kc in range(8):
                                lhs = mixT[:, kc, 128 * t:128 * t + 128] if kc < 4 else cvv[:, kc - 4, 128 * t:128 * t + 128]
                                ins = e.matmul(ps[:, g0 + n, :], lhs, wgatev[:, kc, 512 * n:512 * n + 512],
                                               start=(kc == 0), stop=(kc == 7))
                        return ins
                    sc.op("pe", mgate, reads=[("wgate",), ("mx", t)] + [("cv", c, t) for c in range(4)],
                          writes=[("ps", g0), ("ps", g0 + 1)])

                    def mple(e, t=t, e0=e0):
                        ins = None
                        for n in range(2):
                            for kc in range(2):
                                ins = e.matmul(ps[:, e0 + n, :], pT2[:, t % 2, kc, :],
                                               wplev[:, kc, 512 * n:512 * n + 512], start=(kc == 0), stop=(kc == 1))
                        return ins
                    sc.op("pe", mple, reads=[("wple",), ("pT", par)], writes=[("ps", e0), ("ps", e0 + 1)])
                    bA = bufA[:, par, :]
                    bB = bufB[:, par, :]
                    bAv = bA.rearrange("p (a b) -> p a b", a=2)
                    bBv = bB.rearrange("p (a b) -> p a b", a=2)
                    sc.op("act", lambda e, e0=e0, par=par: e.activation(out=j2, in_=ps[:, e0:e0 + 2, :], func=AF.Square,
                                                                      accum_out=ss[:, 2 + par:3 + par]),
                          writes=[("ps", e0), ("ps", e0 + 1), ("ss", 2 + par), ("junk",)])
                    sc.op("act", lambda e, bAv=bAv, g0=g0: e.activation(out=bAv, in_=ps[:, g0:g0 + 2, :], func=AF.Sigmoid),
                          writes=[("ps", g0), ("ps", g0 + 1), ("bufA", par)])
                    sc.op("dve", lambda e, bBv=bBv, e0=e0: e.tensor_tensor(out=bBv, in0=ps[:, e0:e0 + 2, :], in1=plv, op=ALU.mult),
                          reads=[("pleg",)], writes=[("ps", e0), ("ps", e0 + 1), ("bufB", par)])
                    sc.op("dve", lambda e, par=par: e.tensor_scalar(out=rs[:, 2 + par:3 + par], in0=ss[:, 2 + par:3 + par],
                                                                  scalar1=1.0 / D, scalar2=EPS, op0=ALU.mult, op1=ALU.add),
                          reads=[("ss", 2 + par)], writes=[("rs", 2 + par)])
                    sc.op("pool", lambda e, par=par: e.tensor_tensor(out=rstd[:, 2 + par:3 + par], in0=rs[:, 2 + par:3 + par],
                                                                   in1=neghalf[:, 0:1], op=ALU.pow),
                          reads=[("rs", 2 + par), ("neghalf",)], writes=[("rstd", 2 + par)])
                    sc.op("dve", lambda e, bA=bA, bB=bB: e.tensor_tensor(out=bB, in0=bB, in1=bA, op=ALU.mult),
                          reads=[("bufA", par)], writes=[("bufB", par)])
                    sc.op("dve", lambda e, t=t, bB=bB, par=par: e.scalar_tensor_tensor(
                        out=x_sb[:, t, :], in0=bB, scalar=rstd[:, 2 + par:3 + par], in1=x_sb[:, t, :], op0=ALU.mult, op1=ALU.add),
                        reads=[("bufB", par), ("rstd", 2 + par)], writes=[("x", t)])
                    if last:
                        sc.dma("sp", lambda e, t=t, si=si: e.dma_start(out=out_d[si, 128 * t:128 * t + 128, :], in_=x_sb[:, t, :]),
                               "o%d" % t, reads=[("x", t)])
                sc.barrier()
                sc.op("pool", lambda e: e.memset(Vv[:, :, :, 128:130], 1.0), writes=[("V", t) for t in range(16)])

        except _Stop:
            pass
        sc.emit()
    return nc


_CACHE = {}


def _consts():
    ident = np.eye(128, dtype=np.float32).astype(ml_dtypes.bfloat16)
    rs = np.arange(128)[:, None].astype(np.float64)
    rt = np.arange(128)[None, :].astype(np.float64)
    dcorr = np.zeros((128, 5, 128), np.float32)
    bcol = np.zeros((128, 4, 17), np.float32)
    for h in range(4):
        m = SLOPES[h]
        db = np.where(rs > rt, -2.0 * m * (rs - rt), 0.0)
        masked = (rs // 64) > (rt // 64)
        db = np.where(masked, MASKV, db)
        dcorr[:, h, :] = db * 8.0
        for dl in range(-16, 1):
            bcol[:, h, 16 + dl] = m * (128.0 * dl + np.arange(128))
    dcorr[:, 4, :] = MASKV * 8.0
    return ident, dcorr.reshape(128, 640).astype(ml_dtypes.bfloat16), bcol.reshape(128, 68)


def prep_weights(inp):
    f = lambda a: np.ascontiguousarray(np.asarray(a, dtype=np.float32))
    w_in = f(inp["w_in"])
    parts = [w_in[:, :, 512 * i:512 * (i + 1)] for i in range(8)]
    cols = [parts[0], parts[1], parts[2], parts[3]]
    for c in range(4):
        cols.append(np.concatenate([parts[4][:, :, 128 * c:128 * c + 128], parts[5][:, :, 128 * c:128 * c + 128],
                                    parts[6][:, :, 128 * c:128 * c + 128], parts[7][:, :, 128 * c:128 * c + 128]], axis=2))
    wp = np.stack(cols, axis=1)
    wp = wp.reshape(NL, 8, 8, 128, 512).transpose(0, 1, 3, 2, 4)
    ident, dbase, bcol = _consts()
    sub = f(inp["subln_gain"])
    d = {
        "w_in": np.ascontiguousarray(wp),
        "w_out": np.ascontiguousarray(f(inp["w_out"]).reshape(NL, 8, 128, 1024).transpose(0, 2, 1, 3)),
        "w_gate": np.ascontiguousarray(f(inp["w_ple_gate"]).reshape(NL, 8, 128, 1024).transpose(0, 2, 1, 3)),
        "w_ple": np.ascontiguousarray(f(inp["w_ple_proj"]).reshape(NL, 2, 128, 1024).transpose(0, 2, 1, 3)),
        "pre_g": np.ascontiguousarray(f(inp["pre_norm_gain"]).reshape(NL, 8, 128).transpose(2, 0, 1).reshape(128, NL * 8)),
        "post_g": f(inp["post_norm_gain"]),
        "ple_g": f(inp["ple_norm_gain"]),
        "subln_g": np.ascontiguousarray(np.tile(sub, (1, 4))),
        "conv_w": np.ascontiguousarray(f(inp["conv_w"]).reshape(NL, 3, 4, 128).transpose(3, 0, 2, 1).reshape(128, NL * 12)),
        "lam": np.ascontiguousarray(np.concatenate([f(inp["lambda_q1"]), f(inp["lambda_k1"]),
                                                    f(inp["lambda_q2"]), f(inp["lambda_k2"])], axis=1)),
        "ident": ident, "dbase": dbase, "bcol": bcol,
    }
    return d


def kernel(**inp):
    x = np.asarray(inp["x"], dtype=np.float32)
    p = np.asarray(inp["p"], dtype=np.float32)
    wd = prep_weights(inp)
    nseq = x.shape[0] // NCORES
    if "nc" not in _CACHE:
        _CACHE["nc"] = build(nseq=nseq, layers=(0, 1))
    nc = _CACHE["nc"]
    in_maps = []
    for c in range(NCORES):
        m = dict(wd)
        m["x"] = np.ascontiguousarray(x[c * nseq:(c + 1) * nseq])
        m["p"] = np.ascontiguousarray(p[:, c * nseq:(c + 1) * nseq])
        in_maps.append(m)
    res = run_bass_kernel_spmd(nc, in_maps, core_ids=list(range(NCORES)))
    return np.concatenate([r["out"] for r in res.results], axis=0)
```

```python
import math
from contextlib import ExitStack

import numpy as np
import ml_dtypes

import concourse.bass as bass
import concourse.mybir as mybir
from concourse.bass_utils import run_bass_kernel_spmd

F32 = mybir.dt.float32
BF16 = mybir.dt.bfloat16
AF = mybir.ActivationFunctionType
ALU = mybir.AluOpType
AX = mybir.AxisListType

S = 2048
D = 1024
NT = S // 128
NL = 2
NCORES = 8
EPS = 1e-6
SUBLN_EPS = 1e-5
SLOPES = [2.0 ** (-8.0 * (h + 1) / 4) for h in range(4)]
MASKV = -30000.0


class Sched:
    ENG = ["pe", "act", "dve", "pool", "sp"]

    def __init__(self, nc):
        self.nc = nc
        self.q = {e: [] for e in self.ENG}
        self.cnt = {e: 0 for e in self.ENG}
        self.waited = {e: {} for e in self.ENG}
        self.lastw = {}
        self.readers = {}
        self.dmacnt = {}
        self.pending = {e: [] for e in self.ENG}

    def _deps(self, eng, reads, writes):
        toks = list(self.pending[eng])
        self.pending[eng] = []
        for k in reads:
            if k in self.lastw:
                toks.append(self.lastw[k])
        for k in writes:
            if k in self.lastw:
                toks.append(self.lastw[k])
            toks.extend(self.readers.get(k, ()))
        need = {}
        for (s, v) in toks:
            if eng == "pe" and s == ("e", "pe"):
                continue
            if v > need.get(s, 0):
                need[s] = v
        out = []
        for s, v in need.items():
            if self.waited[eng].get(s, 0) < v:
                self.waited[eng][s] = v
                out.append((s, v))
        return out

    def _commit(self, tok, reads, writes):
        for k in writes:
            self.lastw[k] = tok
            self.readers[k] = []
        for k in reads:
            if k in writes:
                continue
            self.readers.setdefault(k, []).append(tok)

    @staticmethod
    def _excl(reads, writes):
        r = [k for k in reads if k[0] != "ps"]
        w = list(writes) + [k for k in reads if k[0] == "ps"]
        return r, w

    def op(self, eng, fn, reads=(), writes=()):
        reads, writes = self._excl(reads, writes)
        waits = self._deps(eng, reads, writes)
        self.cnt[eng] += 1
        tok = (("e", eng), self.cnt[eng])
        self.q[eng].append((waits, fn, tok))
        self._commit(tok, reads, writes)

    def dma(self, eng, fn, sem, reads=(), writes=()):
        waits = self._deps(eng, reads, writes)
        self.dmacnt[sem] = self.dmacnt.get(sem, 0) + 16
        tok = (("d", sem), self.dmacnt[sem])
        self.q[eng].append((waits, fn, tok))
        self._commit(tok, reads, writes)

    def barrier(self):
        toks = [(("e", e), self.cnt[e]) for e in self.ENG if self.cnt[e] > 0]
        toks += [(("d", s), v) for s, v in self.dmacnt.items()]
        for e in self.ENG:
            self.pending[e].extend(toks)

    def emit(self):
        nc = self.nc
        sems = {}
        with ExitStack() as st:
            for e in self.ENG:
                sems[("e", e)] = st.enter_context(nc.semaphore("s_" + e))
            for i, s in enumerate(self.dmacnt):
                sems[("d", s)] = st.enter_context(nc.semaphore("d_%d" % i))
            block = st.enter_context(nc.Block())
            final = [(("e", e), self.cnt[e]) for e in self.ENG if self.cnt[e] > 0]
            final += [(("d", s), v) for s, v in self.dmacnt.items()]

            def mk(e):
                def body(engobj):
                    for waits, fn, tok in self.q[e]:
                        for s, v in waits:
                            engobj.wait_ge(sems[s], v)
                        inst = fn(engobj)
                        inst.then_inc(sems[tok[0]], 1 if tok[0][0] == "e" else 16)
                    if e == "sp":
                        for s, v in final:
                            engobj.wait_ge(sems[s], v)
                return body

            block.tensor(mk("pe"))
            block.scalar(mk("act"))
            block.vector(mk("dve"))
            block.gpsimd(mk("pool"))
            block.sync(mk("sp"))


class _Stop(Exception):
    pass


def build(nseq=2, layers=(0, 1), debug=False, upto=None, dbg_li=0):
    nc = bass.Bass("TRN2", target_bir_lowering=False)
    NLK = len(layers)

    def din(name, shape, dt=F32):
        return nc.dram_tensor(name, list(shape), dt, kind="ExternalInput").ap()

    x_d = din("x", [nseq, S, D])
    p_d = din("p", [NL, nseq, S, 256])
    win_d = din("w_in", [NL, 8, 128, 8, 512])
    wout_d = din("w_out", [NL, 128, 8, 1024])
    wgate_d = din("w_gate", [NL, 128, 8, 1024])
    wple_d = din("w_ple", [NL, 128, 2, 1024])
    preg_d = din("pre_g", [128, NL * 8])
    postg_d = din("post_g", [NL, 1024])
    pleg_d = din("ple_g", [NL, 1024])
    subln_d = din("subln_g", [NL, 512])
    convw_d = din("conv_w", [128, NL * 12])
    lam_d = din("lam", [NL, 256])
    ident_d = din("ident", [128, 128], BF16)
    dbase_d = din("dbase", [128, 640], BF16)
    bcol_d = din("bcol", [128, 68])
    out_d = nc.dram_tensor("out", [nseq, S, D], F32, kind="ExternalOutput").ap()
    dbg_d = {}
    if debug:
        for nm, shp, dt in [("dQ", [128, 4 * S], BF16), ("dK", [128, 4 * S], BF16),
                            ("dV", [128, 16 * 4 * 130], BF16), ("dza", [128, 16 * 512], BF16),
                            ("dcv", [128, 4 * S], BF16), ("dmx", [128, 4 * S], BF16),
                            ("dhT", [128, 8 * 1024], BF16)]:
            dbg_d[nm] = nc.dram_tensor(nm, shp, dt, kind="ExternalOutput").ap()

    sc = Sched(nc)
    st = ExitStack()

    def sb(name, shape, dt):
        return st.enter_context(nc.sbuf_tensor("sb_" + name, list(shape), dt))

    with st:
        x_sb = sb("x_sb", [128, NT, D], F32)
        hT_r = sb("hT_r", [128, 8 * 1024], BF16)
        wbuf = sb("wbuf", [128, 2, 8, 512], BF16)
        Q_r = sb("Q_r", [128, 4 * S], BF16)
        K_r = sb("K_r", [128, 4 * S], BF16)
        V_r = sb("V_r", [128, 16 * 4 * 130], BF16)
        za_r = sb("za_r", [128, 16 * 512], BF16)
        cv_r = sb("cv_r", [128, 4 * S], BF16)
        hn = sb("hn", [128, 2, D], BF16)
        tmp_r = sb("tmp_r", [128, 4480], F32)
        pb = sb("pb", [128, 2, 256], BF16)
        preg = sb("preg", [128, NL * 8], F32)
        convw = sb("convw", [128, NL * 12], F32)
        sg4 = sb("sg4", [128, 512], F32)
        lamv = sb("lamv", [128, 256], F32)
        ident = sb("ident", [128, 128], BF16)
        dbase = sb("dbase", [128, 640], BF16)
        bcol = sb("bcol", [128, 68], F32)
        small = sb("small", [128, 64], F32)
        lamtmp = sb("lamtmp", [128, 128], F32)
        agb_t = sb("agb_t", [128, 2, 512], BF16)
        ps = st.enter_context(nc.psum_tensor("ps", [128, 8, 512], F32))
        psbf = ps.bitcast(BF16)

        hT = hT_r[:, :].rearrange("p (c t) -> p c t", c=8)
        mixT = hT_r[:, :].rearrange("p (c t) -> p c t", c=4)
        Qv = Q_r[:, :].rearrange("p (h t) -> p h t", h=4)
        Kv = K_r[:, :].rearrange("p (h t) -> p h t", h=4)
        woutv = wbuf[:, :, :, :].rearrange("p s c n -> p (s c n)").rearrange("p (c n) -> p c n", c=8)
        wgatev = K_r[:, :].rearrange("p (c n) -> p c n", c=8)
        Vv = V_r[:, :].rearrange("p (t h e) -> p t h e", t=16, h=4)
        wplev = V_r[:, 0:2048].rearrange("p (c n) -> p c n", c=2)
        postg = V_r[:, 2048:4096].bitcast(F32)
        pleg = V_r[:, 4096:6144].bitcast(F32)
        zav = za_r[:, :].rearrange("p (t n) -> p t n", t=16)
        za_f = za_r[:, :].bitcast(F32)
        bufA = za_f[:, 0:2048].rearrange("p (a b) -> p a b", a=2)
        bufB = za_f[:, 2048:4096].rearrange("p (a b) -> p a b", a=2)
        xb2 = tmp_r[:, 0:1024].bitcast(BF16).rearrange("p (a b) -> p a b", a=2)
        pT2 = tmp_r[:, 1024:1280].bitcast(BF16).rearrange("p (a c t) -> p a c t", a=2, c=2)
        cvv = cv_r[:, :].rearrange("p (c t) -> p c t", c=4)
        cs = tmp_r[:, 0:512]
        gb = tmp_r[:, 512:1540].rearrange("p (a b) -> p a b", a=2)
        cva = tmp_r[:, 1540:2052]
        szb = tmp_r[:, 2052:2564]
        bzb = tmp_r[:, 2564:3076]
        junk = tmp_r[:, 3456:3968].bitcast(BF16)
        PTr = tmp_r[:, 0:768].bitcast(BF16).rearrange("p (s q) -> p s q", s=3)
        a1 = tmp_r[:, 768:896]
        araw_sets = [tmp_r[:, 896:1920].rearrange("p (a b) -> p a b", a=2),
                     tmp_r[:, 2432:3456].rearrange("p (a b) -> p a b", a=2)]
        sqb = tmp_r[:, 1920:2432]
        agb2 = agb_t[:, :, :]
        zf2 = tmp_r[:, 2432:3456].rearrange("p (a b) -> p a b", a=2)
        Qz = tmp_r[:, 3456:4480].bitcast(BF16).rearrange("p (z m q) -> p z m q", z=4, m=2)
        ss = small[:, 0:4]
        rs = small[:, 4:8]
        rstd = small[:, 8:12]
        neghalf = small[:, 12:16]
        lam_s = small[:, 16:20]
        rl = small[:, 24:26]
        rl2 = small[:, 26:27]
        ssh = small[:, 28:32]
        rsh = small[:, 32:36]
        rstdh = small[:, 36:40]
        crr = small[:, 40:48].rearrange("p (c k) -> p c k", c=4)

        def ld(eng, out_ap, in_ap, sem, writes):
            sc.dma(eng, lambda e: e.dma_start(out=out_ap, in_=in_ap), sem, writes=writes)

        ld("sp", ident[:, :], ident_d, "c0", [("ident",)])
        ld("sp", dbase[:, :], dbase_d, "c0", [("dbase",)])
        ld("sp", bcol[:, :], bcol_d, "c0", [("bcol",)])
        ld("sp", preg[:, :], preg_d, "c0", [("preg",)])
        ld("sp", convw[:, :], convw_d, "c0", [("convw",)])
        sc.barrier()
        sc.op("dve", lambda e: e.memset(neghalf, -0.5), writes=[("neghalf",)])
        sc.op("pool", lambda e: e.memset(V_r[:, :], 1.0), writes=[("V", t) for t in range(16)])

        bank_ctr = [0]

        def nextbank():
            b = bank_ctr[0] % 8
            bank_ctr[0] += 1
            return b

        alt = [0]

        def alt_eng():
            alt[0] += 1
            return "act" if alt[0] % 2 else "dve"

        wq = []
        for si in range(nseq):
            for l in layers:
                for hf in range(2):
                    for j in range(8):
                        wq.append((l, j))
        wissued = [0]

        def issue_w():
            n = wissued[0]
            if n >= len(wq):
                return
            l, j = wq[n]
            slot = n % 2
            sc.dma("pool", lambda e: e.dma_start(out=wbuf[:, slot, :, :], in_=win_d[l, j],
                                                  max_dma_last_dim=4096),
                   "w%d" % slot, writes=[("w", slot)])
            wissued[0] += 1

        issue_w()
        wused = [0]

        for li0, l0 in enumerate(layers):
            linit = 0.8 - 0.6 * math.exp(-0.3 * l0)
            sc.dma("sp", lambda e, l0=l0: e.dma_start(out=lamv[:, :].unsqueeze(1),
                                                  in_=lam_d[l0:l0 + 1, :].partition_broadcast(128)),
                   "lamv", writes=[("lamv",)])
            lv = lamv[:, :].rearrange("p (a b d) -> p a b d", a=2, b=2)
            prv = lamtmp[:, :].rearrange("p (a d) -> p a d", a=2)
            sc.op("dve", lambda e, lv=lv, prv=prv: e.tensor_tensor(out=prv, in0=lv[:, :, 0, :], in1=lv[:, :, 1, :],
                                                               op=ALU.mult),
                  reads=[("lamv",)], writes=[("lamtmp",)])
            sc.op("dve", lambda e, prv=prv: e.tensor_reduce(out=lam_s[:, 0:2], in_=prv, axis=AX.X, op=ALU.add),
                  reads=[("lamtmp",)], writes=[("lam_s",)])
            sc.op("act", lambda e: e.activation(out=lam_s[:, 2:4], in_=lam_s[:, 0:2], func=AF.Exp),
                  reads=[("lam_s",)], writes=[("lam_e",)])
            sc.op("dve", lambda e, linit=linit, li0=li0: e.tensor_scalar(
                out=small[:, 20 + li0:21 + li0], in0=lam_s[:, 3:4], scalar1=lam_s[:, 2:3],
                scalar2=-linit, op0=ALU.subtract, op1=ALU.add),
                reads=[("lam_e",)], writes=[("neglam", li0)])
        try:
          if upto == "consts":
              raise _Stop()
          for si in range(nseq):
            for li, l in enumerate(layers):
                lambda_init = 0.8 - 0.6 * math.exp(-0.3 * l)
                first = (li == 0)
                last = (li == NLK - 1)
                neglam = small[:, 20 + li:21 + li]
                sc.dma("sp", lambda e, l=l: e.dma_start(out=sg4[:, :].unsqueeze(1),
                                                   in_=subln_d[l:l + 1, :].partition_broadcast(128)),
                       "sg4", writes=[("sg4",)])
                sc.op("dve", lambda e, lambda_init=lambda_init: e.tensor_scalar(out=sg4[:, :], in0=sg4[:, :],
                                                       scalar1=(1.0 - lambda_init), scalar2=None, op0=ALU.mult),
                      reads=[("sg4",)], writes=[("sg4",)])

                if upto == "lam":
                    raise _Stop()
                for hf in range(2):
                    def n_stat(t):
                        sl = t % 4
                        if first:
                            sc.dma("sp", lambda e, t=t, si=si: e.dma_start(out=x_sb[:, t, :],
                                                                    in_=x_d[si, 128 * t:128 * t + 128, :]),
                                   "x%d" % t, writes=[("x", t)])
                        sc.op("act", lambda e, t=t, sl=sl: e.activation(out=junk[:, :], in_=x_sb[:, t, :],
                                                                      func=AF.Square,
                                                                      accum_out=ss[:, sl:sl + 1]),
                              reads=[("x", t)], writes=[("ss", sl), ("junk",)])
                        sc.op("dve", lambda e, sl=sl: e.tensor_scalar(out=rs[:, sl:sl + 1], in0=ss[:, sl:sl + 1],
                                                                    scalar1=1.0 / D, scalar2=EPS,
                                                                    op0=ALU.mult, op1=ALU.add),
                              reads=[("ss", sl)], writes=[("rs", sl)])
                        sc.op("pool", lambda e, sl=sl: e.tensor_tensor(out=rstd[:, sl:sl + 1], in0=rs[:, sl:sl + 1],
                                                                     in1=neghalf[:, 0:1], op=ALU.pow),
                              reads=[("rs", sl), ("neghalf",)], writes=[("rstd", sl)])

                    def n_norm(t):
                        sl = t % 4
                        hb = t % 2
                        tt = t % 4
                        par = (t // 4) % 2
                        sc.op("dve", lambda e, t=t, sl=sl, hb=hb: e.tensor_scalar(
                            out=hn[:, hb, :], in0=x_sb[:, t, :], scalar1=rstd[:, sl:sl + 1], scalar2=None,
                            op0=ALU.mult),
                            reads=[("x", t), ("rstd", sl)], writes=[("hn", hb)])

                        def tr(e, tt=tt, hb=hb, par=par):
                            ins = None
                            for c in range(8):
                                bk = 4 * par + c // 2
                                off = (c % 2) * 512 + 128 * tt
                                ins = e.transpose(out=psbf[:, bk, off:off + 128],
                                                  in_=hn[:, hb, 128 * c:128 * c + 128], identity=ident[:, :])
                            return ins
                        sc.op("pe", tr, reads=[("hn", hb), ("ident",)],
                              writes=[("ps", 4 * par + b) for b in range(4)])

                    def n_evac(G):
                        g = G % 2
                        par = G % 2
                        for c in range(8):
                            bk = 4 * par + c // 2
                            off = (c % 2) * 512
                            eng = "act" if (c // 2) % 2 == 0 else "dve"
                            gcol = preg[:, l * 8 + c:l * 8 + c + 1]
                            dst = hT[:, c, 512 * g:512 * g + 512]
                            src = psbf[:, bk, off:off + 512]
                            if eng == "act":
                                sc.op("act", lambda e, dst=dst, src=src, gcol=gcol: e.mul(out=dst, in_=src, mul=gcol),
                                      reads=[("ps", bk), ("preg",)], writes=[("hT", g, c)])
                            else:
                                sc.op("dve", lambda e, dst=dst, src=src, gcol=gcol: e.tensor_scalar(
                                    out=dst, in0=src, scalar1=gcol, scalar2=None, op0=ALU.mult),
                                    reads=[("ps", bk), ("preg",)], writes=[("hT", g, c)])

                    tbase = 8 * hf
                    n_stat(tbase)
                    n_stat(tbase + 1)
                    for k in range(8):
                        if k + 2 < 8:
                            n_stat(tbase + k + 2)
                        n_norm(tbase + k)
                        if k % 4 == 3:
                            n_evac(2 * hf + k // 4)
                    if upto == "N":
                        raise _Stop()

                    if debug and si == 0 and li == dbg_li and hf == 0:
                        sc.dma("sp", lambda e: e.dma_start(out=dbg_d["dhT"], in_=hT_r[:, :]), "dbg",
                               reads=[("hT", g, c) for g in range(2) for c in range(8)])

                    for j in range(8):
                        slot = wused[0] % 2
                        if not (hf == 1 and j == 7) and wissued[0] <= wused[0] + 1:
                            issue_w()
                        wused[0] += 1
                        wk = ("w", slot)
                        if j < 2:
                            dstv = Qv if j == 0 else Kv
                            dk = "Q" if j == 0 else "K"
                            for m in range(4):
                                for g in range(2):
                                    G = 2 * hf + g
                                    bk = nextbank()

                                    def mm(e, m=m, g=g, bk=bk, slot=slot):
                                        ins = None
                                        for kc in range(8):
                                            ins = e.matmul(ps[:, bk, :], wbuf[:, slot, kc, 128 * m:128 * m + 128],
                                                           hT[:, kc, 512 * g:512 * g + 512],
                                                           start=(kc == 0), stop=(kc == 7))
                                        return ins
                                    sc.op("pe", mm, reads=[wk] + [("hT", g, c) for c in range(8)],
                                          writes=[("ps", bk)])
                                    dst = dstv[:, m, 512 * G:512 * G + 512]
                                    eng = alt_eng()
                                    if eng == "act":
                                        sc.op("act", lambda e, dst=dst, bk=bk: e.copy(out=dst, in_=ps[:, bk, :]),
                                              reads=[("ps", bk)], writes=[(dk, m, G)])
                                    else:
                                        sc.op("dve", lambda e, dst=dst, bk=bk: e.tensor_copy(out=dst, in_=ps[:, bk, :]),
                                              reads=[("ps", bk)], writes=[(dk, m, G)])
                        elif j < 4:
                            for tt in range(8):
                                t = 8 * hf + tt
                                g = tt // 4
                                bk = nextbank()

                                def mm(e, tt=tt, bk=bk, slot=slot):
                                    ins = None
                                    for kc in range(8):
                                        ins = e.matmul(ps[:, bk, :], hT[:, kc, 128 * tt:128 * tt + 128],
                                                       wbuf[:, slot, kc, :], start=(kc == 0), stop=(kc == 7))
                                    return ins
                                sc.op("pe", mm, reads=[wk] + [("hT", g, c) for c in range(8)], writes=[("ps", bk)])
                                if j == 2:
                                    eng = alt_eng()
                                    dst = Vv[:, t, :, 0:128]
                                    src = ps[:, bk, :].rearrange("p (h e) -> p h e", h=4)
                                    if eng == "act":
                                        sc.op("act", lambda e, dst=dst, src=src: e.copy(out=dst, in_=src),
                                              reads=[("ps", bk)], writes=[("V", t)])
                                    else:
                                        sc.op("dve", lambda e, dst=dst, src=src: e.tensor_copy(out=dst, in_=src),
                                              reads=[("ps", bk)], writes=[("V", t)])
                                else:
                                    sc.op("act", lambda e, bk=bk: e.activation(out=szb, in_=ps[:, bk, :], func=AF.Silu),
                                          reads=[("ps", bk)], writes=[("szb",)])
                                    sc.op("dve", lambda e, t=t: e.tensor_tensor(out=zav[:, t, :], in0=szb, in1=sg4[:, :],
                                                                              op=ALU.mult),
                                          reads=[("szb",), ("sg4",)], writes=[("za", t)])
                        else:
                            c = j - 4
                            for g in range(2):
                                G = 2 * hf + g
                                gp = G % 2
                                bks = [nextbank() for _ in range(4)]
                                for part in range(4):
                                    bk = bks[part]

                                    def mm(e, part=part, g=g, bk=bk, slot=slot):
                                        ins = None
                                        for kc in range(8):
                                            ins = e.matmul(ps[:, bk, :],
                                                           wbuf[:, slot, kc, 128 * part:128 * part + 128],
                                                           hT[:, kc, 512 * g:512 * g + 512],
                                                           start=(kc == 0), stop=(kc == 7))
                                        return ins
                                    sc.op("pe", mm, reads=[wk] + [("hT", g, cc) for cc in range(8)],
                                          writes=[("ps", bk)])
                                bb, bc, bu, bz = bks
                                if G == 0:
                                    sc.op("pool", lambda e, gp=gp: e.memset(gb[:, gp, 0:2], 0.0), writes=[("gbc", gp)])
                                else:
                                    sc.op("pool", lambda e, gp=gp, c=c: e.tensor_copy(out=gb[:, gp, 0:2], in_=crr[:, c, :]),
                                          reads=[("crr", c)], writes=[("gbc", gp)])
                                sc.op("act", lambda e, bc=bc: e.copy(out=cs, in_=ps[:, bc, :]),
                                      reads=[("ps", bc)], writes=[("cs",)])
                                sc.op("dve", lambda e, bu=bu, gp=gp: e.tensor_tensor(out=gb[:, gp, 2:514], in0=ps[:, bu, :],
                                                                                   in1=cs, op=ALU.mult),
                                      reads=[("ps", bu), ("cs",)], writes=[("gb", gp)])
                                w0 = convw[:, l * 12 + c * 3 + 0:l * 12 + c * 3 + 1]
                                w1 = convw[:, l * 12 + c * 3 + 1:l * 12 + c * 3 + 2]
                                w2 = convw[:, l * 12 + c * 3 + 2:l * 12 + c * 3 + 3]
                                sc.op("dve", lambda e, gp=gp, w0=w0: e.tensor_scalar(out=cva, in0=gb[:, gp, 0:512], scalar1=w0,
                                                                                   scalar2=None, op0=ALU.mult),
                                      reads=[("gb", gp), ("gbc", gp), ("convw",)], writes=[("cva",)])
                                sc.op("dve", lambda e, gp=gp, w1=w1: e.scalar_tensor_tensor(
                                    out=cva, in0=gb[:, gp, 1:513], scalar=w1, in1=cva, op0=ALU.mult, op1=ALU.add),
                                    reads=[("gb", gp), ("gbc", gp), ("cva",)], writes=[("cva",)])
                                sc.op("dve", lambda e, gp=gp, w2=w2: e.scalar_tensor_tensor(
                                    out=cva, in0=gb[:, gp, 2:514], scalar=w2, in1=cva, op0=ALU.mult, op1=ALU.add),
                                    reads=[("gb", gp), ("cva",)], writes=[("cva",)])
                                sc.op("pool", lambda e, gp=gp, c=c: e.tensor_copy(out=crr[:, c, :], in_=gb[:, gp, 512:514]),
                                      reads=[("gb", gp)], writes=[("crr", c)])
                                sc.op("act", lambda e, bz=bz: e.activation(out=szb, in_=ps[:, bz, :], func=AF.Silu),
                                      reads=[("ps", bz)], writes=[("szb",)])
                                sc.op("dve", lambda e, bb=bb: e.tensor_tensor(out=bzb, in0=ps[:, bb, :], in1=szb, op=ALU.mult),
                                      reads=[("ps", bb), ("szb",)], writes=[("bzb",)])
                                sc.op("dve", lambda e, c=c, G=G: e.tensor_tensor(out=cvv[:, c, 512 * G:512 * G + 512],
                                                                                 in0=cva, in1=bzb, op=ALU.mult),
                                      reads=[("cva",), ("bzb",)], writes=[("cv", c, 4 * G + k) for k in range(4)])

                sc.barrier()
                if upto == "A":
                    raise _Stop()
                sc.dma("pool", lambda e, l=l: e.dma_start(out=woutv, in_=wout_d[l], max_dma_last_dim=4096), "wout",
                       writes=[("wout",), ("w", 0), ("w", 1)])
                steps = [(I, h, j) for I in range(8) for h in range(4) for j in range(2 * I + 2)]
                LOOK = 2
                nsteps = len(steps)
                sc.op("pool", lambda e: e.memset(Qz[:, :, :, :], 0.0), writes=[("Qz", z) for z in range(4)])

                def emit_qz(I, h):
                    zb = (4 * I + h) % 4
                    for m in range(2):
                        if I < 4:
                            sc.op("act", lambda e, zb=zb, m=m, I=I, h=h: e.copy(
                                out=Qz[64 * m:64 * m + 64, zb, m, :], in_=Qv[64 * m:64 * m + 64, h, 256 * I:256 * I + 256]),
                                reads=[("Q", h, I // 2)], writes=[("Qz", zb)])
                        else:
                            sc.op("pool", lambda e, zb=zb, m=m, I=I, h=h: e.tensor_copy(
                                out=Qz[64 * m:64 * m + 64, zb, m, :], in_=Qv[64 * m:64 * m + 64, h, 256 * I:256 * I + 256]),
                                reads=[("Q", h, I // 2)], writes=[("Qz", zb)])

                def emit_qk(n):
                    I, h, j = steps[n]
                    i0 = 2 * I
                    sbi = n % 3
                    zb = (4 * I + h) % 4
                    if j == 0:
                        nh = 4 * I + h + 1
                        if nh < 32:
                            emit_qz(nh // 4, nh % 4)

                    def qk(e, I=I, h=h, j=j, sbi=sbi, zb=zb, i0=i0):
                        ins = None
                        for m in range(2):
                            ins = e.matmul(ps[:, sbi, 256 * m:256 * m + 256],
                                           Kv[:, h, 128 * j:128 * j + 128], Qz[:, zb, m, :],
                                           start=(m == 0), stop=True, skip_group_check=True)
                        dc = dbase[:, 128 * h:128 * h + 128]
                        if j == i0:
                            for m in range(2):
                                ins = e.matmul(ps[:, sbi, 256 * m:256 * m + 128], ident[:, :], dc,
                                               start=False, stop=True, skip_group_check=True)
                        if j == i0 + 1:
                            for m in range(2):
                                ins = e.matmul(ps[:, sbi, 256 * m + 128:256 * m + 256], ident[:, :], dc,
                                               start=False, stop=True, skip_group_check=True)
                                ins = e.matmul(ps[:, sbi, 256 * m:256 * m + 128], ident[:, :], dbase[:, 512:640],
                                               start=False, stop=True, skip_group_check=True)
                        return ins
                    sc.op("pe", qk, reads=[("K", h, j // 4), ("Qz", zb), ("ident",), ("dbase",)],
                          writes=[("ps", sbi)])

                gcount = 0
                hits = {}

                def chk(nm):
                    hits[nm] = hits.get(nm, 0) + 1
                    if upto == nm or upto == "%s#%d" % (nm, hits[nm]):
                        raise _Stop()
                emit_qz(0, 0)
                for n in range(min(LOOK, nsteps)):
                    emit_qk(n)
                deferred = []
                for n in range(nsteps):
                    I, h, j = steps[n]
                    i0 = 2 * I
                    sbi = n % 3
                    obase = 3 + 2 * (gcount % 2)
                    while deferred and deferred[0][0] <= n:
                        deferred.pop(0)[1]()
                    bidx = h * 17 + 16 + (j - i0 - 1)
                    bias = bcol[:, bidx:bidx + 1]
                    sc.op("act", lambda e, sbi=sbi, bias=bias: e.activation(
                        out=PTr[:, sbi, :], in_=ps[:, sbi, :], func=AF.Exp, bias=bias, scale=0.125),
                        reads=[("bcol",)], writes=[("ps", sbi), ("PT", sbi)])
                    if n + LOOK < nsteps:
                        emit_qk(n + LOOK)

                    def pv(e, sbi=sbi, j=j, h=h, i0=i0, obase=obase):
                        ins = None
                        for a in range(2):
                            if a == 0 and j == i0 + 1:
                                continue
                            for m in range(2):
                                ins = e.matmul(ps[:, obase + a, 130 * m:130 * m + 130],
                                               PTr[:, sbi, 256 * m + 128 * a:256 * m + 128 * a + 128], Vv[:, j, h, :],
                                               start=(j == 0 and m == 0), stop=True, skip_group_check=True)
                        return ins
                    sc.op("pe", pv, reads=[("PT", sbi), ("V", j)], writes=[("ps", obase), ("ps", obase + 1)])
                    chk("B3")
                    if j == i0 + 1:
                        gcount += 1
                        for a in range(2):
                            ob = obase + a
                            o3 = ps[:, ob, 0:260].rearrange("p (a b) -> p a b", a=2)
                            sc.op("dve", lambda e, o3=o3: e.reciprocal(out=rl, in_=o3[:, :, 128]),
                                  writes=[("ps", ob), ("rl",)])
                            sc.op("dve", lambda e, neglam=neglam: e.tensor_scalar(out=rl2, in0=rl[:, 1:2], scalar1=neglam,
                                                                                  scalar2=None, op0=ALU.mult),
                                  reads=[("rl",), ("neglam", li)], writes=[("rl2",)])
                            sc.op("dve", lambda e, ob=ob: e.tensor_scalar(out=a1, in0=ps[:, ob, 0:128], scalar1=rl[:, 0:1],
                                                                         scalar2=None, op0=ALU.mult),
                                  reads=[("rl",)], writes=[("ps", ob), ("a1",)])
                            sc.op("dve", lambda e, ob=ob, a=a, h=h, I=I: e.scalar_tensor_tensor(
                                out=araw_sets[I % 2][:, a, 128 * h:128 * h + 128], in0=ps[:, ob, 130:258], scalar=rl2, in1=a1,
                                op0=ALU.mult, op1=ALU.add),
                                reads=[("rl2",), ("a1",)], writes=[("ps", ob), ("araw", I % 2, a, h)])
                        chk("B4d")
                        if h == 3:
                            for a in range(2):
                                i = i0 + a

                                def chain(a=a, i=i, I=I):
                                    ar = araw_sets[I % 2][:, a, :]
                                    ar3 = ar.rearrange("p (h d) -> p h d", h=4)
                                    sq3 = sqb.rearrange("p (h d) -> p h d", h=4)
                                    sc.op("dve", lambda e, ar=ar: e.tensor_tensor(out=sqb, in0=ar, in1=ar, op=ALU.mult),
                                          reads=[("araw", I % 2, a, hh) for hh in range(4)], writes=[("sqb",)])
                                    sc.op("dve", lambda e, sq3=sq3: e.tensor_reduce(out=ssh, in_=sq3, axis=AX.X, op=ALU.add),
                                          reads=[("sqb",)], writes=[("ssh",)])
                                    sc.op("dve", lambda e: e.tensor_scalar(out=rsh, in0=ssh, scalar1=1.0 / 128, scalar2=SUBLN_EPS,
                                                                           op0=ALU.mult, op1=ALU.add),
                                          reads=[("ssh",)], writes=[("rsh",)])
                                    sc.op("pool", lambda e: e.tensor_tensor(out=rstdh, in0=rsh, in1=neghalf, op=ALU.pow),
                                          reads=[("rsh",), ("neghalf",)], writes=[("rstdh",)])
                                    for hh in range(4):
                                        sc.op("dve", lambda e, ar3=ar3, hh=hh, a=a, i=i: e.scalar_tensor_tensor(
                                            out=agb2[:, a, 128 * hh:128 * hh + 128], in0=ar3[:, hh, :], scalar=rstdh[:, hh:hh + 1],
                                            in1=zav[:, i, 128 * hh:128 * hh + 128], op0=ALU.mult, op1=ALU.mult),
                                            reads=[("araw", I % 2, a, hh), ("rstdh",), ("za", i)], writes=[("agb", a)])

                                def tr(e, a=a):
                                    ins = None
                                    for c in range(4):
                                        ins = e.transpose(out=psbf[:, 7, 128 * c:128 * c + 128],
                                                          in_=agb2[:, a, 128 * c:128 * c + 128], identity=ident[:, :])
                                    return ins

                                def fin(tr=tr, i=i, a=a):
                                    sc.op("pe", tr, reads=[("agb", a), ("ident",)], writes=[("ps", 7)])
                                    sc.op("dve", lambda e, i=i: e.tensor_copy(
                                        out=mixT[:, :, 128 * i:128 * i + 128],
                                        in_=psbf[:, 7, 0:512].rearrange("p (c t) -> p c t", c=4)),
                                        writes=[("ps", 7), ("mx", i)])
                                nk = 2 * I + 4
                                cdue = n + 1 + (a + 1) * nk
                                deferred.append((cdue, chain))
                                deferred.append((cdue + min(20, 8 * I + 6), fin))
                                deferred.sort(key=lambda x: x[0])
                            chk("B6")

                def c1_A(t):
                    b0 = 2 * (t % 3)

                    def mmix(e, t=t, b0=b0):
                        ins = None
                        for n in range(2):
                            for kc in range(8):
                                lhs = mixT[:, kc, 128 * t:128 * t + 128] if kc < 4 else cvv[:, kc - 4, 128 * t:128 * t + 128]
                                ins = e.matmul(ps[:, b0 + n, :], lhs, woutv[:, kc, 512 * n:512 * n + 512],
                                               start=(kc == 0), stop=(kc == 7))
                        return ins
                    sc.op("pe", mmix, reads=[("wout",), ("w", 0), ("w", 1), ("mx", t)] + [("cv", c, t) for c in range(4)],
                          writes=[("ps", b0), ("ps", b0 + 1)])

                c1_A(0)
                c1_A(1)
                c1_A(2)
                while deferred:
                    deferred.pop(0)[1]()
                if debug and si == 0 and li == dbg_li:
                    sc.barrier()
                    for nm, r in [("dQ", Q_r), ("dK", K_r), ("dV", V_r), ("dza", za_r), ("dcv", cv_r), ("dmx", hT_r)]:
                        sc.dma("sp", lambda e, nm=nm, r=r: e.dma_start(out=dbg_d[nm], in_=r[:, :]), "dbg")

                if upto == "B":
                    raise _Stop()
                sc.barrier()
                sc.dma("sp", lambda e, l=l: e.dma_start(out=postg.unsqueeze(1),
                                                   in_=postg_d[l:l + 1, :].partition_broadcast(128)),
                       "postg", writes=[("postg",)])
                sc.dma("sp", lambda e, l=l: e.dma_start(out=pleg.unsqueeze(1),
                                                   in_=pleg_d[l:l + 1, :].partition_broadcast(128)),
                       "pleg", writes=[("pleg",)])

                def issue_p(t):
                    sc.dma("pool", lambda e, t=t, l=l, si=si: e.dma_start(out=pb[:, t % 2, :], in_=p_d[l, si, 128 * t:128 * t + 128, :]),
                           "p%d" % (t % 2), writes=[("pb", t % 2)])
                issue_p(0)
                issue_p(1)
                sc.dma("pool", lambda e, l=l: e.dma_start(out=wgatev, in_=wgate_d[l], max_dma_last_dim=4096), "wgate",
                       writes=[("wgate",)])
                sc.dma("pool", lambda e, l=l: e.dma_start(out=wplev, in_=wple_d[l], max_dma_last_dim=4096), "wple",
                       writes=[("wple",)])
                j2 = junk[:, :].rearrange("p (a b) -> p a b", a=2)
                pgv = postg.rearrange("p (a b) -> p a b", a=2)
                plv = pleg.rearrange("p (a b) -> p a b", a=2)
                def c1_Bsq(t):
                    par = t % 2
                    b0 = 2 * (t % 3)
                    sc.op("act", lambda e, b0=b0, par=par: e.activation(out=j2, in_=ps[:, b0:b0 + 2, :], func=AF.Square,
                                                                      accum_out=ss[:, par:par + 1]),
                          writes=[("ps", b0), ("ps", b0 + 1), ("ss", par), ("junk",)])

                def c1_Btmp(t):
                    par = t % 2
                    b0 = 2 * (t % 3)
                    bA = bufA[:, par, :]
                    bAv = bA.rearrange("p (a b) -> p a b", a=2)
                    sc.op("dve", lambda e, bAv=bAv, b0=b0: e.tensor_tensor(out=bAv, in0=ps[:, b0:b0 + 2, :], in1=pgv, op=ALU.mult),
                          reads=[("postg",)], writes=[("ps", b0), ("ps", b0 + 1), ("bufA", par)])
                    sc.op("dve", lambda e, par=par: e.tensor_scalar(out=rs[:, par:par + 1], in0=ss[:, par:par + 1],
                                                                  scalar1=1.0 / D, scalar2=EPS, op0=ALU.mult, op1=ALU.add),
                          reads=[("ss", par)], writes=[("rs", par)])
                    sc.op("pool", lambda e, par=par: e.tensor_tensor(out=rstd[:, par:par + 1], in0=rs[:, par:par + 1],
                                                                   in1=neghalf[:, 0:1], op=ALU.pow),
                          reads=[("rs", par), ("neghalf",)], writes=[("rstd", par)])

                def c1_Bstt(t):
                    par = t % 2
                    bA = bufA[:, par, :]
                    sc.op("dve", lambda e, t=t, bA=bA, par=par: e.scalar_tensor_tensor(
                        out=x_sb[:, t, :], in0=bA, scalar=rstd[:, par:par + 1], in1=x_sb[:, t, :], op0=ALU.mult, op1=ALU.add),
                        reads=[("bufA", par), ("rstd", par)], writes=[("x", t)])

                def c1_C(t):
                    par = t % 2
                    sc.op("act", lambda e, t=t, par=par: e.copy(out=xb2[:, par, :], in_=x_sb[:, t, :]),
                          reads=[("x", t)], writes=[("xb", par)])

                def c1_Dtrx(t):
                    par = t % 2

                    def trx(e, par=par):
                        ins = None
                        for c in range(8):
                            ins = e.transpose(out=psbf[:, 6, 128 * c:128 * c + 128], in_=xb2[:, par, 128 * c:128 * c + 128],
                                              identity=ident[:, :])
                        return ins
                    sc.op("pe", trx, reads=[("xb", par), ("ident",)], writes=[("ps", 6)])

                def c1_Devac(t):
                    sc.op("dve", lambda e, t=t: e.tensor_copy(
                        out=mixT[:, :, 128 * t:128 * t + 128],
                        in_=psbf[:, 6, 0:512].rearrange("p (c t) -> p c t", c=4)),
                        writes=[("ps", 6), ("mx", t)])
                    sc.op("dve", lambda e, t=t: e.tensor_copy(
                        out=cvv[:, :, 128 * t:128 * t + 128],
                        in_=psbf[:, 6, 512:1024].rearrange("p (c t) -> p c t", c=4)),
                        writes=[("ps", 6)] + [("cv", c, t) for c in range(4)])

                c1_Bsq(0)
                c1_Btmp(0)
                c1_Bstt(0)
                c1_Bsq(1)
                c1_Btmp(1)
                for k in range(16):
                    if k + 3 < 16:
                        c1_A(k + 3)
                    c1_C(k)
                    c1_Dtrx(k)
                    if k + 2 < 16:
                        c1_Bsq(k + 2)
                    c1_Devac(k)
                    if k + 1 < 16:
                        c1_Bstt(k + 1)
                    if k + 2 < 16:
                        c1_Btmp(k + 2)
                issue_w()
                issue_w()
                def c2_p(t):
                    par = t % 2

                    def trp(e, par=par):
                        ins = None
                        for kc in range(2):
                            ins = e.transpose(out=psbf[:, 6, 128 * kc:128 * kc + 128],
                                              in_=pb[:, par, 128 * kc:128 * kc + 128], identity=ident[:, :])
                        return ins
                    sc.op("pe", trp, reads=[("pb", par), ("ident",)], writes=[("ps", 6)])
                    sc.op("act", lambda e, par=par: e.copy(
                        out=pT2[:, par, :, :], in_=psbf[:, 6, 0:256].rearrange("p (c t) -> p c t", c=2)),
                        writes=[("ps", 6), ("pT", par)])
                    if t + 2 < 16:
                        issue_p(t + 2)

                c2_p(0)
                for t in range(16):
                    par = t % 2
                    g0 = 2 * par
                    e0 = 4
                    if t + 1 < 16:
                        c2_p(t + 1)

                    def mgate(e, t=t, g0=g0):
                        ins = None
                        for n in range(2):
                            for kc in range(8):
                                lhs = mixT[:, kc, 128 * t:128 * t + 128] if kc < 4 else cvv[:, kc - 4, 128 * t:128 * t + 128]
                                ins = e.matmul(ps[:, g0 + n, :], lhs, wgatev[:, kc, 512 * n:512 * n + 512],
                                               start=(kc == 0), stop=(kc == 7))
                        return ins
                    sc.op("pe", mgate, reads=[("wgate",), ("mx", t)] + [("cv", c, t) for c in range(4)],
                          writes=[("ps", g0), ("ps", g0 + 1)])

                    def mple(e, t=t, e0=e0):
                        ins = None
                        for n in range(2):
                            for kc in range(2):
                                ins = e.matmul(ps[:, e0 + n, :], pT2[:, t % 2, kc, :],
                                               wplev[:, kc, 512 * n:512 * n + 512], start=(kc == 0), stop=(kc == 1))
                        return ins
                    sc.op("pe", mple, reads=[("wple",), ("pT", par)], writes=[("ps", e0), ("ps", e0 + 1)])
                    bA = bufA[:, par, :]
                    bB = bufB[:, par, :]
                    bAv = bA.rearrange("p (a b) -> p a b", a=2)
                    bBv = bB.rearrange("p (a b) -> p a b", a=2)
                    sc.op("act", lambda e, e0=e0, par=par: e.activation(out=j2, in_=ps[:, e0:e0 + 2, :], func=AF.Square,
                                                                      accum_out=ss[:, 2 + par:3 + par]),
                          writes=[("ps", e0), ("ps", e0 + 1), ("ss", 2 + par), ("junk",)])
                    sc.op("act", lambda e, bAv=bAv, g0=g0: e.activation(out=bAv, in_=ps[:, g0:g0 + 2, :], func=AF.Sigmoid),
                          writes=[("ps", g0), ("ps", g0 + 1), ("bufA", par)])
                    sc.op("dve", lambda e, bBv=bBv, e0=e0: e.tensor_tensor(out=bBv, in0=ps[:, e0:e0 + 2, :], in1=plv, op=ALU.mult),
                          reads=[("pleg",)], writes=[("ps", e0), ("ps", e0 + 1), ("bufB", par)])
                    sc.op("dve", lambda e, par=par: e.tensor_scalar(out=rs[:, 2 + par:3 + par], in0=ss[:, 2 + par:3 + par],
                                                                  scalar1=1.0 / D, scalar2=EPS, op0=ALU.mult, op1=ALU.add),
                          reads=[("ss", 2 + par)], writes=[("rs", 2 + par)])
                    sc.op("pool", lambda e, par=par: e.tensor_tensor(out=rstd[:, 2 + par:3 + par], in0=rs[:, 2 + par:3 + par],
                                                                   in1=neghalf[:, 0:1], op=ALU.pow),
                          reads=[("rs", 2 + par), ("neghalf",)], writes=[("rstd", 2 + par)])
                    sc.op("dve", lambda e, bA=bA, bB=bB: e.tensor_tensor(out=bB, in0=bB, in1=bA, op=ALU.mult),
                          reads=[("bufA", par)], writes=[("bufB", par)])
                    sc.op("dve", lambda e, t=t, bB=bB, par=par: e.scalar_tensor_tensor(
                        out=x_sb[:, t, :], in0=bB, scalar=rstd[:, 2 + par:3 + par], in1=x_sb[:, t, :], op0=ALU.mult, op1=ALU.add),
                        reads=[("bufB", par), ("rstd", 2 + par)], writes=[("x", t)])
                    if last:
                        sc.dma("sp", lambda e, t=t, si=si: e.dma_start(out=out_d[si, 128 * t:128 * t + 128, :], in_=x_sb[:, t, :]),
                               "o%d" % t, reads=[("x", t)])
                sc.barrier()
                sc.op("pool", lambda e: e.memset(Vv[:, :, :, 128:130], 1.0), writes=[("V", t) for t in range(16)])

        except _Stop:
            pass
        sc.emit()
    return nc


_CACHE = {}


def _consts():
    ident = np.eye(128, dtype=np.float32).astype(ml_dtypes.bfloat16)
    rs = np.arange(128)[:, None].astype(np.float64)
    rt = np.arange(128)[None, :].astype(np.float64)
    dcorr = np.zeros((128, 5, 128), np.float32)
    bcol = np.zeros((128, 4, 17), np.float32)
    for h in range(4):
        m = SLOPES[h]
        db = np.where(rs > rt, -2.0 * m * (rs - rt), 0.0)
        masked = (rs // 64) > (rt // 64)
        db = np.where(masked, MASKV, db)
        dcorr[:, h, :] = db * 8.0
        for dl in range(-16, 1):
            bcol[:, h, 16 + dl] = m * (128.0 * dl + np.arange(128))
    dcorr[:, 4, :] = MASKV * 8.0
    return ident, dcorr.reshape(128, 640).astype(ml_dtypes.bfloat16), bcol.reshape(128, 68)


def prep_weights(inp):
    f = lambda a: np.ascontiguousarray(np.asarray(a, dtype=np.float32))
    w_in = f(inp["w_in"])
    parts = [w_in[:, :, 512 * i:512 * (i + 1)] for i in range(8)]
    cols = [parts[0], parts[1], parts[2], parts[3]]
    for c in range(4):
        cols.append(np.concatenate([parts[4][:, :, 128 * c:128 * c + 128], parts[5][:, :, 128 * c:128 * c + 128],
                                    parts[6][:, :, 128 * c:128 * c + 128], parts[7][:, :, 128 * c:128 * c + 128]], axis=2))
    wp = np.stack(cols, axis=1)
    wp = wp.reshape(NL, 8, 8, 128, 512).transpose(0, 1, 3, 2, 4)
    ident, dbase, bcol = _consts()
    sub = f(inp["subln_gain"])
    d = {
        "w_in": np.ascontiguousarray(wp),
        "w_out": np.ascontiguousarray(f(inp["w_out"]).reshape(NL, 8, 128, 1024).transpose(0, 2, 1, 3)),
        "w_gate": np.ascontiguousarray(f(inp["w_ple_gate"]).reshape(NL, 8, 128, 1024).transpose(0, 2, 1, 3)),
        "w_ple": np.ascontiguousarray(f(inp["w_ple_proj"]).reshape(NL, 2, 128, 1024).transpose(0, 2, 1, 3)),
        "pre_g": np.ascontiguousarray(f(inp["pre_norm_gain"]).reshape(NL, 8, 128).transpose(2, 0, 1).reshape(128, NL * 8)),
        "post_g": f(inp["post_norm_gain"]),
        "ple_g": f(inp["ple_norm_gain"]),
        "subln_g": np.ascontiguousarray(np.tile(sub, (1, 4))),
        "conv_w": np.ascontiguousarray(f(inp["conv_w"]).reshape(NL, 3, 4, 128).transpose(3, 0, 2, 1).reshape(128, NL * 12)),
        "lam": np.ascontiguousarray(np.concatenate([f(inp["lambda_q1"]), f(inp["lambda_k1"]),
                                                    f(inp["lambda_q2"]), f(inp["lambda_k2"])], axis=1)),
        "ident": ident, "dbase": dbase, "bcol": bcol,
    }
    return d


def kernel(**inp):
    x = np.asarray(inp["x"], dtype=np.float32)
    p = np.asarray(inp["p"], dtype=np.float32)
    wd = prep_weights(inp)
    nseq = x.shape[0] // NCORES
    if "nc" not in _CACHE:
        _CACHE["nc"] = build(nseq=nseq, layers=(0, 1))
    nc = _CACHE["nc"]
    in_maps = []
    for c in range(NCORES):
        m = dict(wd)
        m["x"] = np.ascontiguousarray(x[c * nseq:(c + 1) * nseq])
        m["p"] = np.ascontiguousarray(p[:, c * nseq:(c + 1) * nseq])
        in_maps.append(m)
    res = run_bass_kernel_spmd(nc, in_maps, core_ids=list(range(NCORES)))
    return np.concatenate([r["out"] for r in res.results], axis=0)
```
